# Optimizing a Trainium2 kernel written in Bass

```python
import math
import jax, jax.numpy as jnp
from jax import lax
import numpy as np

D_MODEL = 1024
BATCH = 32
SEQ = 2048
DEPTH = 2
DEC_BATCH = 16
DEC_SEQ = 16
PAST_LEN = 1024

CHUNK = 64
Q_BLOCK = 128
EPS = 1e-6
W_CONV = 512
CONV_WIDTH = 3
HG_HEADS = 4
HG_DK = 128
HG_DV = 128
W_HG = HG_HEADS * HG_DV
SB_HEADS = 8
SB_DH = 64
W_SB = SB_HEADS * SB_DH
W_MIX = W_CONV + W_HG + W_SB
SEG_SIZES = (W_CONV, W_CONV, W_CONV, W_CONV,
             HG_HEADS * HG_DK, HG_HEADS * HG_DK, W_HG, W_HG,
             W_SB, W_SB, W_SB, W_SB,
             D_MODEL, D_MODEL, D_MODEL)
D_IN = sum(SEG_SIZES)

kernel_name = "hybrid_conv_hgrn2_stickbreak_stream_step"


def rms_norm(x, g):
    xf = x.astype(jnp.float32)
    y = xf * lax.rsqrt(jnp.mean(xf * xf, axis=-1, keepdims=True) + EPS)
    return (y * g.astype(jnp.float32)).astype(x.dtype)


def split_cols(p):
    out = []
    start = 0
    for size in SEG_SIZES:
        out.append(p[..., start:start + size])
        start += size
    return out


def causal_conv(u, w, past):
    T = u.shape[1]
    full = jnp.concatenate([past.astype(u.dtype), u], axis=1)
    y = full[:, 0:T] * w[0]
    for j in range(1, CONV_WIDTH):
        y = y + full[:, j:j + T] * w[j]
    return y, full[:, T:]


def hgrn_chunk(S, inp):
    q, logf, k, v = inp
    L = q.shape[1]
    b = jnp.cumsum(logf, axis=1)
    tri = jnp.tril(jnp.ones((L, L), dtype=bool))[None, :, :, None, None]
    decay = jnp.exp(jnp.where(tri, b[:, :, None] - b[:, None, :], -jnp.inf))
    scores = jnp.einsum('bthk,btshk,bshk->bhts', q, decay, k)
    o = (jnp.einsum('bhts,bshv->bthv', scores, v)
         + jnp.einsum('bthk,bhkv->bthv', q * jnp.exp(b), S))
    b_last = b[:, -1]
    S_new = (jnp.exp(b_last)[..., None] * S
             + jnp.einsum('bshk,bshv->bhkv', k * jnp.exp(b_last[:, None] - b), v))
    return S_new, o


def hgrn_scan(S0, q, logf, k, v):
    B, T = q.shape[:2]
    L = min(CHUNK, T)
    n = T // L

    def to_chunks(a):
        return a.reshape(B, n, L, *a.shape[2:]).swapaxes(0, 1)

    S, o = lax.scan(hgrn_chunk, S0, (to_chunks(q), to_chunks(logf), to_chunks(k), to_chunks(v)))
    return S, o.swapaxes(0, 1).reshape(B, T, HG_HEADS, HG_DV)


def sb_attend(q, k, v, q_pos, k_pos):
    qf = q.astype(jnp.float32)
    kf = k.astype(jnp.float32)
    vf = v.astype(jnp.float32)
    z = jnp.einsum('bqhd,bkhd->bhqk', qf, kf) * (1.0 / math.sqrt(SB_DH))
    causal = (k_pos[None, :] < q_pos[:, None])[None, None]
    log_1mb = jnp.where(causal, jax.nn.log_sigmoid(-z), 0.0)
    rest = lax.cumsum(log_1mb, axis=3, reverse=True) - log_1mb
    A = jnp.where(causal, jnp.exp(jax.nn.log_sigmoid(z) + rest), 0.0)
    return jnp.einsum('bhqk,bkhd->bqhd', A, vf)


def sb_prompt(q, k, v):
    B, T = q.shape[:2]
    nb = T // Q_BLOCK
    qb = q.reshape(B, nb, Q_BLOCK, SB_HEADS, SB_DH).swapaxes(0, 1)
    posb = jnp.arange(T).reshape(nb, Q_BLOCK)
    k_pos = jnp.arange(T)
    o = lax.map(lambda a: sb_attend(a[0], k, v, a[1], k_pos), (qb, posb))
    return o.swapaxes(0, 1).reshape(B, T, SB_HEADS, SB_DH)


def mixer_layer(x, l, conv_past, S0, past_k, past_v, norm_g, w_in, conv_w, lb,
                hg_norm_g, q_norm_g, k_norm_g, w_branch, w_out):
    B, T, _ = x.shape
    f32 = jnp.float32
    h = rms_norm(x, norm_g[l])
    (a_x, a_b, a_c, a_z, g_q, g_f, g_i, g_z,
     s_q, s_k, s_v, s_z, m_a, m_b, m_c) = split_cols(h @ w_in[l])

    y_conv, conv_new = causal_conv(a_c * a_x, conv_w[l], conv_past)
    o_a = a_b * y_conv * jax.nn.silu(a_z)

    lb_l = lb[l].reshape(HG_HEADS, HG_DK)
    fpre = g_f.astype(f32).reshape(B, T, HG_HEADS, HG_DK)
    log_f = jnp.logaddexp(jnp.log(lb_l), jnp.log1p(-lb_l) + jax.nn.log_sigmoid(fpre))
    k_in = (1.0 - lb_l) * jax.nn.sigmoid(-fpre)
    q_hg = g_q.astype(f32).reshape(B, T, HG_HEADS, HG_DK)
    v_hg = g_i.astype(f32).reshape(B, T, HG_HEADS, HG_DV)
    S_new, o_hg = hgrn_scan(S0.astype(f32), q_hg, log_f, k_in, v_hg)
    o_hg = rms_norm(o_hg, hg_norm_g[l]).reshape(B, T, W_HG).astype(x.dtype)
    o_b = o_hg * jax.nn.silu(g_z)

    q = rms_norm(s_q.reshape(B, T, SB_HEADS, SB_DH), q_norm_g[l])
    k = rms_norm(s_k.reshape(B, T, SB_HEADS, SB_DH), k_norm_g[l])
    v = s_v.reshape(B, T, SB_HEADS, SB_DH)
    if past_k is None:
        o_sb = sb_prompt(q, k, v)
    else:
        P = past_k.shape[1]
        k_all = jnp.concatenate([past_k.astype(k.dtype), k], axis=1)
        v_all = jnp.concatenate([past_v.astype(v.dtype), v], axis=1)
        o_sb = sb_attend(q, k_all, v_all, P + jnp.arange(T), jnp.arange(P + T))
    o_c = o_sb.reshape(B, T, W_SB).astype(x.dtype) * jax.nn.silu(s_z)

    wb = w_branch[l]
    y_a = o_a @ wb[:W_CONV]
    y_b = o_b @ wb[W_CONV:W_CONV + W_HG]
    y_c = o_c @ wb[W_CONV + W_HG:]
    m = jax.nn.sigmoid(m_a) * y_a + jax.nn.sigmoid(m_b) * y_b + jax.nn.sigmoid(m_c) * y_c
    return x + m @ w_out[l], conv_new, S_new.astype(x.dtype), k, v


def setup_inputs(seed: int = 0) -> dict:
    key = jax.random.key(seed)
    ks = jax.random.split(key, 16)
    nrm = jax.random.normal
    return {
        'x_prompt': nrm(ks[0], (BATCH, SEQ, D_MODEL), jnp.float32),
        'x_sample': nrm(ks[1], (DEC_BATCH, DEC_SEQ, D_MODEL), jnp.float32),
        'cache_conv': nrm(ks[2], (DEPTH, DEC_BATCH, CONV_WIDTH - 1, W_CONV), jnp.float32),
        'state_hgrn': 0.5 * nrm(ks[3], (DEPTH, DEC_BATCH, HG_HEADS, HG_DK, HG_DV), jnp.float32),
        'cache_k': nrm(ks[4], (DEPTH, DEC_BATCH, PAST_LEN, SB_HEADS, SB_DH), jnp.float32),
        'cache_v': nrm(ks[5], (DEPTH, DEC_BATCH, PAST_LEN, SB_HEADS, SB_DH), jnp.float32),
        'norm_g': 1.0 + 0.1 * nrm(ks[6], (DEPTH, D_MODEL), jnp.float32),
        'w_in': nrm(ks[7], (DEPTH, D_MODEL, D_IN), jnp.float32) * D_MODEL ** -0.5,
        'conv_w': 0.5 * nrm(ks[8], (DEPTH, CONV_WIDTH, W_CONV), jnp.float32),
        'hg_lb_logits': nrm(ks[9], (DEPTH, HG_HEADS * HG_DK), jnp.float32),
        'hg_norm_g': 1.0 + 0.1 * nrm(ks[10], (DEPTH, HG_DV), jnp.float32),
        'q_norm_g': 1.0 + 0.1 * nrm(ks[11], (DEPTH, SB_DH), jnp.float32),
        'k_norm_g': 1.0 + 0.1 * nrm(ks[12], (DEPTH, SB_DH), jnp.float32),
        'w_branch': nrm(ks[13], (DEPTH, W_MIX, D_MODEL), jnp.float32) * W_CONV ** -0.5,
        'w_out': 0.5 * nrm(ks[14], (DEPTH, D_MODEL, D_MODEL), jnp.float32) * D_MODEL ** -0.5,
    }


def reference(x_prompt, x_sample, cache_conv, state_hgrn, cache_k, cache_v, norm_g, w_in, conv_w,
              hg_lb_logits, hg_norm_g, q_norm_g, k_norm_g, w_branch, w_out):
    lb_cum = jnp.cumsum(jax.nn.softmax(hg_lb_logits.astype(jnp.float32), axis=0), axis=0)
    lb = lb_cum - lb_cum[0:1]
    weights = (norm_g, w_in, conv_w, lb, hg_norm_g, q_norm_g, k_norm_g, w_branch, w_out)
    Bp = x_prompt.shape[0]
    conv_zero = jnp.zeros((Bp, CONV_WIDTH - 1, W_CONV), x_prompt.dtype)
    S_zero = jnp.zeros((Bp, HG_HEADS, HG_DK, HG_DV), jnp.float32)
    yp, ys = x_prompt, x_sample
    cp, cs, hp, hs, kp, ksl, vp, vs = [], [], [], [], [], [], [], []
    for l in range(DEPTH):
        yp, c1, s1, k1, v1 = mixer_layer(yp, l, conv_zero, S_zero, None, None, *weights)
        ys, c2, s2, k2, v2 = mixer_layer(ys, l, cache_conv[l], state_hgrn[l], cache_k[l], cache_v[l], *weights)
        cp.append(c1); hp.append(s1); kp.append(k1); vp.append(v1)
        cs.append(c2); hs.append(s2); ksl.append(k2); vs.append(v2)
    conv_prompt = jnp.stack(cp)
    conv_sample = jnp.stack(cs)
    hgrn_prompt = jnp.stack(hp)
    hgrn_sample = jnp.stack(hs)
    k_prompt = jnp.stack(kp)
    k_sample = jnp.stack(ksl)
    v_prompt = jnp.stack(vp)
    v_sample = jnp.stack(vs)
    return (yp, ys, conv_prompt, conv_sample, hgrn_prompt, hgrn_sample, k_prompt, k_sample, v_prompt, v_sample)
```

```python
import numpy as np
from contextlib import ExitStack
import concourse.bass as bass
import concourse.mybir as mybir
from concourse.bass_utils import run_bass_kernel_spmd

F32 = mybir.dt.float32
BF16 = mybir.dt.bfloat16
AF = mybir.ActivationFunctionType
ALU = mybir.AluOpType

EPS = 1e-6
D = 1024
DEPTH = 2
D_IN = 9216
USZ = 4608
NUNITS = 22
SEG_ORDER = [3, 7, 11, 8, 9, 10, 5, 0, 1, 2, 4, 6]
PAST = 1024
DECT = 16
TT = 512

C_IDENT, C_ONES, C_BLK64, C_CAUS, C_RESET, NCP = 0, 128, 256, 384, 448, 960
C_NEGTRI, C_NEGONES, C_MASK01, NCONST = 960, 1088, 1216, 1344
B_IDENT, B_NEGTRI, B_NEGONES, B_MASK01, B_CAUS, NCONSTB = 0, 128, 256, 384, 512, 576


def make_consts():
    c = np.zeros((128, NCONST), np.float32)
    i = np.arange(128)
    c[:, C_IDENT:C_IDENT + 128] = np.eye(128)
    c[:, C_NEGTRI:C_NEGTRI + 128] = -(i[:, None] >= i[None, :]).astype(np.float32)
    c[:, C_NEGONES:C_NEGONES + 128] = -1.0
    c[:, C_MASK01:C_MASK01 + 128] = (i[:, None] < i[None, :]).astype(np.float32)
    c[:, C_ONES:C_ONES + 128] = 1.0
    c[:, C_BLK64:C_BLK64 + 128] = ((i[:, None] // 64) == (i[None, :] // 64)).astype(np.float32)
    t = np.arange(64)
    c[:, C_CAUS:C_CAUS + 64] = ((i[:, None] % 64) <= t[None, :]).astype(np.float32)
    tt = np.arange(512)
    c[:, C_RESET:C_RESET + 512] = (tt % 64 != 0).astype(np.float32)[None, :]
    return c


class Buf:
    def __init__(self, t, name, psum=False):
        self.t = t
        self.name = name
        self.psum = psum
        self.last_write = None
        self.readers = []

    def __getitem__(self, idx):
        return self.t[idx]


class Sched:
    def __init__(self, nc, es):
        self.nc = nc
        self.es = es
        self.eng = {'pe': nc.tensor, 'act': nc.scalar, 'dve': nc.vector, 'pool': nc.gpsimd, 'sp': nc.sync}
        self.cur = {}
        self.cnt = {}
        self.waited = {e: {} for e in self.eng}
        self.pending = {e: [] for e in self.eng}
        self.nsem = 0
        for e in self.eng:
            self.cur[e] = self._sem(f"s_{e}")
            self.cnt[e] = 0
        self.nops = 0

    def _sem(self, name):
        self.nsem += 1
        return self.es.enter_context(self.nc.semaphore(name))

    def sbuf(self, name, shape, dt, es=None):
        return Buf((es or self.es).enter_context(self.nc.sbuf_tensor(name, shape, dt)), name)

    def psum(self, name, shape, dt):
        return Buf(self.es.enter_context(self.nc.psum_tensor(name, shape, dt)), name, psum=True)

    def need(self, e, ev):
        if ev is None:
            return
        sem, val = ev
        w = self.waited[e]
        if w.get(id(sem), 0) < val:
            self.eng[e].wait_ge(sem, val)
            w[id(sem)] = val

    def _deps(self, e, reads, writes):
        for b in reads:
            self.need(e, b.last_write)
            if b.psum:
                for ev in b.readers:
                    self.need(e, ev)
        for b in writes:
            self.need(e, b.last_write)
            for ev in b.readers:
                self.need(e, ev)

    def _commit(self, ev, reads, writes):
        for b in reads:
            b.readers = [r for r in b.readers if r[0] is not ev[0]]
            b.readers.append(ev)
        for b in writes:
            b.last_write = ev
            b.readers = []

    def op(self, e, fn, reads=(), writes=(), inc=True):
        self._deps(e, reads, writes)
        ins = fn(self.eng[e])
        self.nops += 1
        self.pending[e].append((reads, writes))
        if inc:
            self.cnt[e] += 1
            ins.then_inc(self.cur[e], 1)
            ev = (self.cur[e], self.cnt[e])
            for (r, w) in self.pending[e]:
                self._commit(ev, r, w)
            self.pending[e] = []
        return ins

    def dsem(self, name):
        return {'sem': self._sem(name), 'n': 0}

    def dma(self, e, out, in_, reads=(), writes=(), sem=None, group=False, **kw):
        self._deps(e, reads, writes)
        if not group and sem['n']:
            self.need(e, (sem['sem'], sem['n']))
        ins = self.eng[e].dma_start(out=out, in_=in_, **kw)
        self.nops += 1
        sem['n'] += 16
        ins.then_inc(sem['sem'], 16)
        ev = (sem['sem'], sem['n'])
        self._commit(ev, reads, writes)
        return ev


import os as _os
_DBG_STOP = _os.environ.get("KDBG_STOP", "")
_STQ = _os.environ.get("KDBG_STQ", "pool")


class _StopBuild(Exception):
    pass


_STAGE_CNT = {}


def _stage(name):
    if not _DBG_STOP:
        return
    _STAGE_CNT[name] = _STAGE_CNT.get(name, 0) + 1
    if name == _DBG_STOP or f"{name}#{_STAGE_CNT[name]}" == _DBG_STOP:
        raise _StopBuild(name)


class Cfg:
    def __init__(self, nps=4, seq=2048, nss=2):
        self.nps, self.seq, self.nss = nps, seq, nss


def build(cfg):
    nc = bass.Bass("TRN2", target_bir_lowering=False)
    NPS, SEQ, NSS = cfg.nps, cfg.seq, cfg.nss
    NT = SEQ // TT

    def din(name, shape):
        return nc.dram_tensor(name, list(shape), F32, kind="ExternalInput").ap()

    def dout(name, shape):
        return nc.dram_tensor(name, list(shape), F32, kind="ExternalOutput").ap()

    xp = din("xp", [NPS, SEQ, D])
    xs = din("xs", [NSS, DECT, D])
    cconv = din("cconv", [DEPTH, NSS, 2, 512])
    shg = din("shg", [DEPTH, NSS, 4, 128, 128])
    ck = din("ck", [DEPTH, NSS, PAST, 512])
    cv = din("cv", [DEPTH, NSS, PAST, 512])
    norm_g = din("norm_g", [DEPTH, D])
    w_in = din("w_in", [DEPTH, D, D_IN])
    conv_w = din("conv_w", [DEPTH, 3, 512])
    hg_lb = din("hg_lb", [DEPTH, 512])
    hg_ng = din("hg_ng", [DEPTH, 128])
    q_ng = din("q_ng", [DEPTH, 64])
    k_ng = din("k_ng", [DEPTH, 64])
    w_br = din("w_br", [DEPTH, 1536, D])
    w_out = din("w_out", [DEPTH, D, D])
    cst = din("cst", [128, NCONST])

    yp = dout("yp", [NPS, SEQ, D])
    ys = dout("ys", [NSS, DECT, D])
    conv_p = dout("conv_p", [DEPTH, NPS, 2, 512])
    conv_s = dout("conv_s", [DEPTH, NSS, 2, 512])
    hg_p = dout("hg_p", [DEPTH, NPS, 4, 128, 128])
    hg_s = dout("hg_s", [DEPTH, NSS, 4, 128, 128])
    k_p = dout("k_p", [DEPTH, NPS, SEQ, 512])
    k_s = dout("k_s", [DEPTH, NSS, DECT, 512])
    v_p = dout("v_p", [DEPTH, NPS, SEQ, 512])
    v_s = dout("v_s", [DEPTH, NSS, DECT, 512])

    wsc = nc.dram_tensor("wsc", [DEPTH, NUNITS, 128, USZ], BF16, kind="Internal").ap()

    with ExitStack() as es:
        S = Sched(nc, es)
        op = S.op

        def act(out, in_, func, reads, writes, **kw):
            return op('act', lambda e: e.activation(out=out, in_=in_, func=func, **kw), reads, writes)

        def tt(eng, out, in0, in1, alu, reads, writes):
            return op(eng, lambda e: e.tensor_tensor(out=out, in0=in0, in1=in1, op=alu), reads, writes)

        def ts(eng, out, in0, s1, alu, reads, writes, s2=None, alu1=None):
            if alu1 is None:
                return op(eng, lambda e: e.tensor_scalar(out=out, in0=in0, scalar1=s1, scalar2=None, op0=alu),
                          reads, writes)
            return op(eng, lambda e: e.tensor_scalar(out=out, in0=in0, scalar1=s1, scalar2=s2, op0=alu, op1=alu1),
                      reads, writes)

        def stt(out, in0, scalar, in1, op0, op1, reads, writes):
            return op('dve', lambda e: e.scalar_tensor_tensor(out=out, in0=in0, scalar=scalar, in1=in1,
                                                              op0=op0, op1=op1), reads, writes)

        def cp(eng, out, in_, reads, writes):
            if eng == 'act':
                return act(out, in_, AF.Copy, reads, writes)
            return op(eng, lambda e: e.tensor_copy(out=out, in_=in_), reads, writes)

        def mm(out, lhsT, rhs, reads, writes, start=True, stop=True, inc=True, sgc=False):
            return op('pe', lambda e: e.matmul(out, lhsT=lhsT, rhs=rhs, start=start, stop=stop,
                                               skip_group_check=sgc), reads, writes, inc=inc)

        def tr(out, in_, ident, reads, writes, inc=True):
            return op('pe', lambda e: e.transpose(out, in_, ident), reads, writes, inc=inc)

        consts = S.sbuf("consts", [128, NCP], F32)
        cb = S.sbuf("cb", [128, NCONSTB], BF16)
        xres = S.sbuf("xres", [128, 4, D], F32)
        xb = [Buf(xres.t, f"xres_b{b}") for b in range(4)]
        hT = S.sbuf("hT", [128, 8, TT], BF16)
        ring = [S.sbuf(f"ring{i}", [128, USZ], BF16) for i in range(4)]
        ring_sem = [S.dsem(f"rs{i}") for i in range(4)]
        KT = [S.sbuf(f"KT{l}", [128, 4, 2048], BF16) for l in range(DEPTH)]
        VC = [S.sbuf(f"VC{l}", [128, 16, 512], BF16) for l in range(DEPTH)]
        siluz = S.sbuf("siluz", [128, 12, TT], BF16)
        oA = S.sbuf("oA", [128, 4, TT], BF16)
        oB = S.sbuf("oB", [128, 4, TT], BF16)
        oC = S.sbuf("oC", [128, 4, TT], BF16)
        Sst = [S.sbuf(f"Sst{l}", [128, 4, 128], F32) for l in range(DEPTH)]
        utail = [S.sbuf(f"utail{l}", [128, 4, 2], F32) for l in range(DEPTH)]
        ng = S.sbuf("ng", [128, DEPTH, 8], F32)
        cw = S.sbuf("cw", [128, DEPTH, 3, 4], F32)
        lg = S.sbuf("lg", [128, DEPTH, 4], F32)
        LB = S.sbuf("LB", [128, DEPTH, 4], F32)
        OML = S.sbuf("OML", [128, DEPTH, 4], F32)
        NOML = S.sbuf("NOML", [128, DEPTH, 4], F32)
        hgG = S.sbuf("hgG", [128, DEPTH], F32)
        qg = S.sbuf("qg", [128, DEPTH], F32)
        kg = S.sbuf("kg", [128, DEPTH], F32)
        ssq = S.sbuf("ssq", [128, 4], F32)
        rstd4 = S.sbuf("rstd4", [128, 4], F32)
        lnt = S.sbuf("lnt", [128, 4], F32)
        dl = S.sbuf("dl", [128, 8], F32)
        emid = S.sbuf("emid", [128, 8], F32)
        fb = [S.sbuf(f"fb{i}", [128, 512], F32) for i in range(8)]
        hb = [S.sbuf(f"hb{i}", [128, 512], BF16) for i in range(8)]
        hbx = [S.sbuf(f"hbx{i}", [128, 512], BF16) for i in range(6)]
        hbs = [Buf(fb[6 + i // 2].t[:].bitcast(BF16)[:, (i % 2) * 512:(i % 2 + 1) * 512], f"hbs{i}") for i in range(4)]
        ccx = S.sbuf("ccx", [128, 8], F32)
        QTZ = [S.sbuf(f"qtz{i}", [128, 512], BF16) for i in range(2)]
        bmv = S.sbuf("bmv", [128, 8], F32)
        big1 = S.sbuf("big1", [128, 4, 512], F32)
        big2 = S.sbuf("big2", [128, 4, 512], F32)
        banks = [S.psum(f"bank{i}", [128, 512], F32) for i in range(8)]
        bank_i = [0]

        def nb():
            b = banks[bank_i[0] % 8]
            bank_i[0] += 1
            return b

        def bf_view(buf):
            a = buf.t[:]
            if len(a.shape) == 3:
                a = a.rearrange("p a b -> p (a b)")
            return a.bitcast(BF16)

        ld = S.dsem("ld")
        xld = S.dsem("xld")
        st_sems = [S.dsem(f"st{i}") for i in range(8)]
        st_i = [0]

        def store(out, in_, reads, **kw):
            sem = st_sems[st_i[0] % len(st_sems)]
            st_i[0] += 1
            return S.dma(_STQ, out, in_, reads=reads, sem=sem, **kw)

        def sload(out, in_, wbuf, **kw):
            ev = S.dma('sp', out, in_, writes=[wbuf], sem=ld, allow_slow_non_contiguous=True, **kw)
            return ev

        S.dma('sp', consts[:], cst[:, 0:NCP], writes=[consts], sem=ld)
        S.dma('sp', fb[0][:, 0:384], cst[:, C_NEGTRI:C_NEGTRI + 384], writes=[fb[0]], sem=ld)
        cp('dve', cb[:, B_IDENT:B_IDENT + 128], consts[:, C_IDENT:C_IDENT + 128], [consts], [cb])
        cp('dve', cb[:, B_NEGTRI:B_NEGTRI + 384], fb[0][:, 0:384], [fb[0]], [cb])
        cp('dve', cb[:, B_CAUS:B_CAUS + 64], consts[:, C_CAUS:C_CAUS + 64], [consts], [cb])
        for qz in QTZ:
            op('pool', lambda e: e.memset(qz[:, :], 0.0), [], [qz])
        sload(ng[:], norm_g.rearrange("l (kc p) -> p l kc", p=128), ng)
        for l in range(DEPTH):
            sload(cw[:, l, :, :], conv_w[l].rearrange("j (c p) -> p j c", p=128), cw)
        sload(lg[:], hg_lb.rearrange("l (h p) -> p l h", p=128), lg)
        sload(hgG[:], hg_ng.rearrange("l p -> p l"), hgG)
        for half in range(2):
            sload(qg[half * 64:(half + 1) * 64, :], q_ng.rearrange("l d -> d l"), qg)
            sload(kg[half * 64:(half + 1) * 64, :], k_ng.rearrange("l d -> d l"), kg)
        ts('dve', qg[:], qg[:], 0.125, ALU.mult, [qg], [qg])
        op('dve', lambda e: e.memset(LB[:], 0.0), [], [LB])
        tt('dve', lnt[:, 0:4], lg[:, 1, :], lg[:, 0, :], ALU.subtract, [lg], [lnt])
        act(LB[:, 1, :], lnt[:, 0:4], AF.Sigmoid, [lnt], [LB])
        ts('dve', OML[:], LB[:], -1.0, ALU.mult, [LB], [OML], s2=1.0, alu1=ALU.add)
        ts('dve', NOML[:], LB[:], 1.0, ALU.mult, [LB], [NOML], s2=-1.0, alu1=ALU.add)

        NSTG = 4
        stgF = [(tuple(xb), xres.t[:].rearrange("p a b -> p (a b)")),
                ((VC[0],), VC[0].t[:].rearrange("p a b -> p (a b)").bitcast(F32)),
                ((VC[1],), VC[1].t[:].rearrange("p a b -> p (a b)").bitcast(F32)),
                ((KT[0],), KT[0].t[:].rearrange("p a b -> p (a b)").bitcast(F32))]
        stgB = [(ring[i], ring[i].t[:]) for i in range(4)]
        stg_sem = [S.dsem(f"stgs{i}") for i in range(NSTG)]
        stb_sem = [S.dsem(f"stbs{i}") for i in range(NSTG)]
        cast_engs = ['dve', 'act']
        ji = 0
        ei = 0
        for l in range(DEPTH):
            w_in_r = w_in[l].rearrange("(kc p) c -> p kc c", p=128)
            w_br_r = w_br[l].rearrange("(kc p) c -> p kc c", p=128)
            w_out_r = w_out[l].rearrange("(kc p) c -> p kc c", p=128)
            jobs = []
            for u in range(NUNITS):
                if u < 12:
                    g = SEG_ORDER[u]
                    jobs.append((u, 0, 4096, [(0, w_in_r[:, :, g * 512:(g + 1) * 512], 8, 512, True)]))
                elif u < 20:
                    dc = u - 12
                    gcol = [6144 + g * 1024 + dc * 128 for g in range(3)]
                    jobs.append((u, 0, 2560, [(0, w_br_r[:, :, dc * 128:(dc + 1) * 128], 12, 128, False),
                                              (1536, w_in_r[:, :, gcol[0]:gcol[0] + 128], 8, 128, True)]))
                    jobs.append((u, 2560, 2048, [(0, w_in_r[:, :, gcol[1]:gcol[1] + 128], 8, 128, True),
                                                 (1024, w_in_r[:, :, gcol[2]:gcol[2] + 128], 8, 128, True)]))
                else:
                    hf = u - 20
                    jobs.append((u, 0, 4096, [(0, w_out_r[:, hf * 4:(hf + 1) * 4, :], 4, 1024, False)]))
            for (u, off, n, pieces) in jobs:
                fbuf, fap = stgF[ji % NSTG]
                bbuf, bap = stgB[ji % NSTG]
                ssem, bsem = stg_sem[ji % NSTG], stb_sem[ji % NSTG]
                ji += 1
                for (so, src, kcs, cc_, scaled) in pieces:
                    S.dma('sp', fap[:, so:so + kcs * cc_].rearrange("p (kc c) -> p kc c", c=cc_), src,
                          writes=list(fbuf), sem=ssem, group=True)
                for (so, src, kcs, cc_, scaled) in pieces:
                    if scaled and cc_ == 512:
                        for kc in range(8):
                            e2 = cast_engs[ei % 2]
                            ei += 1
                            o_ = bap[:, so + kc * 512:so + (kc + 1) * 512]
                            i_ = fap[:, so + kc * 512:so + (kc + 1) * 512]
                            sc_ = ng[:, l, kc:kc + 1]
                            if e2 == 'act':
                                act(o_, i_, AF.Identity, [*fbuf, ng], [bbuf], scale=sc_)
                            else:
                                ts(e2, o_, i_, sc_, ALU.mult, [*fbuf, ng], [bbuf])
                    elif scaled:
                        e2 = 'dve'
                        o3 = bap[:, so:so + 1024].rearrange("p (kc c) -> p kc c", c=128)
                        i3 = fap[:, so:so + 1024].rearrange("p (kc c) -> p kc c", c=128)
                        tt(e2, o3, i3, ng[:, l, :].unsqueeze(2).to_broadcast([128, 8, 128]), ALU.mult,
                           [*fbuf, ng], [bbuf])
                    else:
                        tot = kcs * cc_
                        step = 1024
                        for o0 in range(0, tot, step):
                            e2 = cast_engs[ei % 2]
                            ei += 1
                            w_ = min(step, tot - o0)
                            cp(e2, bap[:, so + o0:so + o0 + w_], fap[:, so + o0:so + o0 + w_], list(fbuf), [bbuf])
                S.dma('pool', wsc[l, u, :, off:off + n], bap[:, 0:n], reads=[bbuf], sem=bsem)
        for bsem in stb_sem:
            S.need('sp', (bsem['sem'], bsem['n']))

        sched_units = []
        ring_state = {'issued': 0, 'cur': -1, 'rel': 0}

        def ring_issue_upto(n):
            while ring_state['issued'] < min(n, len(sched_units)):
                i = ring_state['issued']
                l, u = sched_units[i]
                slot = i % 4
                nu = USZ if 12 <= u < 20 else 4096
                S.dma('sp', ring[slot][:, 0:nu], wsc[l, u, :, 0:nu], writes=[ring[slot]], sem=ring_sem[slot])
                ring_state['issued'] += 1

        def next_unit(expect_u):
            ring_state['cur'] += 1
            i = ring_state['cur']
            assert sched_units[i][1] == expect_u, (sched_units[i], expect_u)
            assert i < ring_state['rel'] + 4
            ring_issue_upto(ring_state['rel'] + 4)
            return ring[i % 4]

        def release(n=1):
            ring_state['rel'] += n
            assert ring_state['rel'] <= ring_state['cur'] + 1
            ring_issue_upto(ring_state['rel'] + 4)

        def phase0_block(b, r):
            identB = cb[:, B_IDENT:B_IDENT + 128]
            xn = fb[0]
            junk = fb[1]
            act(bf_view(junk)[:r, :], xres[:r, b, :], AF.Square, [xb[b]], [junk, ssq], accum_out=ssq[:r, b:b + 1])
            act(lnt[:r, b:b + 1], ssq[:r, b:b + 1], AF.Ln, [ssq], [lnt], scale=1.0 / D, bias=EPS)
            act(rstd4[:r, b:b + 1], lnt[:r, b:b + 1], AF.Exp, [lnt], [rstd4], scale=-0.5)
            ts('dve', bf_view(xn)[:r, :], xres[:r, b, :], rstd4[:r, b:b + 1], ALU.mult, [xb[b], rstd4], [xn])
            pb = nb()
            pv = bf_view(pb).rearrange("p (a b) -> p a b", b=128)
            for kc in range(8):
                tr(pv[:, kc, :r], bf_view(xn)[:r, kc * 128:(kc + 1) * 128], identB[:r, :r],
                   [xn, cb], [pb], inc=(kc == 7))
            cp('act' if b % 2 == 0 else 'dve', hT[:, :, b * 128:b * 128 + r], pv[:, :, :r], [pb], [hT])

        def tile_layer(l, T, kbase, k_dst, v_dst, post_dma, post_blk):
            nblk = (T + 127) // 128
            rows_of = [min(128, T - b * 128) for b in range(nblk)]
            L = min(64, T)
            nch = T // L
            mid = L // 2 - 1
            identB = cb[:, B_IDENT:B_IDENT + 128]
            _stage('p0')

            def proj_fm(wbuf, col0, out_bank):
                for kc in range(8):
                    mm(out_bank[:, :T], wbuf[:, kc * 512 + col0:kc * 512 + col0 + 128], hT[:, kc, :T],
                       [wbuf, hT], [out_bank], start=(kc == 0), stop=(kc == 7), inc=(kc == 7))

            def proj_tm(wbuf, blk, out_bank):
                r = rows_of[blk]
                for kc in range(8):
                    mm(out_bank[:r, :], hT[:, kc, blk * 128:blk * 128 + r], wbuf[:, kc * 512:(kc + 1) * 512],
                       [wbuf, hT], [out_bank], start=(kc == 0), stop=(kc == 7), inc=(kc == 7))

            _stage('pZ')
            for br in range(3):
                w = next_unit(br)
                for c in range(4):
                    pb = nb()
                    proj_fm(w, c * 128, pb)
                    act(siluz[:, br * 4 + c, :T], pb[:, :T], AF.Silu, [pb], [siluz])
                release(1)

            _stage('pC')
            wsq, wsk, wsv = next_unit(3), next_unit(4), next_unit(5)
            QTc = hbx[0:4]
            kf = big1
            stg_tok = big2
            vb0 = kbase // 128
            for b in range(nblk):
                r = rows_of[b]
                pb = nb()
                proj_tm(wsv, b, pb)
                cp('act', stg_tok[:r, b, :], pb[:r, :], [pb], [stg_tok])
                cp('dve', VC[l][:r, vb0 + b, :], pb[:r, :], [pb], [VC[l]])
            if T >= 128:
                store(v_dst.rearrange("(b p) f -> p b f", p=128), stg_tok[:, :, :], [stg_tok])
            else:
                store(v_dst, stg_tok[:T, 0, :], [stg_tok])
            _stage('pC1')
            blk64 = consts[:, C_BLK64:C_BLK64 + 128]
            for which in range(2):
                wbuf = wsq if which == 0 else wsk
                gain = qg if which == 0 else kg
                for c in range(4):
                    pq = nb()
                    proj_fm(wbuf, c * 128, pq)
                    sq, lnv, rs = (fb[0], fb[1], fb[2]) if (which * 4 + c) % 2 == 0 else (fb[3], fb[4], fb[5])
                    act(sq[:, :T], pq[:, :T], AF.Square, [pq], [sq])
                    pss = nb()
                    mm(pss[:, :T], blk64, sq[:, :T], [consts, sq], [pss])
                    act(lnv[:, :T], pss[:, :T], AF.Ln, [pss], [lnv], scale=1.0 / 64, bias=EPS)
                    act(rs[:, :T], lnv[:, :T], AF.Exp, [lnv], [rs], scale=-0.5)
                    if which == 0:
                        stt(QTc[c][:, :T], pq[:, :T], gain[:, l:l + 1], rs[:, :T], ALU.mult, ALU.mult,
                            [pq, gain, rs], [QTc[c]])
                    else:
                        stt(kf[:, c, :T], pq[:, :T], gain[:, l:l + 1], rs[:, :T], ALU.mult, ALU.mult,
                            [pq, gain, rs], [kf])
                        cp('pool', KT[l][:, c, kbase:kbase + T], kf[:, c, :T], [kf], [KT[l]])
            release(3)
            _stage('pC2')
            identF = consts[:, C_IDENT:C_IDENT + 128]
            for b in range(nblk):
                r = rows_of[b]
                pb = nb()
                for c in range(4):
                    tr(pb[:r, c * 128:(c + 1) * 128], kf[:, c, b * 128:b * 128 + r], identF, [kf, consts], [pb],
                       inc=(c == 3))
                cp('act', stg_tok[:r, b, :], pb[:r, :], [pb], [stg_tok])
            if T >= 128:
                store(k_dst.rearrange("(b p) f -> p b f", p=128), stg_tok[:, :, :], [stg_tok])
            else:
                store(k_dst, stg_tok[:T, 0, :], [stg_tok])

            _stage('pF')
            sg = big1
            w = next_unit(6)
            for h in range(4):
                pb = nb()
                proj_fm(w, h * 128, pb)
                act(sg[:, h, :T], pb[:, :T], AF.Sigmoid, [pb], [sg])
            release(1)

            _stage('pA')
            wx, wb_, wc = next_unit(7), next_unit(8), next_unit(9)
            for c in range(4):
                px, pbb, pc = nb(), nb(), nb()
                proj_fm(wx, c * 128, px)
                proj_fm(wc, c * 128, pc)
                proj_fm(wb_, c * 128, pbb)
                xsb, uext, y0, y1 = fb[2], fb[3], fb[4], fb[5]
                cp('act', xsb[:, :T], px[:, :T], [px], [xsb])
                tt('dve', uext[:, :T], pc[:, :T], xsb[:, :T], ALU.mult, [pc, xsb], [uext])
                cw0, cw1, cw2 = (cw[:, l, j, c:c + 1] for j in range(3))
                act(y0[:, :T], uext[:, :T], AF.Identity, [uext, cw], [y0], scale=cw2)
                stt(y0[:, 1:T], uext[:, 0:T - 1], cw1, y0[:, 1:T], ALU.mult, ALU.add, [uext, cw, y0], [y0])
                stt(y0[:, 2:T], uext[:, 0:T - 2], cw0, y0[:, 2:T], ALU.mult, ALU.add, [uext, cw, y0], [y0])
                stt(y0[:, 0:1], utail[l][:, c, 1:2], cw1, y0[:, 0:1], ALU.mult, ALU.add, [utail[l], cw, y0], [y0])
                stt(y0[:, 0:1], utail[l][:, c, 0:1], cw0, y0[:, 0:1], ALU.mult, ALU.add, [utail[l], cw, y0], [y0])
                stt(y0[:, 1:2], utail[l][:, c, 1:2], cw0, y0[:, 1:2], ALU.mult, ALU.add, [utail[l], cw, y0], [y0])
                cp('pool', utail[l][:, c, :], uext[:, T - 2:T], [uext], [utail[l]])
                tt('dve', y1[:, :T], pbb[:, :T], y0[:, :T], ALU.mult, [pbb, y0], [y1])
                tt('pool', oA[:, c, :T], y1[:, :T], siluz[:, c, :T], ALU.mult, [y1, siluz], [oA])
            release(3)

            _stage('pB')
            RB = [banks[5], banks[6], banks[7]]

            def phaseB():
                wq, wi = next_unit(10), next_unit(11)
                b2 = bf_view(big2).rearrange("p (a b) -> p a b", b=512)
                ktok_v = b2[:, 0:4, :]
                v16_v = b2[:, 4:8, :]
                for b in range(nblk):
                    r = rows_of[b]
                    pb = RB[1 + b % 2]
                    proj_tm(wi, b, pb)
                    yield
                    cp('dve', v16_v[:r, b, :], pb[:r, :], [pb], [big2])
                    yield
                F1, F2, F3 = fb[3], fb[4], fb[5]
                kk, eq, ek, qt = hbs[0], hbs[1], hbs[2], hbs[3]
                kt, scm = hbx[4], hbx[5]
                St16 = bf_view(F3).rearrange("p (a b) -> p a b", b=128)
                for h in range(4):
                    pq = RB[0]
                    proj_fm(wq, h * 128, pq)
                    yield
                    logf, bcs = F1, F2
                    act(logf[:, :T], sg[:, h, :T], AF.Ln, [sg, OML, LB], [logf],
                        scale=OML[:, l, h:h + 1], bias=LB[:, l, h:h + 1])
                    ts('dve', kk[:, :T], sg[:, h, :T], NOML[:, l, h:h + 1], ALU.mult, [sg, OML, NOML], [kk],
                       s2=OML[:, l, h:h + 1], alu1=ALU.add)
                    yield
                    op('dve', lambda e: e.tensor_tensor_scan(out=bcs[:, :T], data0=consts[:, C_RESET:C_RESET + T],
                                                             data1=logf[:, :T], initial=0.0,
                                                             op0=ALU.mult, op1=ALU.add), [consts, logf], [bcs])
                    yield
                    b3 = bcs[:, :T].rearrange("p (c l) -> p c l", l=L)
                    act(dl[:, :nch], b3[:, :, L - 1], AF.Exp, [bcs], [dl])
                    act(emid[:, :nch], b3[:, :, mid], AF.Exp, [bcs], [emid])
                    cp('dve', bmv[:, :nch], b3[:, :, mid], [bcs], [bmv])
                    yield
                    tt('dve', b3, b3, bmv[:, :nch].unsqueeze(2).to_broadcast([128, nch, L]), ALU.subtract,
                       [bcs, bmv], [bcs])
                    yield
                    act(eq[:, :T], bcs[:, :T], AF.Exp, [bcs], [eq])
                    act(ek[:, :T], bcs[:, :T], AF.Exp, [bcs], [ek], scale=-1.0)
                    act(ccx[:, :nch], b3[:, :, L - 1], AF.Exp, [bcs], [ccx])
                    yield
                    tt('dve', qt[:, :T], pq[:, :T], eq[:, :T], ALU.mult, [pq, eq], [qt])
                    tt('pool', kt[:, :T], kk[:, :T], ek[:, :T], ALU.mult, [kk, ek], [kt])
                    yield
                    pb = RB[1]
                    pv = bf_view(pb).rearrange("p (a b) -> p a b", b=128)
                    for b in range(nblk):
                        r = rows_of[b]
                        tr(pv[:r, b, :], kt[:, b * 128:b * 128 + r], identB, [kt, cb], [pb], inc=(b == nblk - 1))
                    yield
                    rr = rows_of[0]
                    cp('dve', ktok_v[:rr, 0:nblk, h * 128:(h + 1) * 128], pv[:rr, 0:nblk, :], [pb], [big2])
                    yield
                    psc = RB[1]
                    psc3 = psc[:, 0:256].rearrange("p (a b) -> p a b", b=64)
                    for c in range(nch):
                        blk, par = c // 2, c % 2
                        r0 = par * 64
                        mm(psc3[r0:r0 + L, blk, 0:L], kt[:, c * L:(c + 1) * L], qt[:, c * L:(c + 1) * L],
                           [kt, qt], [psc])
                    yield
                    scm3 = scm[:, 0:256].rearrange("p (a b) -> p a b", b=64)
                    caus = consts[:, C_CAUS:C_CAUS + 64]
                    if T >= 128:
                        tt('dve', scm3[:, 0:nblk, :], psc3[:, 0:nblk, :],
                           caus.unsqueeze(1).to_broadcast([128, nblk, 64]), ALU.mult, [psc, consts], [scm])
                    else:
                        tt('dve', scm3[:L, 0, 0:L], psc3[:L, 0, 0:L], caus[:L, 0:L], ALU.mult, [psc, consts], [scm])
                    yield
                    po = RB[2]
                    pu = RB[0]
                    tmpu = F1
                    for c in range(nch):
                        blk, par = c // 2, c % 2
                        r0 = par * 64
                        Sv = Sst[l][:, h, :]
                        pus = pu[:, (c % 4) * 128:(c % 4 + 1) * 128]
                        mm(pus, ktok_v[r0:r0 + L, blk, h * 128:(h + 1) * 128],
                           v16_v[r0:r0 + L, blk, h * 128:(h + 1) * 128], [big2], [pu])
                        ts('dve', St16[:, c, :], Sv, emid[:, c:c + 1], ALU.mult, [Sst[l], emid], [F3])
                        yield
                        ts('dve', tmpu[:, 0:128], pus, ccx[:, c:c + 1], ALU.mult, [pu, ccx], [tmpu])
                        stt(Sv, Sv, dl[:, c:c + 1], tmpu[:, 0:128], ALU.mult, ALU.add, [Sst[l], dl, tmpu], [Sst[l]])
                        mm(po[:, c * L:(c + 1) * L], v16_v[r0:r0 + L, blk, h * 128:(h + 1) * 128],
                           scm3[r0:r0 + L, blk, 0:L], [big2, scm], [po], start=True, stop=False, inc=False)
                        mm(po[:, c * L:(c + 1) * L], St16[:, c, :], qt[:, c * L:(c + 1) * L],
                           [F3, qt], [po], start=False, stop=True)
                        yield
                    sq, lnv, rs, t1 = F2, F1, F2, F1
                    act(sq[:, :T], po[:, :T], AF.Square, [po], [sq])
                    yield
                    pss = RB[1]
                    mm(pss[:, :T], consts[:, C_ONES:C_ONES + 128], sq[:, :T], [consts, sq], [pss])
                    yield
                    act(lnv[:, :T], pss[:, :T], AF.Ln, [pss], [lnv], scale=1.0 / 128, bias=EPS)
                    yield
                    act(rs[:, :T], lnv[:, :T], AF.Exp, [lnv], [rs], scale=-0.5)
                    yield
                    stt(t1[:, :T], po[:, :T], hgG[:, l:l + 1], rs[:, :T], ALU.mult, ALU.mult, [po, hgG, rs], [t1])
                    yield
                    tt('pool', oB[:, h, :T], t1[:, :T], siluz[:, 4 + h, :T], ALU.mult, [t1, siluz], [oB])
                    yield
                release(2)

            _stage('pC3')
            units = []
            for h in range(8):
                blks = []
                if T >= 128:
                    for j in range(nblk - 1, -1, -1):
                        blks.append((vb0 + j, 128, 128 * j, T - 128 * j, True))
                else:
                    blks.append((vb0, T, 0, T, True))
                for kb in range(vb0 - 1, -1, -1):
                    blks.append((kb, 128, 0, T, False))
                for i, (kb, nk, c0, N, diag) in enumerate(blks):
                    units.append(dict(h=h, kb=kb, nk=nk, c0=c0, N=N, diag=diag, first=(i == 0),
                                      last=(i == len(blks) - 1)))
            for i, u in enumerate(units):
                u['next'] = units[i + 1] if (i + 1 < len(units) and not u['last']) else None
            zb = [banks[0], banks[1]]
            cbk = [banks[2], banks[3]]
            ob = [banks[4], banks[4]]
            ez = [fb[0], fb[1]]
            spt = [hb[0], hb[1], hb[2]]
            at = [hb[3], hb[4]]
            sps32 = fb[2]
            sps16 = [hb[5], hb[6], hb[7]]
            negtri = cb[:, B_NEGTRI:B_NEGTRI + 128]
            negones = cb[:, B_NEGONES:B_NEGONES + 128]
            mask01 = cb[:, B_MASK01:B_MASK01 + 128]
            nU = len(units)

            def opnds(u):
                pair, hh = u['h'] // 2, u['h'] % 2
                R0 = hh * 64
                kT = KT[l][:, pair, u['kb'] * 128:u['kb'] * 128 + u['nk']]
                qT = QTZ[hh][:, u['c0']:u['c0'] + u['N']]
                return kT, qT, QTZ[hh]

            def P1(i):
                u = units[i]
                if u['first']:
                    pair, hh = u['h'] // 2, u['h'] % 2
                    R = slice(hh * 64, hh * 64 + 64)
                    cp('dve', QTZ[hh][R, :T], QTc[pair][R, :T], [QTc[pair]], [QTZ[hh]])
                kT, qT, qb = opnds(u)
                mm(zb[i % 2][:u['nk'], :u['N']], kT, qT, [KT[l], qb], [zb[i % 2]])

            def E1(i):
                u = units[i]
                nk, N = u['nk'], u['N']
                act(ez[i % 2][:nk, :N], zb[i % 2][:nk, :N], AF.Exp, [zb[i % 2]], [ez[i % 2]])

            def LN(i):
                u = units[i]
                nk, N = u['nk'], u['N']
                s_ = spt[i % 3]
                act(s_[:nk, :N], ez[i % 2][:nk, :N], AF.Ln, [ez[i % 2]], [s_], bias=1.0)
                if u['diag']:
                    w_ = min(128, N)
                    tt('pool', s_[:nk, 0:w_], s_[:nk, 0:w_], mask01[:nk, 0:w_], ALU.mult, [s_, cb], [s_])
                if u['first']:
                    op('pool', lambda e: e.memset(sps32[:, :], 0.0), [], [sps32])
                if u['next'] is not None:
                    c0, N_ = u['c0'], u['N']
                    e_add, e_cast = ('pool', 'dve') if i % 2 == 0 else ('dve', 'pool')
                    tt(e_add, sps32[:nk, c0:c0 + N_], sps32[:nk, c0:c0 + N_], s_[:nk, :N_], ALU.add,
                       [sps32, s_], [sps32])
                    un = u['next']
                    d_ = sps16[(i + 1) % 3]
                    cp(e_cast, d_[:, un['c0']:un['c0'] + un['N']],
                       sps32[:, un['c0']:un['c0'] + un['N']], [sps32], [d_])

            def P2(i):
                u = units[i]
                nk, N, c0 = u['nk'], u['N'], u['c0']
                kT, qT, qb = opnds(u)
                s_ = spt[i % 3]
                bk = cbk[i % 2]
                mm(bk[:nk, :N], kT, qT, [KT[l], qb], [bk], start=True, stop=False, inc=False)
                if u['first']:
                    mm(bk[:nk, :N], negtri[:nk, :nk], s_[:nk, :N], [cb, s_], [bk], start=False, stop=True)
                else:
                    mm(bk[:nk, :N], negtri[:nk, :nk], s_[:nk, :N], [cb, s_], [bk], start=False, stop=False, inc=False)
                    d_ = sps16[i % 3]
                    mm(bk[:nk, :N], negones[:, :nk], d_[:, c0:c0 + N], [cb, d_], [bk],
                       start=False, stop=True)

            def A2(i):
                u = units[i]
                nk, N = u['nk'], u['N']
                a_ = at[i % 2]
                aview = a_[:nk, :N]
                act(aview, cbk[i % 2][:nk, :N], AF.Exp, [cbk[i % 2]], [a_])
                if u['diag']:
                    w_ = min(128, N)
                    av2 = a_[:nk, 0:w_]
                    tt('pool', av2, av2, mask01[:nk, 0:w_], ALU.mult, [a_, cb], [a_])

            def P3(i):
                u = units[i]
                nk, N, c0 = u['nk'], u['N'], u['c0']
                pair = u['h'] // 2
                a_ = at[i % 2]
                aview = a_[:nk, :N]
                o_ = ob[u['h'] % 2]
                mm(o_[:, c0:c0 + N], VC[l][:nk, u['kb'], pair * 128:(pair + 1) * 128], aview,
                   [VC[l], a_], [o_], start=u['first'], stop=u['last'], sgc=True)
                if u['last']:
                    hh = u['h'] % 2
                    R = slice(hh * 64, hh * 64 + 64)
                    tt('dve', oC[R, pair, :T], o_[R, :T], siluz[R, 8 + pair, :T], ALU.mult, [o_, siluz], [oC])

            lag = (0, 1, 2, 3, 4, 5) if T >= 128 else (0, 0, 0, 1, 1, 2)
            stages = (P1, E1, LN, P2, A2, P3)
            genB = phaseB()
            nB_est = 2 * nblk + 4 * (16 + 2 * nch)
            n_it = nU + lag[-1]
            done_b = 0
            for k in range(n_it):
                for fn, lg in zip(stages, lag):
                    if 0 <= k - lg < nU:
                        fn(k - lg)
                want = ((k + 1) * nB_est + n_it - 1) // n_it
                while done_b < want:
                    if next(genB, 'end') == 'end':
                        done_b = 1 << 30
                        break
                    done_b += 1
            for _ in genB:
                pass

            _stage('pM')
            m16v = big1.t[:].rearrange("p a b -> p (a b)").bitcast(BF16).rearrange("p (a b) -> p a b", b=512)
            for dc in range(8):
                w = next_unit(12 + dc)
                ya, yb_, yc = nb(), nb(), nb()
                ga, gb, gc = nb(), nb(), nb()
                for (yb2, osrc, k0) in ((ya, oA, 0), (yb_, oB, 4), (yc, oC, 8)):
                    for kc in range(4):
                        mm(yb2[:, :T], w[:, (k0 + kc) * 128:(k0 + kc + 1) * 128], osrc[:, kc, :T],
                           [w, osrc], [yb2], start=(kc == 0), stop=(kc == 3), inc=(kc == 3))
                for gi, gbk in enumerate((ga, gb, gc)):
                    for kc in range(8):
                        o0 = 1536 + gi * 1024 + kc * 128
                        mm(gbk[:, :T], w[:, o0:o0 + 128], hT[:, kc, :T], [w, hT], [gbk],
                           start=(kc == 0), stop=(kc == 7), inc=(kc == 7))
                sa, sb2, sc2 = hb[0], hb[1], hb[2]
                t1, t2, t3 = fb[0], fb[1], fb[2]
                act(sa[:, :T], ga[:, :T], AF.Sigmoid, [ga], [sa])
                act(sb2[:, :T], gb[:, :T], AF.Sigmoid, [gb], [sb2])
                act(sc2[:, :T], gc[:, :T], AF.Sigmoid, [gc], [sc2])
                tt('dve', t1[:, :T], ya[:, :T], sa[:, :T], ALU.mult, [ya, sa], [t1])
                tt('dve', t2[:, :T], yb_[:, :T], sb2[:, :T], ALU.mult, [yb_, sb2], [t2])
                tt('dve', t3[:, :T], yc[:, :T], sc2[:, :T], ALU.mult, [yc, sc2], [t3])
                tt('pool', t1[:, :T], t1[:, :T], t2[:, :T], ALU.add, [t1, t2], [t1])
                tt('pool', m16v[:, dc, :T], t1[:, :T], t3[:, :T], ALU.add, [t1, t3], [big1])
                release(1)
            wo = [next_unit(20), next_unit(21)]
            LAG = nblk if l == DEPTH - 1 else min(2, nblk)
            for b in range(nblk + LAG):
                if b < nblk:
                    r = rows_of[b]
                    for half in range(2):
                        pb = nb()
                        for dc in range(8):
                            wbuf = wo[dc // 4]
                            o0 = (dc % 4) * 1024 + half * 512
                            mm(pb[:r, :], m16v[:, dc, b * 128:b * 128 + r], wbuf[:, o0:o0 + 512],
                               [big1, wbuf], [pb], start=(dc == 0), stop=(dc == 7), inc=(dc == 7))
                        tt('dve', xres[:r, b, half * 512:(half + 1) * 512], pb[:r, :],
                           xres[:r, b, half * 512:(half + 1) * 512], ALU.add, [pb, xb[b]], [xb[b]])
                    if b == nblk - 1:
                        release(2)
                    post_dma(b, r)
                if 0 <= b - LAG < nblk:
                    post_blk(b - LAG, rows_of[b - LAG])

        for _ in range(NPS * NT + NSS):
            for l in range(DEPTH):
                for u in range(NUNITS):
                    sched_units.append((l, u))

        def seq_finish(conv_dst, hg_dst):
            for l in range(DEPTH):
                for j in range(2):
                    store(conv_dst[l][j].rearrange("(c p) -> p c", p=128), utail[l][:, :, j], [utail[l]],
                          allow_slow_non_contiguous=True)
                store(hg_dst[l].rearrange("h k v -> k h v"), Sst[l][:, :, :], [Sst[l]])

        def prompt_init():
            for l in range(DEPTH):
                op('pool', lambda e: e.memset(utail[l][:], 0.0), [], [utail[l]])
                op('pool', lambda e: e.memset(Sst[l][:], 0.0), [], [Sst[l]])

        def sample_init(s):
            for l in range(DEPTH):
                for j in range(2):
                    sload(utail[l][:, :, j], cconv[l, s, j].rearrange("(c p) -> p c", p=128), utail[l])
                S.dma('sp', Sst[l][:, :, :], shg[l, s].rearrange("h k v -> k h v"), writes=[Sst[l]], sem=ld)
                b2 = bf_view(big2).rearrange("p (a b) -> p a b", b=512)
                identB = cb[:, B_IDENT:B_IDENT + 128]
                for half in range(2):
                    S.dma('sp', big1[:, :, :], ck[l, s, half * 512:(half + 1) * 512, :].rearrange(
                        "(b p) f -> p b f", p=128), writes=[big1], sem=ld)
                    cp('dve', b2[:, 0:4, :], big1[:, :, :], [big1], [big2])
                    for blk in range(4):
                        pb = nb()
                        pv = bf_view(pb).rearrange("p (a b) -> p a b", b=128)
                        for c in range(4):
                            tr(pv[:, c, :], b2[:, blk, c * 128:(c + 1) * 128], identB, [big2, cb], [pb], inc=(c == 3))
                        kcol = (half * 4 + blk) * 128
                        cp('act', KT[l][:, :, kcol:kcol + 128], pv[:, 0:4, :], [pb], [KT[l]])
                    S.dma('sp', big1[:, :, :], cv[l, s, half * 512:(half + 1) * 512, :].rearrange(
                        "(b p) f -> p b f", p=128), writes=[big1], sem=ld)
                    cp('pool', VC[l][:, half * 4:half * 4 + 4, :], big1[:, :, :], [big1], [VC[l]])

        items = []
        for s in range(NPS):
            for ti in range(NT):
                t0 = ti * TT
                items.append(dict(
                    T=TT, kbase=t0,
                    xsrc=(lambda b, s=s, t0=t0: xp[s, t0 + b * 128:t0 + (b + 1) * 128, :]),
                    ydst=(lambda b, s=s, t0=t0: yp[s, t0 + b * 128:t0 + (b + 1) * 128, :]),
                    k_dst=[k_p[l, s, t0:t0 + TT, :] for l in range(DEPTH)],
                    v_dst=[v_p[l, s, t0:t0 + TT, :] for l in range(DEPTH)],
                    pre=(prompt_init if ti == 0 else None),
                    post=((lambda s=s: seq_finish([conv_p[l, s] for l in range(DEPTH)],
                                                  [hg_p[l, s] for l in range(DEPTH)])) if ti == NT - 1 else None)))
        for s in range(NSS):
            items.append(dict(
                T=DECT, kbase=PAST,
                xsrc=(lambda b, s=s: xs[s]), ydst=(lambda b, s=s: ys[s]),
                k_dst=[k_s[l, s] for l in range(DEPTH)], v_dst=[v_s[l, s] for l in range(DEPTH)],
                pre=(lambda s=s: sample_init(s)),
                post=(lambda s=s: seq_finish([conv_s[l, s] for l in range(DEPTH)],
                                             [hg_s[l, s] for l in range(DEPTH)]))))
        xld4 = [S.dsem(f"xld{b}") for b in range(4)]

        def it_rows(it, b):
            return min(128, it['T'] - b * 128)

        def it_nblk(it):
            return (it['T'] + 127) // 128

        def load_and_phase0(it, b):
            r = it_rows(it, b)
            S.dma('sp', xres[:r, b, :], it['xsrc'](b), writes=[xb[b]], sem=xld4[b])
            phase0_block(b, r)

        try:
            _stage('main')
            for b in range(it_nblk(items[0])):
                load_and_phase0(items[0], b)
            for n, it in enumerate(items):
                nxt = items[n + 1] if n + 1 < len(items) else None
                if it['pre'] is not None:
                    it['pre']()
                for l in range(DEPTH):
                    if l < DEPTH - 1:
                        post_dma = (lambda b, r: None)
                        post_blk = phase0_block
                    else:
                        def post_dma(b, r, it=it, nxt=nxt):
                            store(it['ydst'](b), xres[:r, b, :], [xb[b]])
                            if nxt is not None and b < it_nblk(nxt):
                                S.dma('sp', xres[:it_rows(nxt, b), b, :], nxt['xsrc'](b), writes=[xb[b]], sem=xld4[b])

                        def post_blk(b, r, it=it, nxt=nxt):
                            if nxt is not None and b < it_nblk(nxt):
                                phase0_block(b, it_rows(nxt, b))
                    tile_layer(l, it['T'], it['kbase'], it['k_dst'][l], it['v_dst'][l], post_dma, post_blk)
                if nxt is not None:
                    for b in range(it_nblk(it), it_nblk(nxt)):
                        load_and_phase0(nxt, b)
                if it['post'] is not None:
                    it['post']()

        except _StopBuild:
            pass

        for sem in st_sems:
            if sem['n']:
                S.need(_STQ, (sem['sem'], sem['n']))
        build.stats = dict(nops=S.nops, nsem=S.nsem)
    return nc


_CACHE = {}


def _get_nc(cfg_key):
    if cfg_key not in _CACHE:
        _CACHE[cfg_key] = build(Cfg(*cfg_key))
    return _CACHE[cfg_key]


def run(inputs, n_cores, nps, seq, nss):
    f = lambda a: np.ascontiguousarray(np.asarray(a, dtype=np.float32))
    x_prompt, x_sample = f(inputs['x_prompt']), f(inputs['x_sample'])
    cache_conv, state_hgrn = f(inputs['cache_conv']), f(inputs['state_hgrn'])
    cache_k, cache_v = f(inputs['cache_k']), f(inputs['cache_v'])
    consts = make_consts()
    shared = {
        "norm_g": f(inputs['norm_g']), "w_in": f(inputs['w_in']), "conv_w": f(inputs['conv_w']),
        "hg_lb": f(inputs['hg_lb_logits']), "hg_ng": f(inputs['hg_norm_g']), "q_ng": f(inputs['q_norm_g']),
        "k_ng": f(inputs['k_norm_g']), "w_br": f(inputs['w_branch']), "w_out": f(inputs['w_out']), "cst": consts,
    }
    in_maps = []
    for c in range(n_cores):
        ps, ss = slice(c * nps, (c + 1) * nps), slice(c * nss, (c + 1) * nss)
        m = dict(shared)
        m["xp"] = np.ascontiguousarray(x_prompt[ps])
        m["xs"] = np.ascontiguousarray(x_sample[ss])
        m["cconv"] = np.ascontiguousarray(cache_conv[:, ss])
        m["shg"] = np.ascontiguousarray(state_hgrn[:, ss])
        m["ck"] = np.ascontiguousarray(cache_k[:, ss].reshape(DEPTH, nss, PAST, 512))
        m["cv"] = np.ascontiguousarray(cache_v[:, ss].reshape(DEPTH, nss, PAST, 512))
        in_maps.append(m)
    nc = _get_nc((nps, seq, nss))
    res = run_bass_kernel_spmd(nc, in_maps, core_ids=list(range(n_cores)))
    R = res.results
    cat0 = lambda k: np.concatenate([r[k] for r in R], axis=0)
    cat1 = lambda k: np.concatenate([r[k] for r in R], axis=1)
    B, Bs = n_cores * nps, n_cores * nss
    return (
        cat0("yp"), cat0("ys"), cat1("conv_p"), cat1("conv_s"), cat1("hg_p"), cat1("hg_s"),
        cat1("k_p").reshape(DEPTH, B, seq, 8, 64), cat1("k_s").reshape(DEPTH, Bs, DECT, 8, 64),
        cat1("v_p").reshape(DEPTH, B, seq, 8, 64), cat1("v_s").reshape(DEPTH, Bs, DECT, 8, 64),
    )


def kernel(**inputs):
    return run(inputs, 8, 4, 2048, 2)
```

```python
import numpy as np
from contextlib import ExitStack
import concourse.bass as bass
import concourse.mybir as mybir
from concourse.bass_utils import run_bass_kernel_spmd

F32 = mybir.dt.float32
BF16 = mybir.dt.bfloat16
AF = mybir.ActivationFunctionType
ALU = mybir.AluOpType

EPS = 1e-6
D = 1024
DEPTH = 2
D_IN = 9216
USZ = 4608
NUNITS = 22
SEG_ORDER = [3, 7, 11, 8, 9, 10, 5, 0, 1, 2, 4, 6]
PAST = 1024
DECT = 16
TT = 512

C_IDENT, C_ONES, C_BLK64, C_CAUS, C_RESET, NCP = 0, 128, 256, 384, 448, 960
C_NEGTRI, C_NEGONES, C_MASK01, NCONST = 960, 1088, 1216, 1344
B_IDENT, B_NEGTRI, B_NEGONES, B_MASK01, B_CAUS, B_NEGM, NCONSTB = 0, 128, 256, 384, 512, 576, 704


def make_consts():
    c = np.zeros((128, NCONST), np.float32)
    i = np.arange(128)
    c[:, C_IDENT:C_IDENT + 128] = np.eye(128)
    c[:, C_NEGTRI:C_NEGTRI + 128] = -(i[:, None] >= i[None, :]).astype(np.float32)
    c[:, C_NEGONES:C_NEGONES + 128] = -1.0
    c[:, C_MASK01:C_MASK01 + 128] = (i[:, None] < i[None, :]).astype(np.float32)
    c[:, C_ONES:C_ONES + 128] = 1.0
    c[:, C_BLK64:C_BLK64 + 128] = ((i[:, None] // 64) == (i[None, :] // 64)).astype(np.float32)
    t = np.arange(64)
    c[:, C_CAUS:C_CAUS + 64] = ((i[:, None] % 64) <= t[None, :]).astype(np.float32)
    tt = np.arange(512)
    c[:, C_RESET:C_RESET + 512] = (tt % 64 != 0).astype(np.float32)[None, :]
    return c


class Buf:
    def __init__(self, t, name, psum=False):
        self.t = t
        self.name = name
        self.psum = psum
        self.last_write = None
        self.readers = []

    def __getitem__(self, idx):
        return self.t[idx]


class Sched:
    def __init__(self, nc, es):
        self.nc = nc
        self.es = es
        self.eng = {'pe': nc.tensor, 'act': nc.scalar, 'dve': nc.vector, 'pool': nc.gpsimd, 'sp': nc.sync}
        self.cur = {}
        self.cnt = {}
        self.waited = {e: {} for e in self.eng}
        self.pending = {e: [] for e in self.eng}
        self.nsem = 0
        for e in self.eng:
            self.cur[e] = self._sem(f"s_{e}")
            self.cnt[e] = 0
        self.nops = 0

    def _sem(self, name):
        self.nsem += 1
        return self.es.enter_context(self.nc.semaphore(name))

    def sbuf(self, name, shape, dt, es=None):
        return Buf((es or self.es).enter_context(self.nc.sbuf_tensor(name, shape, dt)), name)

    def psum(self, name, shape, dt):
        return Buf(self.es.enter_context(self.nc.psum_tensor(name, shape, dt)), name, psum=True)

    def need(self, e, ev):
        if ev is None:
            return
        sem, val = ev
        w = self.waited[e]
        if w.get(id(sem), 0) < val:
            self.eng[e].wait_ge(sem, val)
            w[id(sem)] = val

    def _deps(self, e, reads, writes):
        for b in reads:
            self.need(e, b.last_write)
            if b.psum:
                for ev in b.readers:
                    self.need(e, ev)
        for b in writes:
            self.need(e, b.last_write)
            for ev in b.readers:
                self.need(e, ev)

    def _commit(self, ev, reads, writes):
        for b in reads:
            b.readers = [r for r in b.readers if r[0] is not ev[0]]
            b.readers.append(ev)
        for b in writes:
            b.last_write = ev
            b.readers = []

    def op(self, e, fn, reads=(), writes=(), inc=True):
        self._deps(e, reads, writes)
        ins = fn(self.eng[e])
        self.nops += 1
        self.pending[e].append((reads, writes))
        if inc:
            self.cnt[e] += 1
            ins.then_inc(self.cur[e], 1)
            ev = (self.cur[e], self.cnt[e])
            for (r, w) in self.pending[e]:
                self._commit(ev, r, w)
            self.pending[e] = []
        return ins

    def dsem(self, name):
        return {'sem': self._sem(name), 'n': 0}

    def dma(self, e, out, in_, reads=(), writes=(), sem=None, group=False, **kw):
        self._deps(e, reads, writes)
        if not group and sem['n']:
            self.need(e, (sem['sem'], sem['n']))
        ins = self.eng[e].dma_start(out=out, in_=in_, **kw)
        self.nops += 1
        sem['n'] += 16
        ins.then_inc(sem['sem'], 16)
        ev = (sem['sem'], sem['n'])
        self._commit(ev, reads, writes)
        return ev


import os as _os
_DBG_STOP = _os.environ.get("KDBG_STOP", "")
_STQ = _os.environ.get("KDBG_STQ", "pool")


class _StopBuild(Exception):
    pass


_STAGE_CNT = {}


def _stage(name):
    if not _DBG_STOP:
        return
    _STAGE_CNT[name] = _STAGE_CNT.get(name, 0) + 1
    if name == _DBG_STOP or f"{name}#{_STAGE_CNT[name]}" == _DBG_STOP:
        raise _StopBuild(name)


class Cfg:
    def __init__(self, nps=4, seq=2048, nss=2):
        self.nps, self.seq, self.nss = nps, seq, nss


def build(cfg):
    nc = bass.Bass("TRN2", target_bir_lowering=False)
    NPS, SEQ, NSS = cfg.nps, cfg.seq, cfg.nss
    NT = SEQ // TT

    def din(name, shape):
        return nc.dram_tensor(name, list(shape), F32, kind="ExternalInput").ap()

    def dout(name, shape):
        return nc.dram_tensor(name, list(shape), F32, kind="ExternalOutput").ap()

    xp = din("xp", [NPS, SEQ, D])
    xs = din("xs", [NSS, DECT, D])
    cconv = din("cconv", [DEPTH, NSS, 2, 512])
    shg = din("shg", [DEPTH, NSS, 4, 128, 128])
    ck = din("ck", [DEPTH, NSS, PAST, 512])
    cv = din("cv", [DEPTH, NSS, PAST, 512])
    norm_g = din("norm_g", [DEPTH, D])
    w_in = din("w_in", [DEPTH, D, D_IN])
    conv_w = din("conv_w", [DEPTH, 3, 512])
    hg_lb = din("hg_lb", [DEPTH, 512])
    hg_ng = din("hg_ng", [DEPTH, 128])
    q_ng = din("q_ng", [DEPTH, 64])
    k_ng = din("k_ng", [DEPTH, 64])
    w_br = din("w_br", [DEPTH, 1536, D])
    w_out = din("w_out", [DEPTH, D, D])
    cst = din("cst", [128, NCONST])

    yp = dout("yp", [NPS, SEQ, D])
    ys = dout("ys", [NSS, DECT, D])
    conv_p = dout("conv_p", [DEPTH, NPS, 2, 512])
    conv_s = dout("conv_s", [DEPTH, NSS, 2, 512])
    hg_p = dout("hg_p", [DEPTH, NPS, 4, 128, 128])
    hg_s = dout("hg_s", [DEPTH, NSS, 4, 128, 128])
    k_p = dout("k_p", [DEPTH, NPS, SEQ, 512])
    k_s = dout("k_s", [DEPTH, NSS, DECT, 512])
    v_p = dout("v_p", [DEPTH, NPS, SEQ, 512])
    v_s = dout("v_s", [DEPTH, NSS, DECT, 512])

    wsc = nc.dram_tensor("wsc", [DEPTH, NUNITS, 128, USZ], BF16, kind="Internal").ap()

    with ExitStack() as es:
        S = Sched(nc, es)
        op = S.op

        def act(out, in_, func, reads, writes, **kw):
            return op('act', lambda e: e.activation(out=out, in_=in_, func=func, **kw), reads, writes)

        def tt(eng, out, in0, in1, alu, reads, writes):
            return op(eng, lambda e: e.tensor_tensor(out=out, in0=in0, in1=in1, op=alu), reads, writes)

        def ts(eng, out, in0, s1, alu, reads, writes, s2=None, alu1=None):
            if alu1 is None:
                return op(eng, lambda e: e.tensor_scalar(out=out, in0=in0, scalar1=s1, scalar2=None, op0=alu),
                          reads, writes)
            return op(eng, lambda e: e.tensor_scalar(out=out, in0=in0, scalar1=s1, scalar2=s2, op0=alu, op1=alu1),
                      reads, writes)

        def stt(out, in0, scalar, in1, op0, op1, reads, writes):
            return op('dve', lambda e: e.scalar_tensor_tensor(out=out, in0=in0, scalar=scalar, in1=in1,
                                                              op0=op0, op1=op1), reads, writes)

        def cp(eng, out, in_, reads, writes):
            if eng == 'act':
                return act(out, in_, AF.Copy, reads, writes)
            return op(eng, lambda e: e.tensor_copy(out=out, in_=in_), reads, writes)

        def mm(out, lhsT, rhs, reads, writes, start=True, stop=True, inc=True, sgc=False):
            return op('pe', lambda e: e.matmul(out, lhsT=lhsT, rhs=rhs, start=start, stop=stop,
                                               skip_group_check=sgc), reads, writes, inc=inc)

        def tr(out, in_, ident, reads, writes, inc=True):
            return op('pe', lambda e: e.transpose(out, in_, ident), reads, writes, inc=inc)

        consts = S.sbuf("consts", [128, NCP], F32)
        cb = S.sbuf("cb", [128, NCONSTB], BF16)
        xres = S.sbuf("xres", [128, 4, D], F32)
        xb = [Buf(xres.t, f"xres_b{b}") for b in range(4)]
        hT = S.sbuf("hT", [128, 8, TT], BF16)
        ring = [S.sbuf(f"ring{i}", [128, USZ], BF16) for i in range(4)]
        ring_sem = [S.dsem(f"rs{i}") for i in range(4)]
        KT = [S.sbuf(f"KT{l}", [128, 4, 2048], BF16) for l in range(DEPTH)]
        VC = [S.sbuf(f"VC{l}", [128, 16, 512], BF16) for l in range(DEPTH)]
        siluz = S.sbuf("siluz", [128, 12, TT], BF16)
        oA = S.sbuf("oA", [128, 4, TT], BF16)
        oB = S.sbuf("oB", [128, 4, TT], BF16)
        oC = S.sbuf("oC", [128, 4, TT], BF16)
        Sst = [S.sbuf(f"Sst{l}", [128, 4, 128], F32) for l in range(DEPTH)]
        utail = [S.sbuf(f"utail{l}", [128, 4, 2], F32) for l in range(DEPTH)]
        ng = S.sbuf("ng", [128, DEPTH, 8], F32)
        cw = S.sbuf("cw", [128, DEPTH, 3, 4], F32)
        lg = S.sbuf("lg", [128, DEPTH, 4], F32)
        LB = S.sbuf("LB", [128, DEPTH, 4], F32)
        OML = S.sbuf("OML", [128, DEPTH, 4], F32)
        NOML = S.sbuf("NOML", [128, DEPTH, 4], F32)
        hgG = S.sbuf("hgG", [128, DEPTH], F32)
        qg = S.sbuf("qg", [128, DEPTH], F32)
        kg = S.sbuf("kg", [128, DEPTH], F32)
        ssq = S.sbuf("ssq", [128, 4], F32)
        rstd4 = S.sbuf("rstd4", [128, 4], F32)
        lnt = S.sbuf("lnt", [128, 4], F32)
        dl = S.sbuf("dl", [128, 8], F32)
        emid = S.sbuf("emid", [128, 8], F32)
        fb = [S.sbuf(f"fb{i}", [128, 512], F32) for i in range(8)]
        hb = [S.sbuf(f"hb{i}", [128, 512], BF16) for i in range(8)]
        hbx = [S.sbuf(f"hbx{i}", [128, 512], BF16) for i in range(6)]
        hbs = [Buf(fb[6 + i // 2].t[:].bitcast(BF16)[:, (i % 2) * 512:(i % 2 + 1) * 512], f"hbs{i}") for i in range(4)]
        ccx = S.sbuf("ccx", [128, 8], F32)
        QTZ = [S.sbuf(f"qtz{i}", [128, 512], BF16) for i in range(2)]
        bmv = S.sbuf("bmv", [128, 8], F32)
        big1 = S.sbuf("big1", [128, 4, 512], F32)
        big2 = S.sbuf("big2", [128, 4, 512], F32)
        banks = [S.psum(f"bank{i}", [128, 512], F32) for i in range(8)]
        bank_i = [0]

        def nb():
            b = banks[bank_i[0] % 8]
            bank_i[0] += 1
            return b

        def bf_view(buf):
            a = buf.t[:]
            if len(a.shape) == 3:
                a = a.rearrange("p a b -> p (a b)")
            return a.bitcast(BF16)

        ld = S.dsem("ld")
        xld = S.dsem("xld")
        st_sems = [S.dsem(f"st{i}") for i in range(8)]
        st_i = [0]

        def store(out, in_, reads, **kw):
            sem = st_sems[st_i[0] % len(st_sems)]
            st_i[0] += 1
            return S.dma(_STQ, out, in_, reads=reads, sem=sem, **kw)

        def sload(out, in_, wbuf, **kw):
            ev = S.dma('sp', out, in_, writes=[wbuf], sem=ld, allow_slow_non_contiguous=True, **kw)
            return ev

        S.dma('sp', consts[:], cst[:, 0:NCP], writes=[consts], sem=ld)
        S.dma('sp', fb[0][:, 0:384], cst[:, C_NEGTRI:C_NEGTRI + 384], writes=[fb[0]], sem=ld)
        cp('dve', cb[:, B_IDENT:B_IDENT + 128], consts[:, C_IDENT:C_IDENT + 128], [consts], [cb])
        cp('dve', cb[:, B_NEGTRI:B_NEGTRI + 384], fb[0][:, 0:384], [fb[0]], [cb])
        ts('dve', cb[:, B_NEGM:B_NEGM + 128], fb[0][:, 256:384], 30000.0, ALU.mult, [fb[0]], [cb], s2=-30000.0, alu1=ALU.add)
        cp('dve', cb[:, B_CAUS:B_CAUS + 64], consts[:, C_CAUS:C_CAUS + 64], [consts], [cb])
        for qz in QTZ:
            op('pool', lambda e: e.memset(qz[:, :], 0.0), [], [qz])
        sload(ng[:], norm_g.rearrange("l (kc p) -> p l kc", p=128), ng)
        for l in range(DEPTH):
            sload(cw[:, l, :, :], conv_w[l].rearrange("j (c p) -> p j c", p=128), cw)
        sload(lg[:], hg_lb.rearrange("l (h p) -> p l h", p=128), lg)
        sload(hgG[:], hg_ng.rearrange("l p -> p l"), hgG)
        for half in range(2):
            sload(qg[half * 64:(half + 1) * 64, :], q_ng.rearrange("l d -> d l"), qg)
            sload(kg[half * 64:(half + 1) * 64, :], k_ng.rearrange("l d -> d l"), kg)
        ts('dve', qg[:], qg[:], 0.125, ALU.mult, [qg], [qg])
        op('dve', lambda e: e.memset(LB[:], 0.0), [], [LB])
        tt('dve', lnt[:, 0:4], lg[:, 1, :], lg[:, 0, :], ALU.subtract, [lg], [lnt])
        act(LB[:, 1, :], lnt[:, 0:4], AF.Sigmoid, [lnt], [LB])
        ts('dve', OML[:], LB[:], -1.0, ALU.mult, [LB], [OML], s2=1.0, alu1=ALU.add)
        ts('dve', NOML[:], LB[:], 1.0, ALU.mult, [LB], [NOML], s2=-1.0, alu1=ALU.add)

        NSTG = 4
        stgF = [(tuple(xb), xres.t[:].rearrange("p a b -> p (a b)")),
                ((VC[0],), VC[0].t[:].rearrange("p a b -> p (a b)").bitcast(F32)),
                ((VC[1],), VC[1].t[:].rearrange("p a b -> p (a b)").bitcast(F32)),
                ((KT[0],), KT[0].t[:].rearrange("p a b -> p (a b)").bitcast(F32))]
        stgB = [(ring[i], ring[i].t[:]) for i in range(4)]
        stg_sem = [S.dsem(f"stgs{i}") for i in range(NSTG)]
        stb_sem = [S.dsem(f"stbs{i}") for i in range(NSTG)]
        cast_engs = ['dve', 'act']
        ji = 0
        ei = 0
        for l in range(DEPTH):
            w_in_r = w_in[l].rearrange("(kc p) c -> p kc c", p=128)
            w_br_r = w_br[l].rearrange("(kc p) c -> p kc c", p=128)
            w_out_r = w_out[l].rearrange("(kc p) c -> p kc c", p=128)
            jobs = []
            for u in range(NUNITS):
                if u < 12:
                    g = SEG_ORDER[u]
                    jobs.append((u, 0, 4096, [(0, w_in_r[:, :, g * 512:(g + 1) * 512], 8, 512, True)]))
                elif u < 20:
                    dc = u - 12
                    gcol = [6144 + g * 1024 + dc * 128 for g in range(3)]
                    jobs.append((u, 0, 2560, [(0, w_br_r[:, :, dc * 128:(dc + 1) * 128], 12, 128, False),
                                              (1536, w_in_r[:, :, gcol[0]:gcol[0] + 128], 8, 128, True)]))
                    jobs.append((u, 2560, 2048, [(0, w_in_r[:, :, gcol[1]:gcol[1] + 128], 8, 128, True),
                                                 (1024, w_in_r[:, :, gcol[2]:gcol[2] + 128], 8, 128, True)]))
                else:
                    hf = u - 20
                    jobs.append((u, 0, 4096, [(0, w_out_r[:, hf * 4:(hf + 1) * 4, :], 4, 1024, False)]))
            for (u, off, n, pieces) in jobs:
                fbuf, fap = stgF[ji % NSTG]
                bbuf, bap = stgB[ji % NSTG]
                ssem, bsem = stg_sem[ji % NSTG], stb_sem[ji % NSTG]
                ji += 1
                for (so, src, kcs, cc_, scaled) in pieces:
                    S.dma('sp', fap[:, so:so + kcs * cc_].rearrange("p (kc c) -> p kc c", c=cc_), src,
                          writes=list(fbuf), sem=ssem, group=True)
                for (so, src, kcs, cc_, scaled) in pieces:
                    if scaled and cc_ == 512:
                        for kc in range(8):
                            e2 = cast_engs[ei % 2]
                            ei += 1
                            o_ = bap[:, so + kc * 512:so + (kc + 1) * 512]
                            i_ = fap[:, so + kc * 512:so + (kc + 1) * 512]
                            sc_ = ng[:, l, kc:kc + 1]
                            if e2 == 'act':
                                act(o_, i_, AF.Identity, [*fbuf, ng], [bbuf], scale=sc_)
                            else:
                                ts(e2, o_, i_, sc_, ALU.mult, [*fbuf, ng], [bbuf])
                    elif scaled:
                        e2 = 'dve'
                        o3 = bap[:, so:so + 1024].rearrange("p (kc c) -> p kc c", c=128)
                        i3 = fap[:, so:so + 1024].rearrange("p (kc c) -> p kc c", c=128)
                        tt(e2, o3, i3, ng[:, l, :].unsqueeze(2).to_broadcast([128, 8, 128]), ALU.mult,
                           [*fbuf, ng], [bbuf])
                    else:
                        tot = kcs * cc_
                        step = 1024
                        for o0 in range(0, tot, step):
                            e2 = cast_engs[ei % 2]
                            ei += 1
                            w_ = min(step, tot - o0)
                            cp(e2, bap[:, so + o0:so + o0 + w_], fap[:, so + o0:so + o0 + w_], list(fbuf), [bbuf])
                S.dma('pool', wsc[l, u, :, off:off + n], bap[:, 0:n], reads=[bbuf], sem=bsem)
        for bsem in stb_sem:
            S.need('sp', (bsem['sem'], bsem['n']))

        sched_units = []
        ring_state = {'issued': 0, 'cur': -1, 'rel': 0}

        def ring_issue_upto(n):
            while ring_state['issued'] < min(n, len(sched_units)):
                i = ring_state['issued']
                l, u = sched_units[i]
                slot = i % 4
                nu = USZ if 12 <= u < 20 else 4096
                S.dma('sp', ring[slot][:, 0:nu], wsc[l, u, :, 0:nu], writes=[ring[slot]], sem=ring_sem[slot])
                ring_state['issued'] += 1

        def next_unit(expect_u):
            ring_state['cur'] += 1
            i = ring_state['cur']
            assert sched_units[i][1] == expect_u, (sched_units[i], expect_u)
            assert i < ring_state['rel'] + 4
            ring_issue_upto(ring_state['rel'] + 4)
            return ring[i % 4]

        def release(n=1):
            ring_state['rel'] += n
            assert ring_state['rel'] <= ring_state['cur'] + 1
            ring_issue_upto(ring_state['rel'] + 4)

        def phase0_block(b, r):
            identB = cb[:, B_IDENT:B_IDENT + 128]
            xn = fb[0]
            junk = fb[1]
            act(bf_view(junk)[:r, :], xres[:r, b, :], AF.Square, [xb[b]], [junk, ssq], accum_out=ssq[:r, b:b + 1])
            act(lnt[:r, b:b + 1], ssq[:r, b:b + 1], AF.Ln, [ssq], [lnt], scale=1.0 / D, bias=EPS)
            act(rstd4[:r, b:b + 1], lnt[:r, b:b + 1], AF.Exp, [lnt], [rstd4], scale=-0.5)
            ts('dve', bf_view(xn)[:r, :], xres[:r, b, :], rstd4[:r, b:b + 1], ALU.mult, [xb[b], rstd4], [xn])
            pb = nb()
            pv = bf_view(pb).rearrange("p (a b) -> p a b", b=128)
            for kc in range(8):
                tr(pv[:, kc, :r], bf_view(xn)[:r, kc * 128:(kc + 1) * 128], identB[:r, :r],
                   [xn, cb], [pb], inc=(kc == 7))
            cp('act' if b % 2 == 0 else 'dve', hT[:, :, b * 128:b * 128 + r], pv[:, :, :r], [pb], [hT])

        def tile_layer(l, T, kbase, k_dst, v_dst, post_dma, post_blk):
            nblk = (T + 127) // 128
            rows_of = [min(128, T - b * 128) for b in range(nblk)]
            L = min(64, T)
            nch = T // L
            mid = L // 2 - 1
            identB = cb[:, B_IDENT:B_IDENT + 128]
            _stage('p0')

            def proj_fm(wbuf, col0, out_bank):
                for kc in range(8):
                    mm(out_bank[:, :T], wbuf[:, kc * 512 + col0:kc * 512 + col0 + 128], hT[:, kc, :T],
                       [wbuf, hT], [out_bank], start=(kc == 0), stop=(kc == 7), inc=(kc == 7))

            def proj_tm(wbuf, blk, out_bank):
                r = rows_of[blk]
                for kc in range(8):
                    mm(out_bank[:r, :], hT[:, kc, blk * 128:blk * 128 + r], wbuf[:, kc * 512:(kc + 1) * 512],
                       [wbuf, hT], [out_bank], start=(kc == 0), stop=(kc == 7), inc=(kc == 7))

            _stage('pZ')
            for br in range(3):
                w = next_unit(br)
                for c in range(4):
                    pb = nb()
                    proj_fm(w, c * 128, pb)
                    act(siluz[:, br * 4 + c, :T], pb[:, :T], AF.Silu, [pb], [siluz])
                release(1)

            _stage('pC')
            wsq, wsk, wsv = next_unit(3), next_unit(4), next_unit(5)
            QTc = hbx[0:4]
            kf = big1
            stg_tok = big2
            vb0 = kbase // 128
            for b in range(nblk):
                r = rows_of[b]
                pb = nb()
                proj_tm(wsv, b, pb)
                cp('act', stg_tok[:r, b, :], pb[:r, :], [pb], [stg_tok])
                cp('dve', VC[l][:r, vb0 + b, :], pb[:r, :], [pb], [VC[l]])
            if T >= 128:
                store(v_dst.rearrange("(b p) f -> p b f", p=128), stg_tok[:, :, :], [stg_tok])
            else:
                store(v_dst, stg_tok[:T, 0, :], [stg_tok])
            _stage('pC1')
            blk64 = consts[:, C_BLK64:C_BLK64 + 128]
            for which in range(2):
                wbuf = wsq if which == 0 else wsk
                gain = qg if which == 0 else kg
                for c in range(4):
                    pq = nb()
                    proj_fm(wbuf, c * 128, pq)
                    sq, lnv, rs = (fb[0], fb[1], fb[2]) if (which * 4 + c) % 2 == 0 else (fb[3], fb[4], fb[5])
                    act(sq[:, :T], pq[:, :T], AF.Square, [pq], [sq])
                    pss = nb()
                    mm(pss[:, :T], blk64, sq[:, :T], [consts, sq], [pss])
                    act(lnv[:, :T], pss[:, :T], AF.Ln, [pss], [lnv], scale=1.0 / 64, bias=EPS)
                    act(rs[:, :T], lnv[:, :T], AF.Exp, [lnv], [rs], scale=-0.5)
                    if which == 0:
                        stt(QTc[c][:, :T], pq[:, :T], gain[:, l:l + 1], rs[:, :T], ALU.mult, ALU.mult,
                            [pq, gain, rs], [QTc[c]])
                    else:
                        stt(kf[:, c, :T], pq[:, :T], gain[:, l:l + 1], rs[:, :T], ALU.mult, ALU.mult,
                            [pq, gain, rs], [kf])
                        cp('pool', KT[l][:, c, kbase:kbase + T], kf[:, c, :T], [kf], [KT[l]])
            release(3)
            _stage('pC2')
            identF = consts[:, C_IDENT:C_IDENT + 128]
            for b in range(nblk):
                r = rows_of[b]
                pb = nb()
                for c in range(4):
                    tr(pb[:r, c * 128:(c + 1) * 128], kf[:, c, b * 128:b * 128 + r], identF, [kf, consts], [pb],
                       inc=(c == 3))
                cp('act', stg_tok[:r, b, :], pb[:r, :], [pb], [stg_tok])
            if T >= 128:
                store(k_dst.rearrange("(b p) f -> p b f", p=128), stg_tok[:, :, :], [stg_tok])
            else:
                store(k_dst, stg_tok[:T, 0, :], [stg_tok])

            _stage('pF')
            sg = big1
            w = next_unit(6)
            for h in range(4):
                pb = nb()
                proj_fm(w, h * 128, pb)
                act(sg[:, h, :T], pb[:, :T], AF.Sigmoid, [pb], [sg])
            release(1)

            _stage('pA')
            wx, wb_, wc = next_unit(7), next_unit(8), next_unit(9)
            for c in range(4):
                px, pbb, pc = nb(), nb(), nb()
                proj_fm(wx, c * 128, px)
                proj_fm(wc, c * 128, pc)
                proj_fm(wb_, c * 128, pbb)
                xsb, uext, y0, y1 = fb[2], fb[3], fb[4], fb[5]
                cp('act', xsb[:, :T], px[:, :T], [px], [xsb])
                tt('dve', uext[:, :T], pc[:, :T], xsb[:, :T], ALU.mult, [pc, xsb], [uext])
                cw0, cw1, cw2 = (cw[:, l, j, c:c + 1] for j in range(3))
                act(y0[:, :T], uext[:, :T], AF.Identity, [uext, cw], [y0], scale=cw2)
                stt(y0[:, 1:T], uext[:, 0:T - 1], cw1, y0[:, 1:T], ALU.mult, ALU.add, [uext, cw, y0], [y0])
                stt(y0[:, 2:T], uext[:, 0:T - 2], cw0, y0[:, 2:T], ALU.mult, ALU.add, [uext, cw, y0], [y0])
                stt(y0[:, 0:1], utail[l][:, c, 1:2], cw1, y0[:, 0:1], ALU.mult, ALU.add, [utail[l], cw, y0], [y0])
                stt(y0[:, 0:1], utail[l][:, c, 0:1], cw0, y0[:, 0:1], ALU.mult, ALU.add, [utail[l], cw, y0], [y0])
                stt(y0[:, 1:2], utail[l][:, c, 1:2], cw0, y0[:, 1:2], ALU.mult, ALU.add, [utail[l], cw, y0], [y0])
                cp('pool', utail[l][:, c, :], uext[:, T - 2:T], [uext], [utail[l]])
                tt('dve', y1[:, :T], pbb[:, :T], y0[:, :T], ALU.mult, [pbb, y0], [y1])
                tt('pool', oA[:, c, :T], y1[:, :T], siluz[:, c, :T], ALU.mult, [y1, siluz], [oA])
            release(3)

            _stage('pB')
            RB = [banks[5], banks[6], banks[7]]

            def phaseB():
                wq, wi = next_unit(10), next_unit(11)
                b2 = bf_view(big2).rearrange("p (a b) -> p a b", b=512)
                ktok_v = b2[:, 0:4, :]
                v16_v = b2[:, 4:8, :]
                for b in range(nblk):
                    r = rows_of[b]
                    pb = RB[1 + b % 2]
                    proj_tm(wi, b, pb)
                    yield
                    cp('dve', v16_v[:r, b, :], pb[:r, :], [pb], [big2])
                    yield
                F1, F2, F3 = fb[3], fb[4], fb[5]
                kk, eq, ek, qt = hbs[0], hbs[1], hbs[2], hbs[3]
                kt, scm = hbx[4], hbx[5]
                St16 = bf_view(F3).rearrange("p (a b) -> p a b", b=128)
                for h in range(4):
                    pq = RB[0]
                    proj_fm(wq, h * 128, pq)
                    yield
                    logf, bcs = F1, F2
                    act(logf[:, :T], sg[:, h, :T], AF.Ln, [sg, OML, LB], [logf],
                        scale=OML[:, l, h:h + 1], bias=LB[:, l, h:h + 1])
                    ts('dve', kk[:, :T], sg[:, h, :T], NOML[:, l, h:h + 1], ALU.mult, [sg, OML, NOML], [kk],
                       s2=OML[:, l, h:h + 1], alu1=ALU.add)
                    yield
                    op('dve', lambda e: e.tensor_tensor_scan(out=bcs[:, :T], data0=consts[:, C_RESET:C_RESET + T],
                                                             data1=logf[:, :T], initial=0.0,
                                                             op0=ALU.mult, op1=ALU.add), [consts, logf], [bcs])
                    yield
                    b3 = bcs[:, :T].rearrange("p (c l) -> p c l", l=L)
                    act(dl[:, :nch], b3[:, :, L - 1], AF.Exp, [bcs], [dl])
                    act(emid[:, :nch], b3[:, :, mid], AF.Exp, [bcs], [emid])
                    cp('dve', bmv[:, :nch], b3[:, :, mid], [bcs], [bmv])
                    yield
                    tt('dve', b3, b3, bmv[:, :nch].unsqueeze(2).to_broadcast([128, nch, L]), ALU.subtract,
                       [bcs, bmv], [bcs])
                    yield
                    act(eq[:, :T], bcs[:, :T], AF.Exp, [bcs], [eq])
                    act(ek[:, :T], bcs[:, :T], AF.Exp, [bcs], [ek], scale=-1.0)
                    act(ccx[:, :nch], b3[:, :, L - 1], AF.Exp, [bcs], [ccx])
                    yield
                    tt('dve', qt[:, :T], pq[:, :T], eq[:, :T], ALU.mult, [pq, eq], [qt])
                    tt('pool', kt[:, :T], kk[:, :T], ek[:, :T], ALU.mult, [kk, ek], [kt])
                    yield
                    pb = RB[1]
                    pv = bf_view(pb).rearrange("p (a b) -> p a b", b=128)
                    for b in range(nblk):
                        r = rows_of[b]
                        tr(pv[:r, b, :], kt[:, b * 128:b * 128 + r], identB, [kt, cb], [pb], inc=(b == nblk - 1))
                    yield
                    rr = rows_of[0]
                    cp('dve', ktok_v[:rr, 0:nblk, h * 128:(h + 1) * 128], pv[:rr, 0:nblk, :], [pb], [big2])
                    yield
                    psc = RB[1]
                    psc3 = psc[:, 0:256].rearrange("p (a b) -> p a b", b=64)
                    for c in range(nch):
                        blk, par = c // 2, c % 2
                        r0 = par * 64
                        mm(psc3[r0:r0 + L, blk, 0:L], kt[:, c * L:(c + 1) * L], qt[:, c * L:(c + 1) * L],
                           [kt, qt], [psc])
                    yield
                    scm3 = scm[:, 0:256].rearrange("p (a b) -> p a b", b=64)
                    caus = consts[:, C_CAUS:C_CAUS + 64]
                    if T >= 128:
                        tt('dve', scm3[:, 0:nblk, :], psc3[:, 0:nblk, :],
                           caus.unsqueeze(1).to_broadcast([128, nblk, 64]), ALU.mult, [psc, consts], [scm])
                    else:
                        tt('dve', scm3[:L, 0, 0:L], psc3[:L, 0, 0:L], caus[:L, 0:L], ALU.mult, [psc, consts], [scm])
                    yield
                    po = RB[2]
                    pu = RB[0]
                    tmpu = F1
                    for c in range(nch):
                        blk, par = c // 2, c % 2
                        r0 = par * 64
                        Sv = Sst[l][:, h, :]
                        pus = pu[:, (c % 4) * 128:(c % 4 + 1) * 128]
                        mm(pus, ktok_v[r0:r0 + L, blk, h * 128:(h + 1) * 128],
                           v16_v[r0:r0 + L, blk, h * 128:(h + 1) * 128], [big2], [pu])
                        ts('dve', St16[:, c, :], Sv, emid[:, c:c + 1], ALU.mult, [Sst[l], emid], [F3])
                        yield
                        ts('dve', tmpu[:, 0:128], pus, ccx[:, c:c + 1], ALU.mult, [pu, ccx], [tmpu])
                        stt(Sv, Sv, dl[:, c:c + 1], tmpu[:, 0:128], ALU.mult, ALU.add, [Sst[l], dl, tmpu], [Sst[l]])
                        mm(po[:, c * L:(c + 1) * L], v16_v[r0:r0 + L, blk, h * 128:(h + 1) * 128],
                           scm3[r0:r0 + L, blk, 0:L], [big2, scm], [po], start=True, stop=False, inc=False)
                        mm(po[:, c * L:(c + 1) * L], St16[:, c, :], qt[:, c * L:(c + 1) * L],
                           [F3, qt], [po], start=False, stop=True)
                        yield
                    sq, lnv, rs, t1 = F2, F1, F2, F1
                    act(sq[:, :T], po[:, :T], AF.Square, [po], [sq])
                    yield
                    pss = RB[1]
                    mm(pss[:, :T], consts[:, C_ONES:C_ONES + 128], sq[:, :T], [consts, sq], [pss])
                    yield
                    act(lnv[:, :T], pss[:, :T], AF.Ln, [pss], [lnv], scale=1.0 / 128, bias=EPS)
                    yield
                    act(rs[:, :T], lnv[:, :T], AF.Exp, [lnv], [rs], scale=-0.5)
                    yield
                    stt(t1[:, :T], po[:, :T], hgG[:, l:l + 1], rs[:, :T], ALU.mult, ALU.mult, [po, hgG, rs], [t1])
                    yield
                    tt('pool', oB[:, h, :T], t1[:, :T], siluz[:, 4 + h, :T], ALU.mult, [t1, siluz], [oB])
                    yield
                release(2)

            _stage('pC3')
            units = []
            for h in range(8):
                blks = []
                if T >= 128:
                    for j in range(nblk - 1, -1, -1):
                        blks.append((vb0 + j, 128, 128 * j, T - 128 * j, True))
                else:
                    blks.append((vb0, T, 0, T, True))
                for kb in range(vb0 - 1, -1, -1):
                    blks.append((kb, 128, 0, T, False))
                for i, (kb, nk, c0, N, diag) in enumerate(blks):
                    units.append(dict(h=h, kb=kb, nk=nk, c0=c0, N=N, diag=diag, first=(i == 0),
                                      last=(i == len(blks) - 1)))
            for i, u in enumerate(units):
                u['next'] = units[i + 1] if (i + 1 < len(units) and not u['last']) else None
            zb = [banks[0], banks[1]]
            cbk = [banks[2], banks[3]]
            ob = [banks[4], banks[4]]
            ez = [fb[0], fb[1]]
            spt = [hb[0], hb[1], hb[2]]
            at = [hb[3], hb[4]]
            sps32 = fb[2]
            sps16 = [hb[5], hb[6], hb[7]]
            negtri = cb[:, B_NEGTRI:B_NEGTRI + 128]
            negones = cb[:, B_NEGONES:B_NEGONES + 128]
            mask01 = cb[:, B_MASK01:B_MASK01 + 128]
            negm = cb[:, B_NEGM:B_NEGM + 128]
            nU = len(units)

            def opnds(u):
                pair, hh = u['h'] // 2, u['h'] % 2
                R0 = hh * 64
                kT = KT[l][:, pair, u['kb'] * 128:u['kb'] * 128 + u['nk']]
                qT = QTZ[hh][:, u['c0']:u['c0'] + u['N']]
                return kT, qT, QTZ[hh]

            def P1(i):
                u = units[i]
                if u['first']:
                    pair, hh = u['h'] // 2, u['h'] % 2
                    R = slice(hh * 64, hh * 64 + 64)
                    cp('dve', QTZ[hh][R, :T], QTc[pair][R, :T], [QTc[pair]], [QTZ[hh]])
                kT, qT, qb = opnds(u)
                if u['diag']:
                    nk_, w_ = u['nk'], min(128, u['N'])
                    mm(zb[i % 2][:nk_, :u['N']], kT, qT, [KT[l], qb], [zb[i % 2]], start=True, stop=False, inc=False)
                    mm(zb[i % 2][:nk_, 0:w_], identB[:nk_, :nk_], negm[:nk_, 0:w_], [cb], [zb[i % 2]],
                       start=False, stop=True, sgc=True)
                else:
                    mm(zb[i % 2][:u['nk'], :u['N']], kT, qT, [KT[l], qb], [zb[i % 2]])

            def E1(i):
                u = units[i]
                nk, N = u['nk'], u['N']
                act(ez[i % 2][:nk, :N], zb[i % 2][:nk, :N], AF.Exp, [zb[i % 2]], [ez[i % 2]])

            def LN(i):
                u = units[i]
                nk, N = u['nk'], u['N']
                s_ = spt[i % 3]
                act(s_[:nk, :N], ez[i % 2][:nk, :N], AF.Ln, [ez[i % 2]], [s_], bias=1.0)
                if u['first']:
                    op('pool', lambda e: e.memset(sps32[:, :], 0.0), [], [sps32])
                if u['next'] is not None:
                    c0, N_ = u['c0'], u['N']
                    tt('pool', sps32[:nk, c0:c0 + N_], sps32[:nk, c0:c0 + N_], s_[:nk, :N_], ALU.add,
                       [sps32, s_], [sps32])
                    un = u['next']
                    d_ = sps16[(i + 1) % 3]
                    cp('dve', d_[:, un['c0']:un['c0'] + un['N']],
                       sps32[:, un['c0']:un['c0'] + un['N']], [sps32], [d_])

            def P2(i):
                u = units[i]
                nk, N, c0 = u['nk'], u['N'], u['c0']
                kT, qT, qb = opnds(u)
                s_ = spt[i % 3]
                bk = cbk[i % 2]
                mm(bk[:nk, :N], kT, qT, [KT[l], qb], [bk], start=True, stop=False, inc=False)
                if u['diag']:
                    w_ = min(128, N)
                    mm(bk[:nk, 0:w_], identB[:nk, :nk], negm[:nk, 0:w_], [cb], [bk], start=False, stop=False,
                       inc=False, sgc=True)
                if u['first']:
                    mm(bk[:nk, :N], negtri[:nk, :nk], s_[:nk, :N], [cb, s_], [bk], start=False, stop=True)
                else:
                    mm(bk[:nk, :N], negtri[:nk, :nk], s_[:nk, :N], [cb, s_], [bk], start=False, stop=False, inc=False)
                    d_ = sps16[i % 3]
                    mm(bk[:nk, :N], negones[:, :nk], d_[:, c0:c0 + N], [cb, d_], [bk],
                       start=False, stop=True)

            def A2(i):
                u = units[i]
                nk, N = u['nk'], u['N']
                a_ = at[i % 2]
                aview = a_[:nk, :N]
                act(aview, cbk[i % 2][:nk, :N], AF.Exp, [cbk[i % 2]], [a_])

            def P3(i):
                u = units[i]
                nk, N, c0 = u['nk'], u['N'], u['c0']
                pair = u['h'] // 2
                a_ = at[i % 2]
                aview = a_[:nk, :N]
                o_ = ob[u['h'] % 2]
                mm(o_[:, c0:c0 + N], VC[l][:nk, u['kb'], pair * 128:(pair + 1) * 128], aview,
                   [VC[l], a_], [o_], start=u['first'], stop=u['last'], sgc=True)
                if u['last']:
                    hh = u['h'] % 2
                    R = slice(hh * 64, hh * 64 + 64)
                    tt('dve', oC[R, pair, :T], o_[R, :T], siluz[R, 8 + pair, :T], ALU.mult, [o_, siluz], [oC])

            lag = (0, 1, 2, 3, 4, 5) if T >= 128 else (0, 0, 0, 1, 1, 2)
            stages = (P1, E1, LN, P2, A2, P3)
            genB = phaseB()
            nB_est = 2 * nblk + 4 * (16 + 2 * nch)
            n_it = nU + lag[-1]
            done_b = 0
            for k in range(n_it):
                for fn, lg in zip(stages, lag):
                    if 0 <= k - lg < nU:
                        fn(k - lg)
                want = ((k + 1) * nB_est + n_it - 1) // n_it
                while done_b < want:
                    if next(genB, 'end') == 'end':
                        done_b = 1 << 30
                        break
                    done_b += 1
            for _ in genB:
                pass

            _stage('pM')
            m16v = big1.t[:].rearrange("p a b -> p (a b)").bitcast(BF16).rearrange("p (a b) -> p a b", b=512)
            for dc in range(8):
                w = next_unit(12 + dc)
                ya, yb_, yc = nb(), nb(), nb()
                ga, gb, gc = nb(), nb(), nb()
                for (yb2, osrc, k0) in ((ya, oA, 0), (yb_, oB, 4), (yc, oC, 8)):
                    for kc in range(4):
                        mm(yb2[:, :T], w[:, (k0 + kc) * 128:(k0 + kc + 1) * 128], osrc[:, kc, :T],
                           [w, osrc], [yb2], start=(kc == 0), stop=(kc == 3), inc=(kc == 3))
                for gi, gbk in enumerate((ga, gb, gc)):
                    for kc in range(8):
                        o0 = 1536 + gi * 1024 + kc * 128
                        mm(gbk[:, :T], w[:, o0:o0 + 128], hT[:, kc, :T], [w, hT], [gbk],
                           start=(kc == 0), stop=(kc == 7), inc=(kc == 7))
                sa, sb2, sc2 = hb[0], hb[1], hb[2]
                t1, t2, t3 = fb[0], fb[1], fb[2]
                act(sa[:, :T], ga[:, :T], AF.Sigmoid, [ga], [sa])
                act(sb2[:, :T], gb[:, :T], AF.Sigmoid, [gb], [sb2])
                act(sc2[:, :T], gc[:, :T], AF.Sigmoid, [gc], [sc2])
                tt('dve', t1[:, :T], ya[:, :T], sa[:, :T], ALU.mult, [ya, sa], [t1])
                tt('dve', t2[:, :T], yb_[:, :T], sb2[:, :T], ALU.mult, [yb_, sb2], [t2])
                tt('dve', t3[:, :T], yc[:, :T], sc2[:, :T], ALU.mult, [yc, sc2], [t3])
                tt('pool', t1[:, :T], t1[:, :T], t2[:, :T], ALU.add, [t1, t2], [t1])
                tt('pool', m16v[:, dc, :T], t1[:, :T], t3[:, :T], ALU.add, [t1, t3], [big1])
                release(1)
            wo = [next_unit(20), next_unit(21)]
            LAG = nblk if l == DEPTH - 1 else min(2, nblk)
            for b in range(nblk + LAG):
                if b < nblk:
                    r = rows_of[b]
                    for half in range(2):
                        pb = nb()
                        for dc in range(8):
                            wbuf = wo[dc // 4]
                            o0 = (dc % 4) * 1024 + half * 512
                            mm(pb[:r, :], m16v[:, dc, b * 128:b * 128 + r], wbuf[:, o0:o0 + 512],
                               [big1, wbuf], [pb], start=(dc == 0), stop=(dc == 7), inc=(dc == 7))
                        tt('dve', xres[:r, b, half * 512:(half + 1) * 512], pb[:r, :],
                           xres[:r, b, half * 512:(half + 1) * 512], ALU.add, [pb, xb[b]], [xb[b]])
                    if b == nblk - 1:
                        release(2)
                    post_dma(b, r)
                if 0 <= b - LAG < nblk:
                    post_blk(b - LAG, rows_of[b - LAG])

        for _ in range(NPS * NT + NSS):
            for l in range(DEPTH):
                for u in range(NUNITS):
                    sched_units.append((l, u))

        def seq_finish(conv_dst, hg_dst):
            for l in range(DEPTH):
                for j in range(2):
                    store(conv_dst[l][j].rearrange("(c p) -> p c", p=128), utail[l][:, :, j], [utail[l]],
                          allow_slow_non_contiguous=True)
                store(hg_dst[l].rearrange("h k v -> k h v"), Sst[l][:, :, :], [Sst[l]])

        def prompt_init():
            for l in range(DEPTH):
                op('pool', lambda e: e.memset(utail[l][:], 0.0), [], [utail[l]])
                op('pool', lambda e: e.memset(Sst[l][:], 0.0), [], [Sst[l]])

        def sample_init(s):
            for l in range(DEPTH):
                for j in range(2):
                    sload(utail[l][:, :, j], cconv[l, s, j].rearrange("(c p) -> p c", p=128), utail[l])
                S.dma('sp', Sst[l][:, :, :], shg[l, s].rearrange("h k v -> k h v"), writes=[Sst[l]], sem=ld)
                b2 = bf_view(big2).rearrange("p (a b) -> p a b", b=512)
                identB = cb[:, B_IDENT:B_IDENT + 128]
                for half in range(2):
                    S.dma('sp', big1[:, :, :], ck[l, s, half * 512:(half + 1) * 512, :].rearrange(
                        "(b p) f -> p b f", p=128), writes=[big1], sem=ld)
                    cp('dve', b2[:, 0:4, :], big1[:, :, :], [big1], [big2])
                    for blk in range(4):
                        pb = nb()
                        pv = bf_view(pb).rearrange("p (a b) -> p a b", b=128)
                        for c in range(4):
                            tr(pv[:, c, :], b2[:, blk, c * 128:(c + 1) * 128], identB, [big2, cb], [pb], inc=(c == 3))
                        kcol = (half * 4 + blk) * 128
                        cp('act', KT[l][:, :, kcol:kcol + 128], pv[:, 0:4, :], [pb], [KT[l]])
                    S.dma('sp', big1[:, :, :], cv[l, s, half * 512:(half + 1) * 512, :].rearrange(
                        "(b p) f -> p b f", p=128), writes=[big1], sem=ld)
                    cp('pool', VC[l][:, half * 4:half * 4 + 4, :], big1[:, :, :], [big1], [VC[l]])

        items = []
        for s in range(NPS):
            for ti in range(NT):
                t0 = ti * TT
                items.append(dict(
                    T=TT, kbase=t0,
                    xsrc=(lambda b, s=s, t0=t0: xp[s, t0 + b * 128:t0 + (b + 1) * 128, :]),
                    ydst=(lambda b, s=s, t0=t0: yp[s, t0 + b * 128:t0 + (b + 1) * 128, :]),
                    k_dst=[k_p[l, s, t0:t0 + TT, :] for l in range(DEPTH)],
                    v_dst=[v_p[l, s, t0:t0 + TT, :] for l in range(DEPTH)],
                    pre=(prompt_init if ti == 0 else None),
                    post=((lambda s=s: seq_finish([conv_p[l, s] for l in range(DEPTH)],
                                                  [hg_p[l, s] for l in range(DEPTH)])) if ti == NT - 1 else None)))
        for s in range(NSS):
            items.append(dict(
                T=DECT, kbase=PAST,
                xsrc=(lambda b, s=s: xs[s]), ydst=(lambda b, s=s: ys[s]),
                k_dst=[k_s[l, s] for l in range(DEPTH)], v_dst=[v_s[l, s] for l in range(DEPTH)],
                pre=(lambda s=s: sample_init(s)),
                post=(lambda s=s: seq_finish([conv_s[l, s] for l in range(DEPTH)],
                                             [hg_s[l, s] for l in range(DEPTH)]))))
        xld4 = [S.dsem(f"xld{b}") for b in range(4)]

        def it_rows(it, b):
            return min(128, it['T'] - b * 128)

        def it_nblk(it):
            return (it['T'] + 127) // 128

        def load_and_phase0(it, b):
            r = it_rows(it, b)
            S.dma('sp', xres[:r, b, :], it['xsrc'](b), writes=[xb[b]], sem=xld4[b])
            phase0_block(b, r)

        try:
            _stage('main')
            for b in range(it_nblk(items[0])):
                load_and_phase0(items[0], b)
            for n, it in enumerate(items):
                nxt = items[n + 1] if n + 1 < len(items) else None
                if it['pre'] is not None:
                    it['pre']()
                for l in range(DEPTH):
                    if l < DEPTH - 1:
                        post_dma = (lambda b, r: None)
                        post_blk = phase0_block
                    else:
                        def post_dma(b, r, it=it, nxt=nxt):
                            store(it['ydst'](b), xres[:r, b, :], [xb[b]])
                            if nxt is not None and b < it_nblk(nxt):
                                S.dma('sp', xres[:it_rows(nxt, b), b, :], nxt['xsrc'](b), writes=[xb[b]], sem=xld4[b])

                        def post_blk(b, r, it=it, nxt=nxt):
                            if nxt is not None and b < it_nblk(nxt):
                                phase0_block(b, it_rows(nxt, b))
                    tile_layer(l, it['T'], it['kbase'], it['k_dst'][l], it['v_dst'][l], post_dma, post_blk)
                if nxt is not None:
                    for b in range(it_nblk(it), it_nblk(nxt)):
                        load_and_phase0(nxt, b)
                if it['post'] is not None:
                    it['post']()

        except _StopBuild:
            pass

        for sem in st_sems:
            if sem['n']:
                S.need(_STQ, (sem['sem'], sem['n']))
        build.stats = dict(nops=S.nops, nsem=S.nsem)
    return nc


_CACHE = {}


def _get_nc(cfg_key):
    if cfg_key not in _CACHE:
        _CACHE[cfg_key] = build(Cfg(*cfg_key))
    return _CACHE[cfg_key]


def run(inputs, n_cores, nps, seq, nss):
    f = lambda a: np.ascontiguousarray(np.asarray(a, dtype=np.float32))
    x_prompt, x_sample = f(inputs['x_prompt']), f(inputs['x_sample'])
    cache_conv, state_hgrn = f(inputs['cache_conv']), f(inputs['state_hgrn'])
    cache_k, cache_v = f(inputs['cache_k']), f(inputs['cache_v'])
    consts = make_consts()
    shared = {
        "norm_g": f(inputs['norm_g']), "w_in": f(inputs['w_in']), "conv_w": f(inputs['conv_w']),
        "hg_lb": f(inputs['hg_lb_logits']), "hg_ng": f(inputs['hg_norm_g']), "q_ng": f(inputs['q_norm_g']),
        "k_ng": f(inputs['k_norm_g']), "w_br": f(inputs['w_branch']), "w_out": f(inputs['w_out']), "cst": consts,
    }
    in_maps = []
    for c in range(n_cores):
        ps, ss = slice(c * nps, (c + 1) * nps), slice(c * nss, (c + 1) * nss)
        m = dict(shared)
        m["xp"] = np.ascontiguousarray(x_prompt[ps])
        m["xs"] = np.ascontiguousarray(x_sample[ss])
        m["cconv"] = np.ascontiguousarray(cache_conv[:, ss])
        m["shg"] = np.ascontiguousarray(state_hgrn[:, ss])
        m["ck"] = np.ascontiguousarray(cache_k[:, ss].reshape(DEPTH, nss, PAST, 512))
        m["cv"] = np.ascontiguousarray(cache_v[:, ss].reshape(DEPTH, nss, PAST, 512))
        in_maps.append(m)
    nc = _get_nc((nps, seq, nss))
    res = run_bass_kernel_spmd(nc, in_maps, core_ids=list(range(n_cores)))
    R = res.results
    cat0 = lambda k: np.concatenate([r[k] for r in R], axis=0)
    cat1 = lambda k: np.concatenate([r[k] for r in R], axis=1)
    B, Bs = n_cores * nps, n_cores * nss
    return (
        cat0("yp"), cat0("ys"), cat1("conv_p"), cat1("conv_s"), cat1("hg_p"), cat1("hg_s"),
        cat1("k_p").reshape(DEPTH, B, seq, 8, 64), cat1("k_s").reshape(DEPTH, Bs, DECT, 8, 64),
        cat1("v_p").reshape(DEPTH, B, seq, 8, 64), cat1("v_s").reshape(DEPTH, Bs, DECT, 8, 64),
    )


def kernel(**inputs):
    return run(inputs, 8, 4, 2048, 2)
```

```python
import numpy as np
from contextlib import ExitStack
import concourse.bass as bass
import concourse.mybir as mybir
from concourse.bass_utils import run_bass_kernel_spmd

F32 = mybir.dt.float32
BF16 = mybir.dt.bfloat16
AF = mybir.ActivationFunctionType
ALU = mybir.AluOpType

EPS = 1e-6
D = 1024
DEPTH = 2
D_IN = 9216
USZ = 4608
NUNITS = 22
SEG_ORDER = [3, 7, 11, 8, 9, 10, 5, 0, 1, 2, 4, 6]
PAST = 1024
DECT = 16
TT = 512

C_IDENT, C_ONES, C_BLK64, C_CAUS, C_RESET, NCP = 0, 128, 256, 384, 448, 960
C_NEGTRI, C_NEGONES, C_MASK01, NCONST = 960, 1088, 1216, 1344
B_IDENT, B_NEGTRI, B_NEGONES, B_MASK01, B_CAUS, NCONSTB = 0, 128, 256, 384, 512, 576


def make_consts():
    c = np.zeros((128, NCONST), np.float32)
    i = np.arange(128)
    c[:, C_IDENT:C_IDENT + 128] = np.eye(128)
    c[:, C_NEGTRI:C_NEGTRI + 128] = -(i[:, None] >= i[None, :]).astype(np.float32)
    c[:, C_NEGONES:C_NEGONES + 128] = -1.0
    c[:, C_MASK01:C_MASK01 + 128] = (i[:, None] < i[None, :]).astype(np.float32)
    c[:, C_ONES:C_ONES + 128] = 1.0
    c[:, C_BLK64:C_BLK64 + 128] = ((i[:, None] // 64) == (i[None, :] // 64)).astype(np.float32)
    t = np.arange(64)
    c[:, C_CAUS:C_CAUS + 64] = ((i[:, None] % 64) <= t[None, :]).astype(np.float32)
    tt = np.arange(512)
    c[:, C_RESET:C_RESET + 512] = (tt % 64 != 0).astype(np.float32)[None, :]
    return c


class Buf:
    def __init__(self, t, name, psum=False):
        self.t = t
        self.name = name
        self.psum = psum
        self.last_write = None
        self.readers = []

    def __getitem__(self, idx):
        return self.t[idx]


class Sched:
    def __init__(self, nc, es):
        self.nc = nc
        self.es = es
        self.eng = {'pe': nc.tensor, 'act': nc.scalar, 'dve': nc.vector, 'pool': nc.gpsimd, 'sp': nc.sync}
        self.cur = {}
        self.cnt = {}
        self.waited = {e: {} for e in self.eng}
        self.pending = {e: [] for e in self.eng}
        self.nsem = 0
        for e in self.eng:
            self.cur[e] = self._sem(f"s_{e}")
            self.cnt[e] = 0
        self.nops = 0

    def _sem(self, name):
        self.nsem += 1
        return self.es.enter_context(self.nc.semaphore(name))

    def sbuf(self, name, shape, dt, es=None):
        return Buf((es or self.es).enter_context(self.nc.sbuf_tensor(name, shape, dt)), name)

    def psum(self, name, shape, dt):
        return Buf(self.es.enter_context(self.nc.psum_tensor(name, shape, dt)), name, psum=True)

    def need(self, e, ev):
        if ev is None:
            return
        sem, val = ev
        w = self.waited[e]
        if w.get(id(sem), 0) < val:
            self.eng[e].wait_ge(sem, val)
            w[id(sem)] = val

    def _deps(self, e, reads, writes):
        for b in reads:
            self.need(e, b.last_write)
            if b.psum:
                for ev in b.readers:
                    self.need(e, ev)
        for b in writes:
            self.need(e, b.last_write)
            for ev in b.readers:
                self.need(e, ev)

    def _commit(self, ev, reads, writes):
        for b in reads:
            b.readers = [r for r in b.readers if r[0] is not ev[0]]
            b.readers.append(ev)
        for b in writes:
            b.last_write = ev
            b.readers = []

    def op(self, e, fn, reads=(), writes=(), inc=True):
        self._deps(e, reads, writes)
        ins = fn(self.eng[e])
        self.nops += 1
        self.pending[e].append((reads, writes))
        if inc:
            self.cnt[e] += 1
            ins.then_inc(self.cur[e], 1)
            ev = (self.cur[e], self.cnt[e])
            for (r, w) in self.pending[e]:
                self._commit(ev, r, w)
            self.pending[e] = []
        return ins

    def dsem(self, name):
        return {'sem': self._sem(name), 'n': 0}

    def dma(self, e, out, in_, reads=(), writes=(), sem=None, group=False, **kw):
        self._deps(e, reads, writes)
        if not group and sem['n']:
            self.need(e, (sem['sem'], sem['n']))
        ins = self.eng[e].dma_start(out=out, in_=in_, **kw)
        self.nops += 1
        sem['n'] += 16
        ins.then_inc(sem['sem'], 16)
        ev = (sem['sem'], sem['n'])
        self._commit(ev, reads, writes)
        return ev


import os as _os
_DBG_STOP = _os.environ.get("KDBG_STOP", "")
_STQ = _os.environ.get("KDBG_STQ", "pool")


class _StopBuild(Exception):
    pass


_STAGE_CNT = {}


def _stage(name):
    if not _DBG_STOP:
        return
    _STAGE_CNT[name] = _STAGE_CNT.get(name, 0) + 1
    if name == _DBG_STOP or f"{name}#{_STAGE_CNT[name]}" == _DBG_STOP:
        raise _StopBuild(name)


class Cfg:
    def __init__(self, nps=4, seq=2048, nss=2):
        self.nps, self.seq, self.nss = nps, seq, nss


def build(cfg):
    nc = bass.Bass("TRN2", target_bir_lowering=False)
    NPS, SEQ, NSS = cfg.nps, cfg.seq, cfg.nss
    NT = SEQ // TT

    def din(name, shape):
        return nc.dram_tensor(name, list(shape), F32, kind="ExternalInput").ap()

    def dout(name, shape):
        return nc.dram_tensor(name, list(shape), F32, kind="ExternalOutput").ap()

    xp = din("xp", [NPS, SEQ, D])
    xs = din("xs", [NSS, DECT, D])
    cconv = din("cconv", [DEPTH, NSS, 2, 512])
    shg = din("shg", [DEPTH, NSS, 4, 128, 128])
    ck = din("ck", [DEPTH, NSS, PAST, 512])
    cv = din("cv", [DEPTH, NSS, PAST, 512])
    norm_g = din("norm_g", [DEPTH, D])
    w_in = din("w_in", [DEPTH, D, D_IN])
    conv_w = din("conv_w", [DEPTH, 3, 512])
    hg_lb = din("hg_lb", [DEPTH, 512])
    hg_ng = din("hg_ng", [DEPTH, 128])
    q_ng = din("q_ng", [DEPTH, 64])
    k_ng = din("k_ng", [DEPTH, 64])
    w_br = din("w_br", [DEPTH, 1536, D])
    w_out = din("w_out", [DEPTH, D, D])
    cst = din("cst", [128, NCONST])

    yp = dout("yp", [NPS, SEQ, D])
    ys = dout("ys", [NSS, DECT, D])
    conv_p = dout("conv_p", [DEPTH, NPS, 2, 512])
    conv_s = dout("conv_s", [DEPTH, NSS, 2, 512])
    hg_p = dout("hg_p", [DEPTH, NPS, 4, 128, 128])
    hg_s = dout("hg_s", [DEPTH, NSS, 4, 128, 128])
    k_p = dout("k_p", [DEPTH, NPS, SEQ, 512])
    k_s = dout("k_s", [DEPTH, NSS, DECT, 512])
    v_p = dout("v_p", [DEPTH, NPS, SEQ, 512])
    v_s = dout("v_s", [DEPTH, NSS, DECT, 512])

    wsc = nc.dram_tensor("wsc", [DEPTH, NUNITS, 128, USZ], BF16, kind="Internal").ap()

    with ExitStack() as es:
        S = Sched(nc, es)
        op = S.op

        def act(out, in_, func, reads, writes, **kw):
            return op('act', lambda e: e.activation(out=out, in_=in_, func=func, **kw), reads, writes)

        def tt(eng, out, in0, in1, alu, reads, writes):
            return op(eng, lambda e: e.tensor_tensor(out=out, in0=in0, in1=in1, op=alu), reads, writes)

        def ts(eng, out, in0, s1, alu, reads, writes, s2=None, alu1=None):
            if alu1 is None:
                return op(eng, lambda e: e.tensor_scalar(out=out, in0=in0, scalar1=s1, scalar2=None, op0=alu),
                          reads, writes)
            return op(eng, lambda e: e.tensor_scalar(out=out, in0=in0, scalar1=s1, scalar2=s2, op0=alu, op1=alu1),
                      reads, writes)

        def stt(out, in0, scalar, in1, op0, op1, reads, writes):
            return op('dve', lambda e: e.scalar_tensor_tensor(out=out, in0=in0, scalar=scalar, in1=in1,
                                                              op0=op0, op1=op1), reads, writes)

        def cp(eng, out, in_, reads, writes):
            if eng == 'act':
                return act(out, in_, AF.Copy, reads, writes)
            return op(eng, lambda e: e.tensor_copy(out=out, in_=in_), reads, writes)

        def mm(out, lhsT, rhs, reads, writes, start=True, stop=True, inc=True, sgc=False):
            return op('pe', lambda e: e.matmul(out, lhsT=lhsT, rhs=rhs, start=start, stop=stop,
                                               skip_group_check=sgc), reads, writes, inc=inc)

        def tr(out, in_, ident, reads, writes, inc=True):
            return op('pe', lambda e: e.transpose(out, in_, ident), reads, writes, inc=inc)

        consts = S.sbuf("consts", [128, NCP], F32)
        cb = S.sbuf("cb", [128, NCONSTB], BF16)
        xres = S.sbuf("xres", [128, 4, D], F32)
        xb = [Buf(xres.t, f"xres_b{b}") for b in range(4)]
        hT = S.sbuf("hT", [128, 8, TT], BF16)
        ring = [S.sbuf(f"ring{i}", [128, USZ], BF16) for i in range(4)]
        ring_sem = [S.dsem(f"rs{i}") for i in range(4)]
        KT = [S.sbuf(f"KT{l}", [128, 4, 2048], BF16) for l in range(DEPTH)]
        VC = [S.sbuf(f"VC{l}", [128, 16, 512], BF16) for l in range(DEPTH)]
        siluz = S.sbuf("siluz", [128, 12, TT], BF16)
        oA = S.sbuf("oA", [128, 4, TT], BF16)
        oB = S.sbuf("oB", [128, 4, TT], BF16)
        oC = S.sbuf("oC", [128, 4, TT], BF16)
        Sst = [S.sbuf(f"Sst{l}", [128, 4, 128], F32) for l in range(DEPTH)]
        utail = [S.sbuf(f"utail{l}", [128, 4, 2], F32) for l in range(DEPTH)]
        ng = S.sbuf("ng", [128, DEPTH, 8], F32)
        cw = S.sbuf("cw", [128, DEPTH, 3, 4], F32)
        lg = S.sbuf("lg", [128, DEPTH, 4], F32)
        LB = S.sbuf("LB", [128, DEPTH, 4], F32)
        OML = S.sbuf("OML", [128, DEPTH, 4], F32)
        NOML = S.sbuf("NOML", [128, DEPTH, 4], F32)
        hgG = S.sbuf("hgG", [128, DEPTH], F32)
        qg = S.sbuf("qg", [128, DEPTH], F32)
        kg = S.sbuf("kg", [128, DEPTH], F32)
        ssq = S.sbuf("ssq", [128, 4], F32)
        rstd4 = S.sbuf("rstd4", [128, 4], F32)
        lnt = S.sbuf("lnt", [128, 4], F32)
        dl = S.sbuf("dl", [128, 8], F32)
        emid = S.sbuf("emid", [128, 8], F32)
        fb = [S.sbuf(f"fb{i}", [128, 512], F32) for i in range(8)]
        hb = [S.sbuf(f"hb{i}", [128, 512], BF16) for i in range(8)]
        hbx = [S.sbuf(f"hbx{i}", [128, 512], BF16) for i in range(6)]
        hbs = [Buf(fb[6 + i // 2].t[:].bitcast(BF16)[:, (i % 2) * 512:(i % 2 + 1) * 512], f"hbs{i}") for i in range(4)]
        ccx = S.sbuf("ccx", [128, 8], F32)
        zeroB = S.sbuf("zeroB", [128, 128], BF16)
        QTZ = [S.sbuf(f"qtz{i}", [128, 512], BF16) for i in range(2)]
        bmv = S.sbuf("bmv", [128, 8], F32)
        big1 = S.sbuf("big1", [128, 4, 512], F32)
        big2 = S.sbuf("big2", [128, 4, 512], F32)
        banks = [S.psum(f"bank{i}", [128, 512], F32) for i in range(8)]
        bank_i = [0]

        def nb():
            b = banks[bank_i[0] % 8]
            bank_i[0] += 1
            return b

        def bf_view(buf):
            a = buf.t[:]
            if len(a.shape) == 3:
                a = a.rearrange("p a b -> p (a b)")
            return a.bitcast(BF16)

        ld = S.dsem("ld")
        xld = S.dsem("xld")
        st_sems = [S.dsem(f"st{i}") for i in range(8)]
        st_i = [0]

        def store(out, in_, reads, **kw):
            sem = st_sems[st_i[0] % len(st_sems)]
            st_i[0] += 1
            return S.dma(_STQ, out, in_, reads=reads, sem=sem, **kw)

        def sload(out, in_, wbuf, **kw):
            ev = S.dma('sp', out, in_, writes=[wbuf], sem=ld, allow_slow_non_contiguous=True, **kw)
            return ev

        S.dma('sp', consts[:], cst[:, 0:NCP], writes=[consts], sem=ld)
        S.dma('sp', fb[0][:, 0:384], cst[:, C_NEGTRI:C_NEGTRI + 384], writes=[fb[0]], sem=ld)
        cp('dve', cb[:, B_IDENT:B_IDENT + 128], consts[:, C_IDENT:C_IDENT + 128], [consts], [cb])
        cp('dve', cb[:, B_NEGTRI:B_NEGTRI + 384], fb[0][:, 0:384], [fb[0]], [cb])
        cp('dve', cb[:, B_CAUS:B_CAUS + 64], consts[:, C_CAUS:C_CAUS + 64], [consts], [cb])
        for qz in QTZ:
            op('pool', lambda e: e.memset(qz[:, :], 0.0), [], [qz])
        op('pool', lambda e: e.memset(zeroB[:, :], 0.0), [], [zeroB])
        sload(ng[:], norm_g.rearrange("l (kc p) -> p l kc", p=128), ng)
        for l in range(DEPTH):
            sload(cw[:, l, :, :], conv_w[l].rearrange("j (c p) -> p j c", p=128), cw)
        sload(lg[:], hg_lb.rearrange("l (h p) -> p l h", p=128), lg)
        sload(hgG[:], hg_ng.rearrange("l p -> p l"), hgG)
        for half in range(2):
            sload(qg[half * 64:(half + 1) * 64, :], q_ng.rearrange("l d -> d l"), qg)
            sload(kg[half * 64:(half + 1) * 64, :], k_ng.rearrange("l d -> d l"), kg)
        ts('dve', qg[:], qg[:], 0.125, ALU.mult, [qg], [qg])
        op('dve', lambda e: e.memset(LB[:], 0.0), [], [LB])
        tt('dve', lnt[:, 0:4], lg[:, 1, :], lg[:, 0, :], ALU.subtract, [lg], [lnt])
        act(LB[:, 1, :], lnt[:, 0:4], AF.Sigmoid, [lnt], [LB])
        ts('dve', OML[:], LB[:], -1.0, ALU.mult, [LB], [OML], s2=1.0, alu1=ALU.add)
        ts('dve', NOML[:], LB[:], 1.0, ALU.mult, [LB], [NOML], s2=-1.0, alu1=ALU.add)

        NSTG = 4
        stgF = [(tuple(xb), xres.t[:].rearrange("p a b -> p (a b)")),
                ((VC[0],), VC[0].t[:].rearrange("p a b -> p (a b)").bitcast(F32)),
                ((VC[1],), VC[1].t[:].rearrange("p a b -> p (a b)").bitcast(F32)),
                ((KT[0],), KT[0].t[:].rearrange("p a b -> p (a b)").bitcast(F32))]
        stgB = [(ring[i], ring[i].t[:]) for i in range(4)]
        stg_sem = [S.dsem(f"stgs{i}") for i in range(NSTG)]
        stb_sem = [S.dsem(f"stbs{i}") for i in range(NSTG)]
        cast_engs = ['dve', 'act']
        ji = 0
        ei = 0
        for l in range(DEPTH):
            w_in_r = w_in[l].rearrange("(kc p) c -> p kc c", p=128)
            w_br_r = w_br[l].rearrange("(kc p) c -> p kc c", p=128)
            w_out_r = w_out[l].rearrange("(kc p) c -> p kc c", p=128)
            jobs = []
            for u in range(NUNITS):
                if u < 12:
                    g = SEG_ORDER[u]
                    jobs.append((u, 0, 4096, [(0, w_in_r[:, :, g * 512:(g + 1) * 512], 8, 512, True)]))
                elif u < 20:
                    dc = u - 12
                    gcol = [6144 + g * 1024 + dc * 128 for g in range(3)]
                    jobs.append((u, 0, 2560, [(0, w_br_r[:, :, dc * 128:(dc + 1) * 128], 12, 128, False),
                                              (1536, w_in_r[:, :, gcol[0]:gcol[0] + 128], 8, 128, True)]))
                    jobs.append((u, 2560, 2048, [(0, w_in_r[:, :, gcol[1]:gcol[1] + 128], 8, 128, True),
                                                 (1024, w_in_r[:, :, gcol[2]:gcol[2] + 128], 8, 128, True)]))
                else:
                    hf = u - 20
                    jobs.append((u, 0, 4096, [(0, w_out_r[:, hf * 4:(hf + 1) * 4, :], 4, 1024, False)]))
            for (u, off, n, pieces) in jobs:
                fbuf, fap = stgF[ji % NSTG]
                bbuf, bap = stgB[ji % NSTG]
                ssem, bsem = stg_sem[ji % NSTG], stb_sem[ji % NSTG]
                ji += 1
                for (so, src, kcs, cc_, scaled) in pieces:
                    S.dma('sp', fap[:, so:so + kcs * cc_].rearrange("p (kc c) -> p kc c", c=cc_), src,
                          writes=list(fbuf), sem=ssem, group=True)
                for (so, src, kcs, cc_, scaled) in pieces:
                    if scaled and cc_ == 512:
                        for kc in range(8):
                            e2 = cast_engs[ei % 2]
                            ei += 1
                            o_ = bap[:, so + kc * 512:so + (kc + 1) * 512]
                            i_ = fap[:, so + kc * 512:so + (kc + 1) * 512]
                            sc_ = ng[:, l, kc:kc + 1]
                            if e2 == 'act':
                                act(o_, i_, AF.Identity, [*fbuf, ng], [bbuf], scale=sc_)
                            else:
                                ts(e2, o_, i_, sc_, ALU.mult, [*fbuf, ng], [bbuf])
                    elif scaled:
                        e2 = 'dve'
                        o3 = bap[:, so:so + 1024].rearrange("p (kc c) -> p kc c", c=128)
                        i3 = fap[:, so:so + 1024].rearrange("p (kc c) -> p kc c", c=128)
                        tt(e2, o3, i3, ng[:, l, :].unsqueeze(2).to_broadcast([128, 8, 128]), ALU.mult,
                           [*fbuf, ng], [bbuf])
                    else:
                        tot = kcs * cc_
                        step = 1024
                        for o0 in range(0, tot, step):
                            e2 = cast_engs[ei % 2]
                            ei += 1
                            w_ = min(step, tot - o0)
                            cp(e2, bap[:, so + o0:so + o0 + w_], fap[:, so + o0:so + o0 + w_], list(fbuf), [bbuf])
                S.dma('pool', wsc[l, u, :, off:off + n], bap[:, 0:n], reads=[bbuf], sem=bsem)
        for bsem in stb_sem:
            S.need('sp', (bsem['sem'], bsem['n']))

        sched_units = []
        ring_state = {'issued': 0, 'cur': -1, 'rel': 0}

        def ring_issue_upto(n):
            while ring_state['issued'] < min(n, len(sched_units)):
                i = ring_state['issued']
                l, u = sched_units[i]
                slot = i % 4
                nu = USZ if 12 <= u < 20 else 4096
                S.dma('sp', ring[slot][:, 0:nu], wsc[l, u, :, 0:nu], writes=[ring[slot]], sem=ring_sem[slot])
                ring_state['issued'] += 1

        def next_unit(expect_u):
            ring_state['cur'] += 1
            i = ring_state['cur']
            assert sched_units[i][1] == expect_u, (sched_units[i], expect_u)
            assert i < ring_state['rel'] + 4
            ring_issue_upto(ring_state['rel'] + 4)
            return ring[i % 4]

        def release(n=1):
            ring_state['rel'] += n
            assert ring_state['rel'] <= ring_state['cur'] + 1
            ring_issue_upto(ring_state['rel'] + 4)

        def phase0_block(b, r):
            identB = cb[:, B_IDENT:B_IDENT + 128]
            xn = fb[0]
            junk = fb[1]
            act(bf_view(junk)[:r, :], xres[:r, b, :], AF.Square, [xb[b]], [junk, ssq], accum_out=ssq[:r, b:b + 1])
            act(lnt[:r, b:b + 1], ssq[:r, b:b + 1], AF.Ln, [ssq], [lnt], scale=1.0 / D, bias=EPS)
            act(rstd4[:r, b:b + 1], lnt[:r, b:b + 1], AF.Exp, [lnt], [rstd4], scale=-0.5)
            ts('dve', bf_view(xn)[:r, :], xres[:r, b, :], rstd4[:r, b:b + 1], ALU.mult, [xb[b], rstd4], [xn])
            pb = nb()
            pv = bf_view(pb).rearrange("p (a b) -> p a b", b=128)
            for kc in range(8):
                tr(pv[:, kc, :r], bf_view(xn)[:r, kc * 128:(kc + 1) * 128], identB[:r, :r],
                   [xn, cb], [pb], inc=(kc == 7))
            cp('act' if b % 2 == 0 else 'dve', hT[:, :, b * 128:b * 128 + r], pv[:, :, :r], [pb], [hT])

        def tile_layer(l, T, kbase, k_dst, v_dst, post_dma, post_blk):
            nblk = (T + 127) // 128
            rows_of = [min(128, T - b * 128) for b in range(nblk)]
            L = min(64, T)
            nch = T // L
            mid = L // 2 - 1
            identB = cb[:, B_IDENT:B_IDENT + 128]
            _stage('p0')

            def proj_fm(wbuf, col0, out_bank):
                for kc in range(8):
                    mm(out_bank[:, :T], wbuf[:, kc * 512 + col0:kc * 512 + col0 + 128], hT[:, kc, :T],
                       [wbuf, hT], [out_bank], start=(kc == 0), stop=(kc == 7), inc=(kc == 7))

            def proj_tm(wbuf, blk, out_bank):
                r = rows_of[blk]
                for kc in range(8):
                    mm(out_bank[:r, :], hT[:, kc, blk * 128:blk * 128 + r], wbuf[:, kc * 512:(kc + 1) * 512],
                       [wbuf, hT], [out_bank], start=(kc == 0), stop=(kc == 7), inc=(kc == 7))

            _stage('pZ')
            for br in range(3):
                w = next_unit(br)
                for c in range(4):
                    pb = nb()
                    proj_fm(w, c * 128, pb)
                    act(siluz[:, br * 4 + c, :T], pb[:, :T], AF.Silu, [pb], [siluz])
                release(1)

            _stage('pC')
            wsq, wsk, wsv = next_unit(3), next_unit(4), next_unit(5)
            QTc = hbx[0:4]
            kf = big1
            stg_tok = big2
            vb0 = kbase // 128
            for b in range(nblk):
                r = rows_of[b]
                pb = nb()
                proj_tm(wsv, b, pb)
                cp('act', stg_tok[:r, b, :], pb[:r, :], [pb], [stg_tok])
                cp('dve', VC[l][:r, vb0 + b, :], pb[:r, :], [pb], [VC[l]])
            if T >= 128:
                store(v_dst.rearrange("(b p) f -> p b f", p=128), stg_tok[:, :, :], [stg_tok])
            else:
                store(v_dst, stg_tok[:T, 0, :], [stg_tok])
            _stage('pC1')
            blk64 = consts[:, C_BLK64:C_BLK64 + 128]
            for which in range(2):
                wbuf = wsq if which == 0 else wsk
                gain = qg if which == 0 else kg
                for c in range(4):
                    pq = nb()
                    proj_fm(wbuf, c * 128, pq)
                    sq, lnv, rs = (fb[0], fb[1], fb[2]) if (which * 4 + c) % 2 == 0 else (fb[3], fb[4], fb[5])
                    act(sq[:, :T], pq[:, :T], AF.Square, [pq], [sq])
                    pss = nb()
                    mm(pss[:, :T], blk64, sq[:, :T], [consts, sq], [pss])
                    act(lnv[:, :T], pss[:, :T], AF.Ln, [pss], [lnv], scale=1.0 / 64, bias=EPS)
                    act(rs[:, :T], lnv[:, :T], AF.Exp, [lnv], [rs], scale=-0.5)
                    if which == 0:
                        stt(QTc[c][:, :T], pq[:, :T], gain[:, l:l + 1], rs[:, :T], ALU.mult, ALU.mult,
                            [pq, gain, rs], [QTc[c]])
                    else:
                        stt(kf[:, c, :T], pq[:, :T], gain[:, l:l + 1], rs[:, :T], ALU.mult, ALU.mult,
                            [pq, gain, rs], [kf])
                        cp('pool', KT[l][:, c, kbase:kbase + T], kf[:, c, :T], [kf], [KT[l]])
            release(3)
            _stage('pC2')
            identF = consts[:, C_IDENT:C_IDENT + 128]
            for b in range(nblk):
                r = rows_of[b]
                pb = nb()
                for c in range(4):
                    tr(pb[:r, c * 128:(c + 1) * 128], kf[:, c, b * 128:b * 128 + r], identF, [kf, consts], [pb],
                       inc=(c == 3))
                cp('act', stg_tok[:r, b, :], pb[:r, :], [pb], [stg_tok])
            if T >= 128:
                store(k_dst.rearrange("(b p) f -> p b f", p=128), stg_tok[:, :, :], [stg_tok])
            else:
                store(k_dst, stg_tok[:T, 0, :], [stg_tok])

            _stage('pF')
            sg = big1
            w = next_unit(6)
            for h in range(4):
                pb = nb()
                proj_fm(w, h * 128, pb)
                act(sg[:, h, :T], pb[:, :T], AF.Sigmoid, [pb], [sg])
            release(1)

            _stage('pA')
            wx, wb_, wc = next_unit(7), next_unit(8), next_unit(9)
            for c in range(4):
                px, pbb, pc = nb(), nb(), nb()
                proj_fm(wx, c * 128, px)
                proj_fm(wc, c * 128, pc)
                proj_fm(wb_, c * 128, pbb)
                xsb, uext, y0, y1 = fb[2], fb[3], fb[4], fb[5]
                cp('act', xsb[:, :T], px[:, :T], [px], [xsb])
                tt('dve', uext[:, :T], pc[:, :T], xsb[:, :T], ALU.mult, [pc, xsb], [uext])
                cw0, cw1, cw2 = (cw[:, l, j, c:c + 1] for j in range(3))
                act(y0[:, :T], uext[:, :T], AF.Identity, [uext, cw], [y0], scale=cw2)
                stt(y0[:, 1:T], uext[:, 0:T - 1], cw1, y0[:, 1:T], ALU.mult, ALU.add, [uext, cw, y0], [y0])
                stt(y0[:, 2:T], uext[:, 0:T - 2], cw0, y0[:, 2:T], ALU.mult, ALU.add, [uext, cw, y0], [y0])
                stt(y0[:, 0:1], utail[l][:, c, 1:2], cw1, y0[:, 0:1], ALU.mult, ALU.add, [utail[l], cw, y0], [y0])
                stt(y0[:, 0:1], utail[l][:, c, 0:1], cw0, y0[:, 0:1], ALU.mult, ALU.add, [utail[l], cw, y0], [y0])
                stt(y0[:, 1:2], utail[l][:, c, 1:2], cw0, y0[:, 1:2], ALU.mult, ALU.add, [utail[l], cw, y0], [y0])
                cp('pool', utail[l][:, c, :], uext[:, T - 2:T], [uext], [utail[l]])
                tt('dve', y1[:, :T], pbb[:, :T], y0[:, :T], ALU.mult, [pbb, y0], [y1])
                tt('pool', oA[:, c, :T], y1[:, :T], siluz[:, c, :T], ALU.mult, [y1, siluz], [oA])
            release(3)

            _stage('pB')
            RB = [None, banks[6], banks[7]]

            def phaseB():
                wq, wi = next_unit(10), next_unit(11)
                b2 = bf_view(big2).rearrange("p (a b) -> p a b", b=512)
                ktok_v = b2[:, 0:4, :]
                v16_v = b2[:, 4:8, :]
                for b in range(nblk):
                    r = rows_of[b]
                    pb = RB[1 + b % 2]
                    proj_tm(wi, b, pb)
                    yield
                    cp('dve', v16_v[:r, b, :], pb[:r, :], [pb], [big2])
                    yield
                F1, F2, F3 = fb[3], fb[4], fb[5]
                kk, eq, ek, qt = hbs[0], hbs[1], hbs[2], hbs[3]
                kt, scm = hbx[4], hbx[5]
                St16 = bf_view(F3).rearrange("p (a b) -> p a b", b=128)
                for h in range(4):
                    pq = RB[2]
                    proj_fm(wq, h * 128, pq)
                    yield
                    logf, bcs = F1, F2
                    act(logf[:, :T], sg[:, h, :T], AF.Ln, [sg, OML, LB], [logf],
                        scale=OML[:, l, h:h + 1], bias=LB[:, l, h:h + 1])
                    ts('dve', kk[:, :T], sg[:, h, :T], NOML[:, l, h:h + 1], ALU.mult, [sg, OML, NOML], [kk],
                       s2=OML[:, l, h:h + 1], alu1=ALU.add)
                    yield
                    op('dve', lambda e: e.tensor_tensor_scan(out=bcs[:, :T], data0=consts[:, C_RESET:C_RESET + T],
                                                             data1=logf[:, :T], initial=0.0,
                                                             op0=ALU.mult, op1=ALU.add), [consts, logf], [bcs])
                    yield
                    b3 = bcs[:, :T].rearrange("p (c l) -> p c l", l=L)
                    act(dl[:, :nch], b3[:, :, L - 1], AF.Exp, [bcs], [dl])
                    act(emid[:, :nch], b3[:, :, mid], AF.Exp, [bcs], [emid])
                    cp('dve', bmv[:, :nch], b3[:, :, mid], [bcs], [bmv])
                    yield
                    tt('dve', b3, b3, bmv[:, :nch].unsqueeze(2).to_broadcast([128, nch, L]), ALU.subtract,
                       [bcs, bmv], [bcs])
                    yield
                    act(eq[:, :T], bcs[:, :T], AF.Exp, [bcs], [eq])
                    act(ek[:, :T], bcs[:, :T], AF.Exp, [bcs], [ek], scale=-1.0)
                    act(ccx[:, :nch], b3[:, :, L - 1], AF.Exp, [bcs], [ccx])
                    yield
                    tt('dve', qt[:, :T], pq[:, :T], eq[:, :T], ALU.mult, [pq, eq], [qt])
                    tt('pool', kt[:, :T], kk[:, :T], ek[:, :T], ALU.mult, [kk, ek], [kt])
                    yield
                    pb = RB[1]
                    pv = bf_view(pb).rearrange("p (a b) -> p a b", b=128)
                    for b in range(nblk):
                        r = rows_of[b]
                        tr(pv[:r, b, :], kt[:, b * 128:b * 128 + r], identB, [kt, cb], [pb], inc=(b == nblk - 1))
                    yield
                    rr = rows_of[0]
                    cp('dve', ktok_v[:rr, 0:nblk, h * 128:(h + 1) * 128], pv[:rr, 0:nblk, :], [pb], [big2])
                    yield
                    psc = RB[1]
                    psc3 = psc[:, 0:256].rearrange("p (a b) -> p a b", b=64)
                    for c in range(nch):
                        blk, par = c // 2, c % 2
                        r0 = par * 64
                        mm(psc3[r0:r0 + L, blk, 0:L], kt[:, c * L:(c + 1) * L], qt[:, c * L:(c + 1) * L],
                           [kt, qt], [psc])
                    yield
                    scm3 = scm[:, 0:256].rearrange("p (a b) -> p a b", b=64)
                    caus = consts[:, C_CAUS:C_CAUS + 64]
                    if T >= 128:
                        tt('dve', scm3[:, 0:nblk, :], psc3[:, 0:nblk, :],
                           caus.unsqueeze(1).to_broadcast([128, nblk, 64]), ALU.mult, [psc, consts], [scm])
                    else:
                        tt('dve', scm3[:L, 0, 0:L], psc3[:L, 0, 0:L], caus[:L, 0:L], ALU.mult, [psc, consts], [scm])
                    yield
                    po = RB[2]
                    pu = RB[1]
                    tmpu = F1
                    for c in range(nch):
                        blk, par = c // 2, c % 2
                        r0 = par * 64
                        Sv = Sst[l][:, h, :]
                        pus = pu[:, (c % 4) * 128:(c % 4 + 1) * 128]
                        mm(pus, ktok_v[r0:r0 + L, blk, h * 128:(h + 1) * 128],
                           v16_v[r0:r0 + L, blk, h * 128:(h + 1) * 128], [big2], [pu])
                        ts('dve', St16[:, c, :], Sv, emid[:, c:c + 1], ALU.mult, [Sst[l], emid], [F3])
                        yield
                        ts('dve', tmpu[:, 0:128], pus, ccx[:, c:c + 1], ALU.mult, [pu, ccx], [tmpu])
                        stt(Sv, Sv, dl[:, c:c + 1], tmpu[:, 0:128], ALU.mult, ALU.add, [Sst[l], dl, tmpu], [Sst[l]])
                        mm(po[:, c * L:(c + 1) * L], v16_v[r0:r0 + L, blk, h * 128:(h + 1) * 128],
                           scm3[r0:r0 + L, blk, 0:L], [big2, scm], [po], start=True, stop=False, inc=False)
                        mm(po[:, c * L:(c + 1) * L], St16[:, c, :], qt[:, c * L:(c + 1) * L],
                           [F3, qt], [po], start=False, stop=True)
                        yield
                    sq, lnv, rs, t1 = F2, F1, F2, F1
                    act(sq[:, :T], po[:, :T], AF.Square, [po], [sq])
                    yield
                    pss = RB[1]
                    mm(pss[:, :T], consts[:, C_ONES:C_ONES + 128], sq[:, :T], [consts, sq], [pss])
                    yield
                    act(lnv[:, :T], pss[:, :T], AF.Ln, [pss], [lnv], scale=1.0 / 128, bias=EPS)
                    yield
                    act(rs[:, :T], lnv[:, :T], AF.Exp, [lnv], [rs], scale=-0.5)
                    yield
                    stt(t1[:, :T], po[:, :T], hgG[:, l:l + 1], rs[:, :T], ALU.mult, ALU.mult, [po, hgG, rs], [t1])
                    yield
                    tt('pool', oB[:, h, :T], t1[:, :T], siluz[:, 4 + h, :T], ALU.mult, [t1, siluz], [oB])
                    yield
                release(2)

            _stage('pC3')
            units = []
            for h in range(8):
                blks = []
                if T >= 128:
                    for j in range(nblk - 1, -1, -1):
                        blks.append((vb0 + j, 128, 128 * j, T - 128 * j, True))
                else:
                    blks.append((vb0, T, 0, T, True))
                for kb in range(vb0 - 1, -1, -1):
                    blks.append((kb, 128, 0, T, False))
                for i, (kb, nk, c0, N, diag) in enumerate(blks):
                    units.append(dict(h=h, kb=kb, nk=nk, c0=c0, N=N, diag=diag, first=(i == 0),
                                      last=(i == len(blks) - 1)))
            for i, u in enumerate(units):
                u['next'] = units[i + 1] if (i + 1 < len(units) and not u['last']) else None
            zb = [banks[0], banks[1]]
            cbk = [banks[2], banks[3]]
            ob = [banks[4], banks[4]]
            ez = [fb[0], fb[1]]
            spt = [hb[0], hb[1], hb[2]]
            at = [hb[3], hb[4]]
            cyb = banks[5]
            sps16 = [hb[5], hb[6], hb[7]]
            negtri = cb[:, B_NEGTRI:B_NEGTRI + 128]
            negones = cb[:, B_NEGONES:B_NEGONES + 128]
            mask01 = cb[:, B_MASK01:B_MASK01 + 128]
            nU = len(units)

            def opnds(u):
                pair, hh = u['h'] // 2, u['h'] % 2
                R0 = hh * 64
                kT = KT[l][:, pair, u['kb'] * 128:u['kb'] * 128 + u['nk']]
                qT = QTZ[hh][:, u['c0']:u['c0'] + u['N']]
                return kT, qT, QTZ[hh]

            def P1(i):
                u = units[i]
                if u['first']:
                    pair, hh = u['h'] // 2, u['h'] % 2
                    R = slice(hh * 64, hh * 64 + 64)
                    cp('dve', QTZ[hh][R, :T], QTc[pair][R, :T], [QTc[pair]], [QTZ[hh]])
                kT, qT, qb = opnds(u)
                mm(zb[i % 2][:u['nk'], :u['N']], kT, qT, [KT[l], qb], [zb[i % 2]])

            def E1(i):
                u = units[i]
                nk, N = u['nk'], u['N']
                act(ez[i % 2][:nk, :N], zb[i % 2][:nk, :N], AF.Exp, [zb[i % 2]], [ez[i % 2]])

            def LN(i):
                u = units[i]
                nk, N = u['nk'], u['N']
                s_ = spt[i % 3]
                act(s_[:nk, :N], ez[i % 2][:nk, :N], AF.Ln, [ez[i % 2]], [s_], bias=1.0)
                if u['diag']:
                    w_ = min(128, N)
                    tt('pool', s_[:nk, 0:w_], s_[:nk, 0:w_], mask01[:nk, 0:w_], ALU.mult, [s_, cb], [s_])

            def P2(i):
                u = units[i]
                nk, N, c0 = u['nk'], u['N'], u['c0']
                kT, qT, qb = opnds(u)
                s_ = spt[i % 3]
                bk = cbk[i % 2]
                mm(bk[:nk, :N], kT, qT, [KT[l], qb], [bk], start=True, stop=False, inc=False)
                if u['first']:
                    mm(bk[:nk, :N], negtri[:nk, :nk], s_[:nk, :N], [cb, s_], [bk], start=False, stop=True)
                else:
                    mm(bk[:nk, :N], negtri[:nk, :nk], s_[:nk, :N], [cb, s_], [bk], start=False, stop=False, inc=False)
                    d_ = sps16[i % 3]
                    mm(bk[:nk, :N], negones[:, :nk], d_[:, c0:c0 + N], [cb, d_], [bk],
                       start=False, stop=True)
                if u['first']:
                    mm(cyb[:, 0:T], zeroB[:, :], cb[:, 0:T], [zeroB, cb], [cyb], start=True, stop=False, sgc=True)
                if u['next'] is not None:
                    mm(cyb[:nk, c0:c0 + N], identB[:nk, :nk], s_[:nk, :N], [cb, s_], [cyb], start=False, stop=False,
                       sgc=True)
                    un = u['next']
                    dn = sps16[(i + 1) % 3]
                    cp('dve', dn[:, un['c0']:un['c0'] + un['N']], cyb[:, un['c0']:un['c0'] + un['N']], [cyb], [dn])

            def A2(i):
                u = units[i]
                nk, N = u['nk'], u['N']
                a_ = at[i % 2]
                aview = a_[:nk, :N]
                act(aview, cbk[i % 2][:nk, :N], AF.Exp, [cbk[i % 2]], [a_])
                if u['diag']:
                    w_ = min(128, N)
                    av2 = a_[:nk, 0:w_]
                    tt('pool', av2, av2, mask01[:nk, 0:w_], ALU.mult, [a_, cb], [a_])

            def P3(i):
                u = units[i]
                nk, N, c0 = u['nk'], u['N'], u['c0']
                pair = u['h'] // 2
                a_ = at[i % 2]
                aview = a_[:nk, :N]
                o_ = ob[u['h'] % 2]
                mm(o_[:, c0:c0 + N], VC[l][:nk, u['kb'], pair * 128:(pair + 1) * 128], aview,
                   [VC[l], a_], [o_], start=u['first'], stop=u['last'], sgc=True)
                if u['last']:
                    hh = u['h'] % 2
                    R = slice(hh * 64, hh * 64 + 64)
                    tt('dve', oC[R, pair, :T], o_[R, :T], siluz[R, 8 + pair, :T], ALU.mult, [o_, siluz], [oC])

            lag = (0, 1, 2, 3, 4, 5) if T >= 128 else (0, 0, 0, 1, 1, 2)
            stages = (P1, E1, LN, P2, A2, P3)
            genB = phaseB()
            nB_est = 2 * nblk + 4 * (16 + 2 * nch)
            n_it = nU + lag[-1]
            done_b = 0
            for k in range(n_it):
                for fn, lg in zip(stages, lag):
                    if 0 <= k - lg < nU:
                        fn(k - lg)
                want = ((k + 1) * nB_est + n_it - 1) // n_it
                while done_b < want:
                    if next(genB, 'end') == 'end':
                        done_b = 1 << 30
                        break
                    done_b += 1
            for _ in genB:
                pass

            _stage('pM')
            m16v = big1.t[:].rearrange("p a b -> p (a b)").bitcast(BF16).rearrange("p (a b) -> p a b", b=512)
            for dc in range(8):
                w = next_unit(12 + dc)
                ya, yb_, yc = nb(), nb(), nb()
                ga, gb, gc = nb(), nb(), nb()
                for (yb2, osrc, k0) in ((ya, oA, 0), (yb_, oB, 4), (yc, oC, 8)):
                    for kc in range(4):
                        mm(yb2[:, :T], w[:, (k0 + kc) * 128:(k0 + kc + 1) * 128], osrc[:, kc, :T],
                           [w, osrc], [yb2], start=(kc == 0), stop=(kc == 3), inc=(kc == 3))
                for gi, gbk in enumerate((ga, gb, gc)):
                    for kc in range(8):
                        o0 = 1536 + gi * 1024 + kc * 128
                        mm(gbk[:, :T], w[:, o0:o0 + 128], hT[:, kc, :T], [w, hT], [gbk],
                           start=(kc == 0), stop=(kc == 7), inc=(kc == 7))
                sa, sb2, sc2 = hb[0], hb[1], hb[2]
                t1, t2, t3 = fb[0], fb[1], fb[2]
                act(sa[:, :T], ga[:, :T], AF.Sigmoid, [ga], [sa])
                act(sb2[:, :T], gb[:, :T], AF.Sigmoid, [gb], [sb2])
                act(sc2[:, :T], gc[:, :T], AF.Sigmoid, [gc], [sc2])
                tt('dve', t1[:, :T], ya[:, :T], sa[:, :T], ALU.mult, [ya, sa], [t1])
                tt('dve', t2[:, :T], yb_[:, :T], sb2[:, :T], ALU.mult, [yb_, sb2], [t2])
                tt('dve', t3[:, :T], yc[:, :T], sc2[:, :T], ALU.mult, [yc, sc2], [t3])
                tt('pool', t1[:, :T], t1[:, :T], t2[:, :T], ALU.add, [t1, t2], [t1])
                tt('pool', m16v[:, dc, :T], t1[:, :T], t3[:, :T], ALU.add, [t1, t3], [big1])
                release(1)
            wo = [next_unit(20), next_unit(21)]
            LAG = nblk if l == DEPTH - 1 else min(2, nblk)
            for b in range(nblk + LAG):
                if b < nblk:
                    r = rows_of[b]
                    for half in range(2):
                        pb = nb()
                        for dc in range(8):
                            wbuf = wo[dc // 4]
                            o0 = (dc % 4) * 1024 + half * 512
                            mm(pb[:r, :], m16v[:, dc, b * 128:b * 128 + r], wbuf[:, o0:o0 + 512],
                               [big1, wbuf], [pb], start=(dc == 0), stop=(dc == 7), inc=(dc == 7))
                        tt('dve', xres[:r, b, half * 512:(half + 1) * 512], pb[:r, :],
                           xres[:r, b, half * 512:(half + 1) * 512], ALU.add, [pb, xb[b]], [xb[b]])
                    if b == nblk - 1:
                        release(2)
                    post_dma(b, r)
                if 0 <= b - LAG < nblk:
                    post_blk(b - LAG, rows_of[b - LAG])

        for _ in range(NPS * NT + NSS):
            for l in range(DEPTH):
                for u in range(NUNITS):
                    sched_units.append((l, u))

        def seq_finish(conv_dst, hg_dst):
            for l in range(DEPTH):
                for j in range(2):
                    store(conv_dst[l][j].rearrange("(c p) -> p c", p=128), utail[l][:, :, j], [utail[l]],
                          allow_slow_non_contiguous=True)
                store(hg_dst[l].rearrange("h k v -> k h v"), Sst[l][:, :, :], [Sst[l]])

        def prompt_init():
            for l in range(DEPTH):
                op('pool', lambda e: e.memset(utail[l][:], 0.0), [], [utail[l]])
                op('pool', lambda e: e.memset(Sst[l][:], 0.0), [], [Sst[l]])

        def sample_init(s):
            for l in range(DEPTH):
                for j in range(2):
                    sload(utail[l][:, :, j], cconv[l, s, j].rearrange("(c p) -> p c", p=128), utail[l])
                S.dma('sp', Sst[l][:, :, :], shg[l, s].rearrange("h k v -> k h v"), writes=[Sst[l]], sem=ld)
                b2 = bf_view(big2).rearrange("p (a b) -> p a b", b=512)
                identB = cb[:, B_IDENT:B_IDENT + 128]
                for half in range(2):
                    S.dma('sp', big1[:, :, :], ck[l, s, half * 512:(half + 1) * 512, :].rearrange(
                        "(b p) f -> p b f", p=128), writes=[big1], sem=ld)
                    cp('dve', b2[:, 0:4, :], big1[:, :, :], [big1], [big2])
                    for blk in range(4):
                        pb = nb()
                        pv = bf_view(pb).rearrange("p (a b) -> p a b", b=128)
                        for c in range(4):
                            tr(pv[:, c, :], b2[:, blk, c * 128:(c + 1) * 128], identB, [big2, cb], [pb], inc=(c == 3))
                        kcol = (half * 4 + blk) * 128
                        cp('act', KT[l][:, :, kcol:kcol + 128], pv[:, 0:4, :], [pb], [KT[l]])
                    S.dma('sp', big1[:, :, :], cv[l, s, half * 512:(half + 1) * 512, :].rearrange(
                        "(b p) f -> p b f", p=128), writes=[big1], sem=ld)
                    cp('pool', VC[l][:, half * 4:half * 4 + 4, :], big1[:, :, :], [big1], [VC[l]])

        items = []
        for s in range(NPS):
            for ti in range(NT):
                t0 = ti * TT
                items.append(dict(
                    T=TT, kbase=t0,
                    xsrc=(lambda b, s=s, t0=t0: xp[s, t0 + b * 128:t0 + (b + 1) * 128, :]),
                    ydst=(lambda b, s=s, t0=t0: yp[s, t0 + b * 128:t0 + (b + 1) * 128, :]),
                    k_dst=[k_p[l, s, t0:t0 + TT, :] for l in range(DEPTH)],
                    v_dst=[v_p[l, s, t0:t0 + TT, :] for l in range(DEPTH)],
                    pre=(prompt_init if ti == 0 else None),
                    post=((lambda s=s: seq_finish([conv_p[l, s] for l in range(DEPTH)],
                                                  [hg_p[l, s] for l in range(DEPTH)])) if ti == NT - 1 else None)))
        for s in range(NSS):
            items.append(dict(
                T=DECT, kbase=PAST,
                xsrc=(lambda b, s=s: xs[s]), ydst=(lambda b, s=s: ys[s]),
                k_dst=[k_s[l, s] for l in range(DEPTH)], v_dst=[v_s[l, s] for l in range(DEPTH)],
                pre=(lambda s=s: sample_init(s)),
                post=(lambda s=s: seq_finish([conv_s[l, s] for l in range(DEPTH)],
                                             [hg_s[l, s] for l in range(DEPTH)]))))
        xld4 = [S.dsem(f"xld{b}") for b in range(4)]

        def it_rows(it, b):
            return min(128, it['T'] - b * 128)

        def it_nblk(it):
            return (it['T'] + 127) // 128

        def load_and_phase0(it, b):
            r = it_rows(it, b)
            S.dma('sp', xres[:r, b, :], it['xsrc'](b), writes=[xb[b]], sem=xld4[b])
            phase0_block(b, r)

        try:
            _stage('main')
            for b in range(it_nblk(items[0])):
                load_and_phase0(items[0], b)
            for n, it in enumerate(items):
                nxt = items[n + 1] if n + 1 < len(items) else None
                if it['pre'] is not None:
                    it['pre']()
                for l in range(DEPTH):
                    if l < DEPTH - 1:
                        post_dma = (lambda b, r: None)
                        post_blk = phase0_block
                    else:
                        def post_dma(b, r, it=it, nxt=nxt):
                            store(it['ydst'](b), xres[:r, b, :], [xb[b]])
                            if nxt is not None and b < it_nblk(nxt):
                                S.dma('sp', xres[:it_rows(nxt, b), b, :], nxt['xsrc'](b), writes=[xb[b]], sem=xld4[b])

                        def post_blk(b, r, it=it, nxt=nxt):
                            if nxt is not None and b < it_nblk(nxt):
                                phase0_block(b, it_rows(nxt, b))
                    tile_layer(l, it['T'], it['kbase'], it['k_dst'][l], it['v_dst'][l], post_dma, post_blk)
                if nxt is not None:
                    for b in range(it_nblk(it), it_nblk(nxt)):
                        load_and_phase0(nxt, b)
                if it['post'] is not None:
                    it['post']()

        except _StopBuild:
            pass

        for sem in st_sems:
            if sem['n']:
                S.need(_STQ, (sem['sem'], sem['n']))
        build.stats = dict(nops=S.nops, nsem=S.nsem)
    return nc


_CACHE = {}


def _get_nc(cfg_key):
    if cfg_key not in _CACHE:
        _CACHE[cfg_key] = build(Cfg(*cfg_key))
    return _CACHE[cfg_key]


def run(inputs, n_cores, nps, seq, nss):
    f = lambda a: np.ascontiguousarray(np.asarray(a, dtype=np.float32))
    x_prompt, x_sample = f(inputs['x_prompt']), f(inputs['x_sample'])
    cache_conv, state_hgrn = f(inputs['cache_conv']), f(inputs['state_hgrn'])
    cache_k, cache_v = f(inputs['cache_k']), f(inputs['cache_v'])
    consts = make_consts()
    shared = {
        "norm_g": f(inputs['norm_g']), "w_in": f(inputs['w_in']), "conv_w": f(inputs['conv_w']),
        "hg_lb": f(inputs['hg_lb_logits']), "hg_ng": f(inputs['hg_norm_g']), "q_ng": f(inputs['q_norm_g']),
        "k_ng": f(inputs['k_norm_g']), "w_br": f(inputs['w_branch']), "w_out": f(inputs['w_out']), "cst": consts,
    }
    in_maps = []
    for c in range(n_cores):
        ps, ss = slice(c * nps, (c + 1) * nps), slice(c * nss, (c + 1) * nss)
        m = dict(shared)
        m["xp"] = np.ascontiguousarray(x_prompt[ps])
        m["xs"] = np.ascontiguousarray(x_sample[ss])
        m["cconv"] = np.ascontiguousarray(cache_conv[:, ss])
        m["shg"] = np.ascontiguousarray(state_hgrn[:, ss])
        m["ck"] = np.ascontiguousarray(cache_k[:, ss].reshape(DEPTH, nss, PAST, 512))
        m["cv"] = np.ascontiguousarray(cache_v[:, ss].reshape(DEPTH, nss, PAST, 512))
        in_maps.append(m)
    nc = _get_nc((nps, seq, nss))
    res = run_bass_kernel_spmd(nc, in_maps, core_ids=list(range(n_cores)))
    R = res.results
    cat0 = lambda k: np.concatenate([r[k] for r in R], axis=0)
    cat1 = lambda k: np.concatenate([r[k] for r in R], axis=1)
    B, Bs = n_cores * nps, n_cores * nss
    return (
        cat0("yp"), cat0("ys"), cat1("conv_p"), cat1("conv_s"), cat1("hg_p"), cat1("hg_s"),
        cat1("k_p").reshape(DEPTH, B, seq, 8, 64), cat1("k_s").reshape(DEPTH, Bs, DECT, 8, 64),
        cat1("v_p").reshape(DEPTH, B, seq, 8, 64), cat1("v_s").reshape(DEPTH, Bs, DECT, 8, 64),
    )


def kernel(**inputs):
    return run(inputs, 8, 4, 2048, 2)
```

```python
import numpy as np
from contextlib import ExitStack
import concourse.bass as bass
import concourse.mybir as mybir
from concourse.bass_utils import run_bass_kernel_spmd

F32 = mybir.dt.float32
BF16 = mybir.dt.bfloat16
AF = mybir.ActivationFunctionType
ALU = mybir.AluOpType

EPS = 1e-6
D = 1024
DEPTH = 2
D_IN = 9216
USZ = 4608
NUNITS = 22
SEG_ORDER = [3, 7, 11, 8, 9, 10, 5, 0, 1, 2, 4, 6]
PAST = 1024
DECT = 16
TT = 512

C_IDENT, C_ONES, C_BLK64, C_CAUS, C_RESET, NCP = 0, 128, 256, 384, 448, 960
C_NEGTRI, C_NEGONES, C_MASK01, NCONST = 960, 1088, 1216, 1344
B_IDENT, B_NEGTRI, B_NEGONES, B_MASK01, B_CAUS, B_ONES, B_BLK64, NCONSTB = 0, 128, 256, 384, 512, 576, 704, 832


def make_consts():
    c = np.zeros((128, NCONST), np.float32)
    i = np.arange(128)
    c[:, C_IDENT:C_IDENT + 128] = np.eye(128)
    c[:, C_NEGTRI:C_NEGTRI + 128] = -(i[:, None] >= i[None, :]).astype(np.float32)
    c[:, C_NEGONES:C_NEGONES + 128] = -1.0
    c[:, C_MASK01:C_MASK01 + 128] = (i[:, None] < i[None, :]).astype(np.float32)
    c[:, C_ONES:C_ONES + 128] = 1.0
    c[:, C_BLK64:C_BLK64 + 128] = ((i[:, None] // 64) == (i[None, :] // 64)).astype(np.float32)
    t = np.arange(64)
    c[:, C_CAUS:C_CAUS + 64] = ((i[:, None] % 64) <= t[None, :]).astype(np.float32)
    tt = np.arange(512)
    c[:, C_RESET:C_RESET + 512] = (tt % 64 != 0).astype(np.float32)[None, :]
    return c


class Buf:
    def __init__(self, t, name, psum=False):
        self.t = t
        self.name = name
        self.psum = psum
        self.last_write = None
        self.readers = []

    def __getitem__(self, idx):
        return self.t[idx]


class Sched:
    def __init__(self, nc, es):
        self.nc = nc
        self.es = es
        self.eng = {'pe': nc.tensor, 'act': nc.scalar, 'dve': nc.vector, 'pool': nc.gpsimd, 'sp': nc.sync}
        self.cur = {}
        self.cnt = {}
        self.waited = {e: {} for e in self.eng}
        self.pending = {e: [] for e in self.eng}
        self.nsem = 0
        for e in self.eng:
            self.cur[e] = self._sem(f"s_{e}")
            self.cnt[e] = 0
        self.nops = 0

    def _sem(self, name):
        self.nsem += 1
        return self.es.enter_context(self.nc.semaphore(name))

    def sbuf(self, name, shape, dt, es=None):
        return Buf((es or self.es).enter_context(self.nc.sbuf_tensor(name, shape, dt)), name)

    def psum(self, name, shape, dt):
        return Buf(self.es.enter_context(self.nc.psum_tensor(name, shape, dt)), name, psum=True)

    def need(self, e, ev):
        if ev is None:
            return
        sem, val = ev
        w = self.waited[e]
        if w.get(id(sem), 0) < val:
            self.eng[e].wait_ge(sem, val)
            w[id(sem)] = val

    def _deps(self, e, reads, writes):
        for b in reads:
            self.need(e, b.last_write)
            if b.psum:
                for ev in b.readers:
                    self.need(e, ev)
        for b in writes:
            self.need(e, b.last_write)
            for ev in b.readers:
                self.need(e, ev)

    def _commit(self, ev, reads, writes):
        for b in reads:
            b.readers = [r for r in b.readers if r[0] is not ev[0]]
            b.readers.append(ev)
        for b in writes:
            b.last_write = ev
            b.readers = []

    def op(self, e, fn, reads=(), writes=(), inc=True):
        self._deps(e, reads, writes)
        ins = fn(self.eng[e])
        self.nops += 1
        self.pending[e].append((reads, writes))
        if inc:
            self.cnt[e] += 1
            ins.then_inc(self.cur[e], 1)
            ev = (self.cur[e], self.cnt[e])
            for (r, w) in self.pending[e]:
                self._commit(ev, r, w)
            self.pending[e] = []
        return ins

    def dsem(self, name):
        return {'sem': self._sem(name), 'n': 0}

    def dma(self, e, out, in_, reads=(), writes=(), sem=None, group=False, **kw):
        self._deps(e, reads, writes)
        if not group and sem['n']:
            self.need(e, (sem['sem'], sem['n']))
        ins = self.eng[e].dma_start(out=out, in_=in_, **kw)
        self.nops += 1
        sem['n'] += 16
        ins.then_inc(sem['sem'], 16)
        ev = (sem['sem'], sem['n'])
        self._commit(ev, reads, writes)
        return ev


import os as _os
_DBG_STOP = _os.environ.get("KDBG_STOP", "")
_STQ = _os.environ.get("KDBG_STQ", "pool")


class _StopBuild(Exception):
    pass


_STAGE_CNT = {}


def _stage(name):
    if not _DBG_STOP:
        return
    _STAGE_CNT[name] = _STAGE_CNT.get(name, 0) + 1
    if name == _DBG_STOP or f"{name}#{_STAGE_CNT[name]}" == _DBG_STOP:
        raise _StopBuild(name)


class Cfg:
    def __init__(self, nps=4, seq=2048, nss=2):
        self.nps, self.seq, self.nss = nps, seq, nss


def build(cfg):
    nc = bass.Bass("TRN2", target_bir_lowering=False)
    NPS, SEQ, NSS = cfg.nps, cfg.seq, cfg.nss
    NT = SEQ // TT

    def din(name, shape):
        return nc.dram_tensor(name, list(shape), F32, kind="ExternalInput").ap()

    def dout(name, shape):
        return nc.dram_tensor(name, list(shape), F32, kind="ExternalOutput").ap()

    xp = din("xp", [NPS, SEQ, D])
    xs = din("xs", [NSS, DECT, D])
    cconv = din("cconv", [DEPTH, NSS, 2, 512])
    shg = din("shg", [DEPTH, NSS, 4, 128, 128])
    ck = din("ck", [DEPTH, NSS, PAST, 512])
    cv = din("cv", [DEPTH, NSS, PAST, 512])
    norm_g = din("norm_g", [DEPTH, D])
    w_in = din("w_in", [DEPTH, D, D_IN])
    conv_w = din("conv_w", [DEPTH, 3, 512])
    hg_lb = din("hg_lb", [DEPTH, 512])
    hg_ng = din("hg_ng", [DEPTH, 128])
    q_ng = din("q_ng", [DEPTH, 64])
    k_ng = din("k_ng", [DEPTH, 64])
    w_br = din("w_br", [DEPTH, 1536, D])
    w_out = din("w_out", [DEPTH, D, D])
    cst = din("cst", [128, NCONST])

    yp = dout("yp", [NPS, SEQ, D])
    ys = dout("ys", [NSS, DECT, D])
    conv_p = dout("conv_p", [DEPTH, NPS, 2, 512])
    conv_s = dout("conv_s", [DEPTH, NSS, 2, 512])
    hg_p = dout("hg_p", [DEPTH, NPS, 4, 128, 128])
    hg_s = dout("hg_s", [DEPTH, NSS, 4, 128, 128])
    k_p = dout("k_p", [DEPTH, NPS, SEQ, 512])
    k_s = dout("k_s", [DEPTH, NSS, DECT, 512])
    v_p = dout("v_p", [DEPTH, NPS, SEQ, 512])
    v_s = dout("v_s", [DEPTH, NSS, DECT, 512])

    wsc = nc.dram_tensor("wsc", [DEPTH, NUNITS, 128, USZ], BF16, kind="Internal").ap()

    with ExitStack() as es:
        S = Sched(nc, es)
        op = S.op

        def act(out, in_, func, reads, writes, **kw):
            return op('act', lambda e: e.activation(out=out, in_=in_, func=func, **kw), reads, writes)

        def tt(eng, out, in0, in1, alu, reads, writes):
            return op(eng, lambda e: e.tensor_tensor(out=out, in0=in0, in1=in1, op=alu), reads, writes)

        def ts(eng, out, in0, s1, alu, reads, writes, s2=None, alu1=None):
            if alu1 is None:
                return op(eng, lambda e: e.tensor_scalar(out=out, in0=in0, scalar1=s1, scalar2=None, op0=alu),
                          reads, writes)
            return op(eng, lambda e: e.tensor_scalar(out=out, in0=in0, scalar1=s1, scalar2=s2, op0=alu, op1=alu1),
                      reads, writes)

        def stt(out, in0, scalar, in1, op0, op1, reads, writes):
            return op('dve', lambda e: e.scalar_tensor_tensor(out=out, in0=in0, scalar=scalar, in1=in1,
                                                              op0=op0, op1=op1), reads, writes)

        def cp(eng, out, in_, reads, writes):
            if eng == 'act':
                return act(out, in_, AF.Copy, reads, writes)
            return op(eng, lambda e: e.tensor_copy(out=out, in_=in_), reads, writes)

        def mm(out, lhsT, rhs, reads, writes, start=True, stop=True, inc=True, sgc=False):
            return op('pe', lambda e: e.matmul(out, lhsT=lhsT, rhs=rhs, start=start, stop=stop,
                                               skip_group_check=sgc), reads, writes, inc=inc)

        def tr(out, in_, ident, reads, writes, inc=True):
            return op('pe', lambda e: e.transpose(out, in_, ident), reads, writes, inc=inc)

        consts = S.sbuf("consts", [128, NCP], F32)
        cb = S.sbuf("cb", [128, NCONSTB], BF16)
        xres = S.sbuf("xres", [128, 4, D], F32)
        xb = [Buf(xres.t, f"xres_b{b}") for b in range(4)]
        hT = S.sbuf("hT", [128, 8, TT], BF16)
        ring = [S.sbuf(f"ring{i}", [128, USZ], BF16) for i in range(4)]
        ring_sem = [S.dsem(f"rs{i}") for i in range(4)]
        KT = [S.sbuf(f"KT{l}", [128, 4, 2048], BF16) for l in range(DEPTH)]
        VC = [S.sbuf(f"VC{l}", [128, 16, 512], BF16) for l in range(DEPTH)]
        siluz = S.sbuf("siluz", [128, 12, TT], BF16)
        oA = S.sbuf("oA", [128, 4, TT], BF16)
        oB = S.sbuf("oB", [128, 4, TT], BF16)
        oC = S.sbuf("oC", [128, 4, TT], BF16)
        Sst = [S.sbuf(f"Sst{l}", [128, 4, 128], F32) for l in range(DEPTH)]
        utail = [S.sbuf(f"utail{l}", [128, 4, 2], F32) for l in range(DEPTH)]
        ng = S.sbuf("ng", [128, DEPTH, 8], F32)
        cw = S.sbuf("cw", [128, DEPTH, 3, 4], F32)
        lg = S.sbuf("lg", [128, DEPTH, 4], F32)
        LB = S.sbuf("LB", [128, DEPTH, 4], F32)
        OML = S.sbuf("OML", [128, DEPTH, 4], F32)
        NOML = S.sbuf("NOML", [128, DEPTH, 4], F32)
        hgG = S.sbuf("hgG", [128, DEPTH], F32)
        qg = S.sbuf("qg", [128, DEPTH], F32)
        kg = S.sbuf("kg", [128, DEPTH], F32)
        ssq = S.sbuf("ssq", [128, 4], F32)
        rstd4 = S.sbuf("rstd4", [128, 4], F32)
        lnt = S.sbuf("lnt", [128, 4], F32)
        dl = S.sbuf("dl", [128, 8], F32)
        emid = S.sbuf("emid", [128, 8], F32)
        fb = [S.sbuf(f"fb{i}", [128, 512], F32) for i in range(8)]
        hb = [S.sbuf(f"hb{i}", [128, 512], BF16) for i in range(8)]
        hbx = [S.sbuf(f"hbx{i}", [128, 512], BF16) for i in range(6)]
        hbs = [Buf(fb[6 + i // 2].t[:].bitcast(BF16)[:, (i % 2) * 512:(i % 2 + 1) * 512], f"hbs{i}") for i in range(4)]
        ccx = S.sbuf("ccx", [128, 8], F32)
        zeroB = S.sbuf("zeroB", [128, 128], BF16)
        QTZ = [S.sbuf(f"qtz{i}", [128, 512], BF16) for i in range(2)]
        bmv = S.sbuf("bmv", [128, 8], F32)
        big1 = S.sbuf("big1", [128, 4, 512], F32)
        big2 = S.sbuf("big2", [128, 4, 512], F32)
        banks = [S.psum(f"bank{i}", [128, 512], F32) for i in range(8)]
        bank_i = [0]

        def nb():
            b = banks[bank_i[0] % 8]
            bank_i[0] += 1
            return b

        def bf_view(buf):
            a = buf.t[:]
            if len(a.shape) == 3:
                a = a.rearrange("p a b -> p (a b)")
            return a.bitcast(BF16)

        ld = S.dsem("ld")
        xld = S.dsem("xld")
        st_sems = [S.dsem(f"st{i}") for i in range(8)]
        st_i = [0]

        def store(out, in_, reads, **kw):
            sem = st_sems[st_i[0] % len(st_sems)]
            st_i[0] += 1
            return S.dma(_STQ, out, in_, reads=reads, sem=sem, **kw)

        def sload(out, in_, wbuf, **kw):
            ev = S.dma('sp', out, in_, writes=[wbuf], sem=ld, allow_slow_non_contiguous=True, **kw)
            return ev

        S.dma('sp', consts[:], cst[:, 0:NCP], writes=[consts], sem=ld)
        S.dma('sp', fb[0][:, 0:384], cst[:, C_NEGTRI:C_NEGTRI + 384], writes=[fb[0]], sem=ld)
        cp('dve', cb[:, B_IDENT:B_IDENT + 128], consts[:, C_IDENT:C_IDENT + 128], [consts], [cb])
        cp('dve', cb[:, B_NEGTRI:B_NEGTRI + 384], fb[0][:, 0:384], [fb[0]], [cb])
        cp('dve', cb[:, B_CAUS:B_CAUS + 64], consts[:, C_CAUS:C_CAUS + 64], [consts], [cb])
        cp('dve', cb[:, B_ONES:B_ONES + 256], consts[:, C_ONES:C_ONES + 256], [consts], [cb])
        for qz in QTZ:
            op('pool', lambda e: e.memset(qz[:, :], 0.0), [], [qz])
        op('pool', lambda e: e.memset(zeroB[:, :], 0.0), [], [zeroB])
        sload(ng[:], norm_g.rearrange("l (kc p) -> p l kc", p=128), ng)
        for l in range(DEPTH):
            sload(cw[:, l, :, :], conv_w[l].rearrange("j (c p) -> p j c", p=128), cw)
        sload(lg[:], hg_lb.rearrange("l (h p) -> p l h", p=128), lg)
        sload(hgG[:], hg_ng.rearrange("l p -> p l"), hgG)
        for half in range(2):
            sload(qg[half * 64:(half + 1) * 64, :], q_ng.rearrange("l d -> d l"), qg)
            sload(kg[half * 64:(half + 1) * 64, :], k_ng.rearrange("l d -> d l"), kg)
        ts('dve', qg[:], qg[:], 0.125, ALU.mult, [qg], [qg])
        op('dve', lambda e: e.memset(LB[:], 0.0), [], [LB])
        tt('dve', lnt[:, 0:4], lg[:, 1, :], lg[:, 0, :], ALU.subtract, [lg], [lnt])
        act(LB[:, 1, :], lnt[:, 0:4], AF.Sigmoid, [lnt], [LB])
        ts('dve', OML[:], LB[:], -1.0, ALU.mult, [LB], [OML], s2=1.0, alu1=ALU.add)
        ts('dve', NOML[:], LB[:], 1.0, ALU.mult, [LB], [NOML], s2=-1.0, alu1=ALU.add)

        NSTG = 4
        stgF = [(tuple(xb), xres.t[:].rearrange("p a b -> p (a b)")),
                ((VC[0],), VC[0].t[:].rearrange("p a b -> p (a b)").bitcast(F32)),
                ((VC[1],), VC[1].t[:].rearrange("p a b -> p (a b)").bitcast(F32)),
                ((KT[0],), KT[0].t[:].rearrange("p a b -> p (a b)").bitcast(F32))]
        stgB = [(ring[i], ring[i].t[:]) for i in range(4)]
        stg_sem = [S.dsem(f"stgs{i}") for i in range(NSTG)]
        stb_sem = [S.dsem(f"stbs{i}") for i in range(NSTG)]
        cast_engs = ['dve', 'act']
        ji = 0
        ei = 0
        for l in range(DEPTH):
            w_in_r = w_in[l].rearrange("(kc p) c -> p kc c", p=128)
            w_br_r = w_br[l].rearrange("(kc p) c -> p kc c", p=128)
            w_out_r = w_out[l].rearrange("(kc p) c -> p kc c", p=128)
            jobs = []
            for u in range(NUNITS):
                if u < 12:
                    g = SEG_ORDER[u]
                    jobs.append((u, 0, 4096, [(0, w_in_r[:, :, g * 512:(g + 1) * 512], 8, 512, True)]))
                elif u < 20:
                    dc = u - 12
                    gcol = [6144 + g * 1024 + dc * 128 for g in range(3)]
                    jobs.append((u, 0, 2560, [(0, w_br_r[:, :, dc * 128:(dc + 1) * 128], 12, 128, False),
                                              (1536, w_in_r[:, :, gcol[0]:gcol[0] + 128], 8, 128, True)]))
                    jobs.append((u, 2560, 2048, [(0, w_in_r[:, :, gcol[1]:gcol[1] + 128], 8, 128, True),
                                                 (1024, w_in_r[:, :, gcol[2]:gcol[2] + 128], 8, 128, True)]))
                else:
                    hf = u - 20
                    jobs.append((u, 0, 4096, [(0, w_out_r[:, hf * 4:(hf + 1) * 4, :], 4, 1024, False)]))
            for (u, off, n, pieces) in jobs:
                fbuf, fap = stgF[ji % NSTG]
                bbuf, bap = stgB[ji % NSTG]
                ssem, bsem = stg_sem[ji % NSTG], stb_sem[ji % NSTG]
                ji += 1
                for (so, src, kcs, cc_, scaled) in pieces:
                    S.dma('sp', fap[:, so:so + kcs * cc_].rearrange("p (kc c) -> p kc c", c=cc_), src,
                          writes=list(fbuf), sem=ssem, group=True)
                for (so, src, kcs, cc_, scaled) in pieces:
                    if scaled and cc_ == 512:
                        for kc in range(8):
                            e2 = cast_engs[ei % 2]
                            ei += 1
                            o_ = bap[:, so + kc * 512:so + (kc + 1) * 512]
                            i_ = fap[:, so + kc * 512:so + (kc + 1) * 512]
                            sc_ = ng[:, l, kc:kc + 1]
                            if e2 == 'act':
                                act(o_, i_, AF.Identity, [*fbuf, ng], [bbuf], scale=sc_)
                            else:
                                ts(e2, o_, i_, sc_, ALU.mult, [*fbuf, ng], [bbuf])
                    elif scaled:
                        e2 = 'dve'
                        o3 = bap[:, so:so + 1024].rearrange("p (kc c) -> p kc c", c=128)
                        i3 = fap[:, so:so + 1024].rearrange("p (kc c) -> p kc c", c=128)
                        tt(e2, o3, i3, ng[:, l, :].unsqueeze(2).to_broadcast([128, 8, 128]), ALU.mult,
                           [*fbuf, ng], [bbuf])
                    else:
                        tot = kcs * cc_
                        step = 1024
                        for o0 in range(0, tot, step):
                            e2 = cast_engs[ei % 2]
                            ei += 1
                            w_ = min(step, tot - o0)
                            cp(e2, bap[:, so + o0:so + o0 + w_], fap[:, so + o0:so + o0 + w_], list(fbuf), [bbuf])
                S.dma('pool', wsc[l, u, :, off:off + n], bap[:, 0:n], reads=[bbuf], sem=bsem)
        for bsem in stb_sem:
            S.need('sp', (bsem['sem'], bsem['n']))

        sched_units = []
        ring_state = {'issued': 0, 'cur': -1, 'rel': 0}

        def ring_issue_upto(n):
            while ring_state['issued'] < min(n, len(sched_units)):
                i = ring_state['issued']
                l, u = sched_units[i]
                slot = i % 4
                nu = USZ if 12 <= u < 20 else 4096
                S.dma('sp', ring[slot][:, 0:nu], wsc[l, u, :, 0:nu], writes=[ring[slot]], sem=ring_sem[slot])
                ring_state['issued'] += 1

        def next_unit(expect_u):
            ring_state['cur'] += 1
            i = ring_state['cur']
            assert sched_units[i][1] == expect_u, (sched_units[i], expect_u)
            assert i < ring_state['rel'] + 4
            ring_issue_upto(ring_state['rel'] + 4)
            return ring[i % 4]

        def release(n=1):
            ring_state['rel'] += n
            assert ring_state['rel'] <= ring_state['cur'] + 1
            ring_issue_upto(ring_state['rel'] + 4)

        def phase0_block(b, r):
            identB = cb[:, B_IDENT:B_IDENT + 128]
            xn = fb[0]
            junk = fb[1]
            act(bf_view(junk)[:r, :], xres[:r, b, :], AF.Square, [xb[b]], [junk, ssq], accum_out=ssq[:r, b:b + 1])
            act(lnt[:r, b:b + 1], ssq[:r, b:b + 1], AF.Ln, [ssq], [lnt], scale=1.0 / D, bias=EPS)
            act(rstd4[:r, b:b + 1], lnt[:r, b:b + 1], AF.Exp, [lnt], [rstd4], scale=-0.5)
            ts('dve', bf_view(xn)[:r, :], xres[:r, b, :], rstd4[:r, b:b + 1], ALU.mult, [xb[b], rstd4], [xn])
            pb = nb()
            pv = bf_view(pb).rearrange("p (a b) -> p a b", b=128)
            for kc in range(8):
                tr(pv[:, kc, :r], bf_view(xn)[:r, kc * 128:(kc + 1) * 128], identB[:r, :r],
                   [xn, cb], [pb], inc=(kc == 7))
            cp('act' if b % 2 == 0 else 'dve', hT[:, :, b * 128:b * 128 + r], pv[:, :, :r], [pb], [hT])

        def tile_layer(l, T, kbase, k_dst, v_dst, post_dma, post_blk):
            nblk = (T + 127) // 128
            rows_of = [min(128, T - b * 128) for b in range(nblk)]
            L = min(64, T)
            nch = T // L
            mid = L // 2 - 1
            identB = cb[:, B_IDENT:B_IDENT + 128]
            _stage('p0')

            def proj_fm(wbuf, col0, out_bank):
                for kc in range(8):
                    mm(out_bank[:, :T], wbuf[:, kc * 512 + col0:kc * 512 + col0 + 128], hT[:, kc, :T],
                       [wbuf, hT], [out_bank], start=(kc == 0), stop=(kc == 7), inc=(kc == 7))

            def proj_tm(wbuf, blk, out_bank):
                r = rows_of[blk]
                for kc in range(8):
                    mm(out_bank[:r, :], hT[:, kc, blk * 128:blk * 128 + r], wbuf[:, kc * 512:(kc + 1) * 512],
                       [wbuf, hT], [out_bank], start=(kc == 0), stop=(kc == 7), inc=(kc == 7))

            _stage('pZ')
            for br in range(3):
                w = next_unit(br)
                for c in range(4):
                    pb = nb()
                    proj_fm(w, c * 128, pb)
                    act(siluz[:, br * 4 + c, :T], pb[:, :T], AF.Silu, [pb], [siluz])
                release(1)

            _stage('pC')
            wsq, wsk, wsv = next_unit(3), next_unit(4), next_unit(5)
            QTc = hbx[0:4]
            kf = big1
            stg_tok = big2
            vb0 = kbase // 128
            for b in range(nblk):
                r = rows_of[b]
                pb = nb()
                proj_tm(wsv, b, pb)
                cp('act', stg_tok[:r, b, :], pb[:r, :], [pb], [stg_tok])
                cp('dve', VC[l][:r, vb0 + b, :], pb[:r, :], [pb], [VC[l]])
            if T >= 128:
                store(v_dst.rearrange("(b p) f -> p b f", p=128), stg_tok[:, :, :], [stg_tok])
            else:
                store(v_dst, stg_tok[:T, 0, :], [stg_tok])
            _stage('pC1')
            blk64 = consts[:, C_BLK64:C_BLK64 + 128]
            for which in range(2):
                wbuf = wsq if which == 0 else wsk
                gain = qg if which == 0 else kg
                for c in range(4):
                    pq = nb()
                    proj_fm(wbuf, c * 128, pq)
                    sq, lnv, rs = (fb[0], fb[1], fb[2]) if (which * 4 + c) % 2 == 0 else (fb[3], fb[4], fb[5])
                    act(bf_view(sq)[:, :T], pq[:, :T], AF.Square, [pq], [sq])
                    pss = nb()
                    mm(pss[:, :T], cb[:, B_BLK64:B_BLK64 + 128], bf_view(sq)[:, :T], [cb, sq], [pss])
                    act(lnv[:, :T], pss[:, :T], AF.Ln, [pss], [lnv], scale=1.0 / 64, bias=EPS)
                    act(rs[:, :T], lnv[:, :T], AF.Exp, [lnv], [rs], scale=-0.5)
                    if which == 0:
                        stt(QTc[c][:, :T], pq[:, :T], gain[:, l:l + 1], rs[:, :T], ALU.mult, ALU.mult,
                            [pq, gain, rs], [QTc[c]])
                    else:
                        stt(kf[:, c, :T], pq[:, :T], gain[:, l:l + 1], rs[:, :T], ALU.mult, ALU.mult,
                            [pq, gain, rs], [kf])
                        cp('pool', KT[l][:, c, kbase:kbase + T], kf[:, c, :T], [kf], [KT[l]])
            release(3)
            _stage('pC2')
            identF = consts[:, C_IDENT:C_IDENT + 128]
            for b in range(nblk):
                r = rows_of[b]
                pb = nb()
                for c in range(4):
                    tr(pb[:r, c * 128:(c + 1) * 128], kf[:, c, b * 128:b * 128 + r], identF, [kf, consts], [pb],
                       inc=(c == 3))
                cp('act', stg_tok[:r, b, :], pb[:r, :], [pb], [stg_tok])
            if T >= 128:
                store(k_dst.rearrange("(b p) f -> p b f", p=128), stg_tok[:, :, :], [stg_tok])
            else:
                store(k_dst, stg_tok[:T, 0, :], [stg_tok])

            _stage('pF')
            sg = big1
            w = next_unit(6)
            for h in range(4):
                pb = nb()
                proj_fm(w, h * 128, pb)
                act(sg[:, h, :T], pb[:, :T], AF.Sigmoid, [pb], [sg])
            release(1)

            _stage('pA')
            wx, wb_, wc = next_unit(7), next_unit(8), next_unit(9)
            for c in range(4):
                px, pbb, pc = nb(), nb(), nb()
                proj_fm(wx, c * 128, px)
                proj_fm(wc, c * 128, pc)
                proj_fm(wb_, c * 128, pbb)
                xsb, uext = (fb[2], fb[3]) if c % 2 == 0 else (fb[0], fb[1])
                y0, y1 = fb[4], fb[5]
                cp('act', xsb[:, :T], px[:, :T], [px], [xsb])
                tt('dve', uext[:, :T], pc[:, :T], xsb[:, :T], ALU.mult, [pc, xsb], [uext])
                cw0, cw1, cw2 = (cw[:, l, j, c:c + 1] for j in range(3))
                act(y0[:, :T], uext[:, :T], AF.Identity, [uext, cw], [y0], scale=cw2)
                stt(y0[:, 1:T], uext[:, 0:T - 1], cw1, y0[:, 1:T], ALU.mult, ALU.add, [uext, cw, y0], [y0])
                stt(y0[:, 2:T], uext[:, 0:T - 2], cw0, y0[:, 2:T], ALU.mult, ALU.add, [uext, cw, y0], [y0])
                stt(y0[:, 0:1], utail[l][:, c, 1:2], cw1, y0[:, 0:1], ALU.mult, ALU.add, [utail[l], cw, y0], [y0])
                stt(y0[:, 0:1], utail[l][:, c, 0:1], cw0, y0[:, 0:1], ALU.mult, ALU.add, [utail[l], cw, y0], [y0])
                stt(y0[:, 1:2], utail[l][:, c, 1:2], cw0, y0[:, 1:2], ALU.mult, ALU.add, [utail[l], cw, y0], [y0])
                cp('pool', utail[l][:, c, :], uext[:, T - 2:T], [uext], [utail[l]])
                tt('dve', y1[:, :T], pbb[:, :T], y0[:, :T], ALU.mult, [pbb, y0], [y1])
                tt('pool', oA[:, c, :T], y1[:, :T], siluz[:, c, :T], ALU.mult, [y1, siluz], [oA])
            release(3)

            _stage('pB')
            RB = [None, banks[6], banks[7]]

            def phaseB():
                wq, wi = next_unit(10), next_unit(11)
                b2 = bf_view(big2).rearrange("p (a b) -> p a b", b=512)
                ktok_v = b2[:, 0:4, :]
                v16_v = b2[:, 4:8, :]
                for b in range(nblk):
                    r = rows_of[b]
                    pb = RB[1 + b % 2]
                    proj_tm(wi, b, pb)
                    yield
                    cp('dve', v16_v[:r, b, :], pb[:r, :], [pb], [big2])
                    yield
                F1, F2, F3 = fb[3], fb[4], fb[5]
                kk, eq, ek, qt = hbs[0], hbs[1], hbs[2], hbs[3]
                kt, scm = hbx[4], hbx[5]
                St16 = bf_view(F3).rearrange("p (a b) -> p a b", b=128)
                for h in range(4):
                    pq = RB[2]
                    proj_fm(wq, h * 128, pq)
                    yield
                    logf, bcs = F1, F2
                    act(logf[:, :T], sg[:, h, :T], AF.Ln, [sg, OML, LB], [logf],
                        scale=OML[:, l, h:h + 1], bias=LB[:, l, h:h + 1])
                    ts('dve', kk[:, :T], sg[:, h, :T], NOML[:, l, h:h + 1], ALU.mult, [sg, OML, NOML], [kk],
                       s2=OML[:, l, h:h + 1], alu1=ALU.add)
                    yield
                    op('dve', lambda e: e.tensor_tensor_scan(out=bcs[:, :T], data0=consts[:, C_RESET:C_RESET + T],
                                                             data1=logf[:, :T], initial=0.0,
                                                             op0=ALU.mult, op1=ALU.add), [consts, logf], [bcs])
                    yield
                    b3 = bcs[:, :T].rearrange("p (c l) -> p c l", l=L)
                    act(dl[:, :nch], b3[:, :, L - 1], AF.Exp, [bcs], [dl])
                    act(emid[:, :nch], b3[:, :, mid], AF.Exp, [bcs], [emid])
                    cp('dve', bmv[:, :nch], b3[:, :, mid], [bcs], [bmv])
                    yield
                    tt('dve', b3, b3, bmv[:, :nch].unsqueeze(2).to_broadcast([128, nch, L]), ALU.subtract,
                       [bcs, bmv], [bcs])
                    yield
                    act(eq[:, :T], bcs[:, :T], AF.Exp, [bcs], [eq])
                    act(ek[:, :T], bcs[:, :T], AF.Exp, [bcs], [ek], scale=-1.0)
                    act(ccx[:, :nch], b3[:, :, L - 1], AF.Exp, [bcs], [ccx])
                    yield
                    tt('dve', qt[:, :T], pq[:, :T], eq[:, :T], ALU.mult, [pq, eq], [qt])
                    tt('pool', kt[:, :T], kk[:, :T], ek[:, :T], ALU.mult, [kk, ek], [kt])
                    yield
                    pb = RB[1]
                    pv = bf_view(pb).rearrange("p (a b) -> p a b", b=128)
                    for b in range(nblk):
                        r = rows_of[b]
                        tr(pv[:r, b, :], kt[:, b * 128:b * 128 + r], identB, [kt, cb], [pb], inc=(b == nblk - 1))
                    yield
                    rr = rows_of[0]
                    cp('dve', ktok_v[:rr, 0:nblk, h * 128:(h + 1) * 128], pv[:rr, 0:nblk, :], [pb], [big2])
                    yield
                    psc = RB[1]
                    psc3 = psc[:, 0:256].rearrange("p (a b) -> p a b", b=64)
                    for c in range(nch):
                        blk, par = c // 2, c % 2
                        r0 = par * 64
                        mm(psc3[r0:r0 + L, blk, 0:L], kt[:, c * L:(c + 1) * L], qt[:, c * L:(c + 1) * L],
                           [kt, qt], [psc])
                    yield
                    scm3 = scm[:, 0:256].rearrange("p (a b) -> p a b", b=64)
                    caus = consts[:, C_CAUS:C_CAUS + 64]
                    if T >= 128:
                        tt('dve', scm3[:, 0:nblk, :], psc3[:, 0:nblk, :],
                           caus.unsqueeze(1).to_broadcast([128, nblk, 64]), ALU.mult, [psc, consts], [scm])
                    else:
                        tt('dve', scm3[:L, 0, 0:L], psc3[:L, 0, 0:L], caus[:L, 0:L], ALU.mult, [psc, consts], [scm])
                    yield
                    po = RB[2]
                    pu = RB[1]
                    tmpu = F1
                    for c in range(nch):
                        blk, par = c // 2, c % 2
                        r0 = par * 64
                        Sv = Sst[l][:, h, :]
                        pus = pu[:, (c % 4) * 128:(c % 4 + 1) * 128]
                        mm(pus, ktok_v[r0:r0 + L, blk, h * 128:(h + 1) * 128],
                           v16_v[r0:r0 + L, blk, h * 128:(h + 1) * 128], [big2], [pu])
                        ts('dve', St16[:, c, :], Sv, emid[:, c:c + 1], ALU.mult, [Sst[l], emid], [F3])
                        yield
                        ts('dve', tmpu[:, 0:128], pus, ccx[:, c:c + 1], ALU.mult, [pu, ccx], [tmpu])
                        stt(Sv, Sv, dl[:, c:c + 1], tmpu[:, 0:128], ALU.mult, ALU.add, [Sst[l], dl, tmpu], [Sst[l]])
                        mm(po[:, c * L:(c + 1) * L], v16_v[r0:r0 + L, blk, h * 128:(h + 1) * 128],
                           scm3[r0:r0 + L, blk, 0:L], [big2, scm], [po], start=True, stop=False, inc=False)
                        mm(po[:, c * L:(c + 1) * L], St16[:, c, :], qt[:, c * L:(c + 1) * L],
                           [F3, qt], [po], start=False, stop=True)
                        yield
                    sq, lnv, rs, t1 = F2, F1, F2, F1
                    act(bf_view(sq)[:, :T], po[:, :T], AF.Square, [po], [sq])
                    yield
                    pss = RB[1]
                    mm(pss[:, :T], cb[:, B_ONES:B_ONES + 128], bf_view(sq)[:, :T], [cb, sq], [pss])
                    yield
                    act(lnv[:, :T], pss[:, :T], AF.Ln, [pss], [lnv], scale=1.0 / 128, bias=EPS)
                    yield
                    act(rs[:, :T], lnv[:, :T], AF.Exp, [lnv], [rs], scale=-0.5)
                    yield
                    stt(t1[:, :T], po[:, :T], hgG[:, l:l + 1], rs[:, :T], ALU.mult, ALU.mult, [po, hgG, rs], [t1])
                    yield
                    tt('pool', oB[:, h, :T], t1[:, :T], siluz[:, 4 + h, :T], ALU.mult, [t1, siluz], [oB])
                    yield
                release(2)

            _stage('pC3')
            units = []
            for h in range(8):
                blks = []
                if T >= 128:
                    for j in range(nblk - 1, -1, -1):
                        blks.append((vb0 + j, 128, 128 * j, T - 128 * j, True))
                else:
                    blks.append((vb0, T, 0, T, True))
                for kb in range(vb0 - 1, -1, -1):
                    blks.append((kb, 128, 0, T, False))
                for i, (kb, nk, c0, N, diag) in enumerate(blks):
                    units.append(dict(h=h, kb=kb, nk=nk, c0=c0, N=N, diag=diag, first=(i == 0),
                                      last=(i == len(blks) - 1)))
            for i, u in enumerate(units):
                u['next'] = units[i + 1] if (i + 1 < len(units) and not u['last']) else None
            zb = [banks[0], banks[1]]
            cbk = [banks[2], banks[3]]
            ob = [banks[4], banks[4]]
            ez = [fb[0], fb[1]]
            spt = [hb[0], hb[1], hb[2]]
            at = [hb[3], hb[4]]
            cyb = banks[5]
            sps16 = [hb[5], hb[6], hb[7]]
            negtri = cb[:, B_NEGTRI:B_NEGTRI + 128]
            negones = cb[:, B_NEGONES:B_NEGONES + 128]
            mask01 = cb[:, B_MASK01:B_MASK01 + 128]
            nU = len(units)

            def opnds(u):
                pair, hh = u['h'] // 2, u['h'] % 2
                R0 = hh * 64
                kT = KT[l][:, pair, u['kb'] * 128:u['kb'] * 128 + u['nk']]
                qT = QTZ[hh][:, u['c0']:u['c0'] + u['N']]
                return kT, qT, QTZ[hh]

            def P1(i):
                u = units[i]
                if u['first']:
                    pair, hh = u['h'] // 2, u['h'] % 2
                    R = slice(hh * 64, hh * 64 + 64)
                    cp('dve', QTZ[hh][R, :T], QTc[pair][R, :T], [QTc[pair]], [QTZ[hh]])
                kT, qT, qb = opnds(u)
                mm(zb[i % 2][:u['nk'], :u['N']], kT, qT, [KT[l], qb], [zb[i % 2]])

            def E1(i):
                u = units[i]
                nk, N = u['nk'], u['N']
                act(ez[i % 2][:nk, :N], zb[i % 2][:nk, :N], AF.Exp, [zb[i % 2]], [ez[i % 2]])

            def LN(i):
                u = units[i]
                nk, N = u['nk'], u['N']
                s_ = spt[i % 3]
                act(s_[:nk, :N], ez[i % 2][:nk, :N], AF.Ln, [ez[i % 2]], [s_], bias=1.0)
                if u['diag']:
                    w_ = min(128, N)
                    tt('pool', s_[:nk, 0:w_], s_[:nk, 0:w_], mask01[:nk, 0:w_], ALU.mult, [s_, cb], [s_])

            def P2(i):
                u = units[i]
                nk, N, c0 = u['nk'], u['N'], u['c0']
                kT, qT, qb = opnds(u)
                s_ = spt[i % 3]
                bk = cbk[i % 2]
                mm(bk[:nk, :N], kT, qT, [KT[l], qb], [bk], start=True, stop=False, inc=False)
                if u['first']:
                    mm(bk[:nk, :N], negtri[:nk, :nk], s_[:nk, :N], [cb, s_], [bk], start=False, stop=True)
                else:
                    mm(bk[:nk, :N], negtri[:nk, :nk], s_[:nk, :N], [cb, s_], [bk], start=False, stop=False, inc=False)
                    d_ = sps16[i % 3]
                    mm(bk[:nk, :N], negones[:, :nk], d_[:, c0:c0 + N], [cb, d_], [bk],
                       start=False, stop=True)
                if u['first']:
                    mm(cyb[:, 0:T], zeroB[:, :], cb[:, 0:T], [zeroB, cb], [cyb], start=True, stop=False, sgc=True)
                if u['next'] is not None:
                    mm(cyb[:nk, c0:c0 + N], identB[:nk, :nk], s_[:nk, :N], [cb, s_], [cyb], start=False, stop=False,
                       sgc=True)
                    un = u['next']
                    dn = sps16[(i + 1) % 3]
                    cp('dve', dn[:, un['c0']:un['c0'] + un['N']], cyb[:, un['c0']:un['c0'] + un['N']], [cyb], [dn])

            def A2(i):
                u = units[i]
                nk, N = u['nk'], u['N']
                a_ = at[i % 2]
                aview = a_[:nk, :N]
                act(aview, cbk[i % 2][:nk, :N], AF.Exp, [cbk[i % 2]], [a_])
                if u['diag']:
                    w_ = min(128, N)
                    av2 = a_[:nk, 0:w_]
                    tt('pool', av2, av2, mask01[:nk, 0:w_], ALU.mult, [a_, cb], [a_])

            def P3(i):
                u = units[i]
                nk, N, c0 = u['nk'], u['N'], u['c0']
                pair = u['h'] // 2
                a_ = at[i % 2]
                aview = a_[:nk, :N]
                o_ = ob[u['h'] % 2]
                mm(o_[:, c0:c0 + N], VC[l][:nk, u['kb'], pair * 128:(pair + 1) * 128], aview,
                   [VC[l], a_], [o_], start=u['first'], stop=u['last'], sgc=True)
                if u['last']:
                    hh = u['h'] % 2
                    R = slice(hh * 64, hh * 64 + 64)
                    tt('dve', oC[R, pair, :T], o_[R, :T], siluz[R, 8 + pair, :T], ALU.mult, [o_, siluz], [oC])

            lag = (0, 1, 2, 3, 4, 5) if T >= 128 else (0, 0, 0, 1, 1, 2)
            stages = (P1, E1, LN, P2, A2, P3)
            genB = phaseB()
            nB_est = 2 * nblk + 4 * (16 + 2 * nch)
            n_it = nU + lag[-1]
            done_b = 0
            for k in range(n_it):
                for fn, lg in zip(stages, lag):
                    if 0 <= k - lg < nU:
                        fn(k - lg)
                want = ((k + 1) * nB_est + n_it - 1) // n_it
                while done_b < want:
                    if next(genB, 'end') == 'end':
                        done_b = 1 << 30
                        break
                    done_b += 1
            for _ in genB:
                pass

            _stage('pM')
            m16v = big1.t[:].rearrange("p a b -> p (a b)").bitcast(BF16).rearrange("p (a b) -> p a b", b=512)
            for dc in range(8):
                w = next_unit(12 + dc)
                ya, yb_, yc = nb(), nb(), nb()
                ga, gb, gc = nb(), nb(), nb()
                for (yb2, osrc, k0) in ((ya, oA, 0), (yb_, oB, 4), (yc, oC, 8)):
                    for kc in range(4):
                        mm(yb2[:, :T], w[:, (k0 + kc) * 128:(k0 + kc + 1) * 128], osrc[:, kc, :T],
                           [w, osrc], [yb2], start=(kc == 0), stop=(kc == 3), inc=(kc == 3))
                for gi, gbk in enumerate((ga, gb, gc)):
                    for kc in range(8):
                        o0 = 1536 + gi * 1024 + kc * 128
                        mm(gbk[:, :T], w[:, o0:o0 + 128], hT[:, kc, :T], [w, hT], [gbk],
                           start=(kc == 0), stop=(kc == 7), inc=(kc == 7))
                sa, sb2, sc2 = hb[0], hb[1], hb[2]
                t1, t2, t3 = fb[0], fb[1], fb[2]
                act(sa[:, :T], ga[:, :T], AF.Sigmoid, [ga], [sa])
                act(sb2[:, :T], gb[:, :T], AF.Sigmoid, [gb], [sb2])
                act(sc2[:, :T], gc[:, :T], AF.Sigmoid, [gc], [sc2])
                tt('dve', t1[:, :T], ya[:, :T], sa[:, :T], ALU.mult, [ya, sa], [t1])
                tt('dve', t2[:, :T], yb_[:, :T], sb2[:, :T], ALU.mult, [yb_, sb2], [t2])
                tt('dve', t3[:, :T], yc[:, :T], sc2[:, :T], ALU.mult, [yc, sc2], [t3])
                tt('pool', t1[:, :T], t1[:, :T], t2[:, :T], ALU.add, [t1, t2], [t1])
                tt('pool', m16v[:, dc, :T], t1[:, :T], t3[:, :T], ALU.add, [t1, t3], [big1])
                release(1)
            wo = [next_unit(20), next_unit(21)]
            LAG = nblk if l == DEPTH - 1 else min(2, nblk)
            for b in range(nblk + LAG):
                if b < nblk:
                    r = rows_of[b]
                    for half in range(2):
                        pb = nb()
                        for dc in range(8):
                            wbuf = wo[dc // 4]
                            o0 = (dc % 4) * 1024 + half * 512
                            mm(pb[:r, :], m16v[:, dc, b * 128:b * 128 + r], wbuf[:, o0:o0 + 512],
                               [big1, wbuf], [pb], start=(dc == 0), stop=(dc == 7), inc=(dc == 7))
                        tt('dve', xres[:r, b, half * 512:(half + 1) * 512], pb[:r, :],
                           xres[:r, b, half * 512:(half + 1) * 512], ALU.add, [pb, xb[b]], [xb[b]])
                    if b == nblk - 1:
                        release(2)
                    post_dma(b, r)
                if 0 <= b - LAG < nblk:
                    post_blk(b - LAG, rows_of[b - LAG])

        for _ in range(NPS * NT + NSS):
            for l in range(DEPTH):
                for u in range(NUNITS):
                    sched_units.append((l, u))

        def seq_finish(conv_dst, hg_dst):
            for l in range(DEPTH):
                for j in range(2):
                    store(conv_dst[l][j].rearrange("(c p) -> p c", p=128), utail[l][:, :, j], [utail[l]],
                          allow_slow_non_contiguous=True)
                store(hg_dst[l].rearrange("h k v -> k h v"), Sst[l][:, :, :], [Sst[l]])

        def prompt_init():
            for l in range(DEPTH):
                op('pool', lambda e: e.memset(utail[l][:], 0.0), [], [utail[l]])
                op('pool', lambda e: e.memset(Sst[l][:], 0.0), [], [Sst[l]])

        def sample_init(s):
            for l in range(DEPTH):
                for j in range(2):
                    sload(utail[l][:, :, j], cconv[l, s, j].rearrange("(c p) -> p c", p=128), utail[l])
                S.dma('sp', Sst[l][:, :, :], shg[l, s].rearrange("h k v -> k h v"), writes=[Sst[l]], sem=ld)
                b2 = bf_view(big2).rearrange("p (a b) -> p a b", b=512)
                identB = cb[:, B_IDENT:B_IDENT + 128]
                for half in range(2):
                    S.dma('sp', big1[:, :, :], ck[l, s, half * 512:(half + 1) * 512, :].rearrange(
                        "(b p) f -> p b f", p=128), writes=[big1], sem=ld)
                    cp('dve', b2[:, 0:4, :], big1[:, :, :], [big1], [big2])
                    for blk in range(4):
                        pb = nb()
                        pv = bf_view(pb).rearrange("p (a b) -> p a b", b=128)
                        for c in range(4):
                            tr(pv[:, c, :], b2[:, blk, c * 128:(c + 1) * 128], identB, [big2, cb], [pb], inc=(c == 3))
                        kcol = (half * 4 + blk) * 128
                        cp('act', KT[l][:, :, kcol:kcol + 128], pv[:, 0:4, :], [pb], [KT[l]])
                    S.dma('sp', big1[:, :, :], cv[l, s, half * 512:(half + 1) * 512, :].rearrange(
                        "(b p) f -> p b f", p=128), writes=[big1], sem=ld)
                    cp('pool', VC[l][:, half * 4:half * 4 + 4, :], big1[:, :, :], [big1], [VC[l]])

        items = []
        for s in range(NPS):
            for ti in range(NT):
                t0 = ti * TT
                items.append(dict(
                    T=TT, kbase=t0,
                    xsrc=(lambda b, s=s, t0=t0: xp[s, t0 + b * 128:t0 + (b + 1) * 128, :]),
                    ydst=(lambda b, s=s, t0=t0: yp[s, t0 + b * 128:t0 + (b + 1) * 128, :]),
                    k_dst=[k_p[l, s, t0:t0 + TT, :] for l in range(DEPTH)],
                    v_dst=[v_p[l, s, t0:t0 + TT, :] for l in range(DEPTH)],
                    pre=(prompt_init if ti == 0 else None),
                    post=((lambda s=s: seq_finish([conv_p[l, s] for l in range(DEPTH)],
                                                  [hg_p[l, s] for l in range(DEPTH)])) if ti == NT - 1 else None)))
        for s in range(NSS):
            items.append(dict(
                T=DECT, kbase=PAST,
                xsrc=(lambda b, s=s: xs[s]), ydst=(lambda b, s=s: ys[s]),
                k_dst=[k_s[l, s] for l in range(DEPTH)], v_dst=[v_s[l, s] for l in range(DEPTH)],
                pre=(lambda s=s: sample_init(s)),
                post=(lambda s=s: seq_finish([conv_s[l, s] for l in range(DEPTH)],
                                             [hg_s[l, s] for l in range(DEPTH)]))))
        xld4 = [S.dsem(f"xld{b}") for b in range(4)]

        def it_rows(it, b):
            return min(128, it['T'] - b * 128)

        def it_nblk(it):
            return (it['T'] + 127) // 128

        def load_and_phase0(it, b):
            r = it_rows(it, b)
            S.dma('sp', xres[:r, b, :], it['xsrc'](b), writes=[xb[b]], sem=xld4[b])
            phase0_block(b, r)

        try:
            _stage('main')
            for b in range(it_nblk(items[0])):
                load_and_phase0(items[0], b)
            for n, it in enumerate(items):
                nxt = items[n + 1] if n + 1 < len(items) else None
                if it['pre'] is not None:
                    it['pre']()
                for l in range(DEPTH):
                    if l < DEPTH - 1:
                        post_dma = (lambda b, r: None)
                        post_blk = phase0_block
                    else:
                        def post_dma(b, r, it=it, nxt=nxt):
                            store(it['ydst'](b), xres[:r, b, :], [xb[b]])
                            if nxt is not None and b < it_nblk(nxt):
                                S.dma('sp', xres[:it_rows(nxt, b), b, :], nxt['xsrc'](b), writes=[xb[b]], sem=xld4[b])

                        def post_blk(b, r, it=it, nxt=nxt):
                            if nxt is not None and b < it_nblk(nxt):
                                phase0_block(b, it_rows(nxt, b))
                    tile_layer(l, it['T'], it['kbase'], it['k_dst'][l], it['v_dst'][l], post_dma, post_blk)
                if nxt is not None:
                    for b in range(it_nblk(it), it_nblk(nxt)):
                        load_and_phase0(nxt, b)
                if it['post'] is not None:
                    it['post']()

        except _StopBuild:
            pass

        for sem in st_sems:
            if sem['n']:
                S.need(_STQ, (sem['sem'], sem['n']))
        build.stats = dict(nops=S.nops, nsem=S.nsem)
    return nc


_CACHE = {}


def _get_nc(cfg_key):
    if cfg_key not in _CACHE:
        _CACHE[cfg_key] = build(Cfg(*cfg_key))
    return _CACHE[cfg_key]


def run(inputs, n_cores, nps, seq, nss):
    f = lambda a: np.ascontiguousarray(np.asarray(a, dtype=np.float32))
    x_prompt, x_sample = f(inputs['x_prompt']), f(inputs['x_sample'])
    cache_conv, state_hgrn = f(inputs['cache_conv']), f(inputs['state_hgrn'])
    cache_k, cache_v = f(inputs['cache_k']), f(inputs['cache_v'])
    consts = make_consts()
    shared = {
        "norm_g": f(inputs['norm_g']), "w_in": f(inputs['w_in']), "conv_w": f(inputs['conv_w']),
        "hg_lb": f(inputs['hg_lb_logits']), "hg_ng": f(inputs['hg_norm_g']), "q_ng": f(inputs['q_norm_g']),
        "k_ng": f(inputs['k_norm_g']), "w_br": f(inputs['w_branch']), "w_out": f(inputs['w_out']), "cst": consts,
    }
    in_maps = []
    for c in range(n_cores):
        ps, ss = slice(c * nps, (c + 1) * nps), slice(c * nss, (c + 1) * nss)
        m = dict(shared)
        m["xp"] = np.ascontiguousarray(x_prompt[ps])
        m["xs"] = np.ascontiguousarray(x_sample[ss])
        m["cconv"] = np.ascontiguousarray(cache_conv[:, ss])
        m["shg"] = np.ascontiguousarray(state_hgrn[:, ss])
        m["ck"] = np.ascontiguousarray(cache_k[:, ss].reshape(DEPTH, nss, PAST, 512))
        m["cv"] = np.ascontiguousarray(cache_v[:, ss].reshape(DEPTH, nss, PAST, 512))
        in_maps.append(m)
    nc = _get_nc((nps, seq, nss))
    res = run_bass_kernel_spmd(nc, in_maps, core_ids=list(range(n_cores)))
    R = res.results
    cat0 = lambda k: np.concatenate([r[k] for r in R], axis=0)
    cat1 = lambda k: np.concatenate([r[k] for r in R], axis=1)
    B, Bs = n_cores * nps, n_cores * nss
    return (
        cat0("yp"), cat0("ys"), cat1("conv_p"), cat1("conv_s"), cat1("hg_p"), cat1("hg_s"),
        cat1("k_p").reshape(DEPTH, B, seq, 8, 64), cat1("k_s").reshape(DEPTH, Bs, DECT, 8, 64),
        cat1("v_p").reshape(DEPTH, B, seq, 8, 64), cat1("v_s").reshape(DEPTH, Bs, DECT, 8, 64),
    )


def kernel(**inputs):
    return run(inputs, 8, 4, 2048, 2)
```
